# Optimizing a Trainium2 kernel written in Bass

```python
import math
import jax
import jax.numpy as jnp
from jax import lax
import numpy as np

D_MODEL = 1024
BATCH = 4
SEQ = 4096
DEPTH = 2

GRID_W = 64
CTX_LEN = 256
HEAD_DIM = 64
ROPE_BASE = 10000.0
EPS = 1e-6
BLK = 128
A_HEADS = 8
A_KV_HEADS = 2
WINDOW = 128
SSM_HEADS = 16
SSM_HEADDIM = 64
SSM_INNER = SSM_HEADS * SSM_HEADDIM
SSM_GROUPS = 2
SSM_STATE = 128
SSM_CONV = 3
SSM_CHUNK = 128
SSM_BC = SSM_GROUPS * SSM_STATE
SSM_XBC = SSM_INNER + 2 * SSM_BC
C_HEADS = 8
C_KV_HEADS = 2
D_FF = 2816
FFN_CONV = 3
A_Q_W = A_HEADS * HEAD_DIM
A_KV_W = A_KV_HEADS * HEAD_DIM
C_Q_W = C_HEADS * HEAD_DIM
C_KV_W = C_KV_HEADS * HEAD_DIM
IN_WIDTH = (A_Q_W + 2 * A_KV_W) + (SSM_INNER + SSM_XBC + 2 * SSM_HEADS) + (C_Q_W + 2 * C_KV_W) + 3 * D_MODEL

kernel_name = 'hybrid_swa_ssd_gridattn_convffn_prefix'


def _in_spans():
    sizes = (('a_q', A_Q_W), ('a_k', A_KV_W), ('a_v', A_KV_W),
             ('b_z', SSM_INNER), ('b_xbc', SSM_XBC), ('b_dt', 2 * SSM_HEADS),
             ('c_q', C_Q_W), ('c_k', C_KV_W), ('c_v', C_KV_W),
             ('gates', 3 * D_MODEL))
    spans, start = {}, 0
    for name, n in sizes:
        spans[name] = (start, start + n)
        start += n
    return spans


def rms_norm(x, g):
    xf = x.astype(jnp.float32)
    y = xf * lax.rsqrt(jnp.mean(xf * xf, axis=-1, keepdims=True) + EPS)
    return (y * g.astype(jnp.float32)).astype(x.dtype)


def modulate(h, shift, scale):
    return h * (1 + scale) + shift


def grid_rope(rows):
    t_row = jnp.repeat(jnp.arange(rows), GRID_W).astype(jnp.float32)
    t_col = jnp.tile(jnp.arange(GRID_W), rows).astype(jnp.float32)
    n = HEAD_DIM // 4
    inv = ROPE_BASE ** (-jnp.arange(n, dtype=jnp.float32) / n)
    ang = jnp.concatenate([t_row[:, None] * inv, t_col[:, None] * inv], axis=-1)
    return jnp.cos(ang), jnp.sin(ang)


def apply_rope(x, cos, sin):
    b, L, h, dh = x.shape
    xr = x.astype(jnp.float32).reshape(b, L, h, dh // 2, 2)
    c = cos[None, :, None, :]
    s = sin[None, :, None, :]
    x1, x2 = xr[..., 0], xr[..., 1]
    out = jnp.stack([x1 * c - x2 * s, x1 * s + x2 * c], axis=-1)
    return out.reshape(b, L, h, dh).astype(x.dtype)


def dwconv(u, w, bias):
    k = w.shape[0]
    pad = k // 2
    y = lax.conv_general_dilated(u, w[:, None, :].astype(u.dtype), window_strides=(1,),
                                 padding=[(pad, pad)], dimension_numbers=('NWC', 'WIO', 'NWC'),
                                 feature_group_count=u.shape[-1])
    return y + bias.astype(u.dtype)


def dense_attention(q, k, v, sink):
    b, lq, hq, dh = q.shape
    hkv = k.shape[2]
    g = hq // hkv
    qg = q.reshape(b, lq, hkv, g, dh)
    s = jnp.einsum('bqhgd,bkhd->bhgqk', qg, k).astype(jnp.float32) * (dh ** -0.5)
    if sink is not None:
        s_sink = jnp.broadcast_to(sink.astype(jnp.float32).reshape(1, hkv, g, 1, 1), s.shape[:-1] + (1,))
        s = jnp.concatenate([s, s_sink], axis=-1)
    p = jax.nn.softmax(s, axis=-1)
    if sink is not None:
        p = p[..., :-1]
    o = jnp.einsum('bhgqk,bkhd->bqhgd', p.astype(v.dtype), v)
    return o.reshape(b, lq, hq * dh)


def window_attention(q, k, v, kc, vc, sink):
    b, L, hq, dh = q.shape
    hkv = k.shape[2]
    g = hq // hkv
    nb = L // BLK
    lc = kc.shape[1]
    qb = q.reshape(b, nb, BLK, hkv, g, dh)
    pad = ((0, 0), (BLK, BLK), (0, 0), (0, 0))
    kp = jnp.pad(k, pad).reshape(b, nb + 2, BLK, hkv, dh)
    vp = jnp.pad(v, pad).reshape(b, nb + 2, BLK, hkv, dh)
    kw = jnp.concatenate([kp[:, :-2], kp[:, 1:-1], kp[:, 2:]], axis=2)
    vw = jnp.concatenate([vp[:, :-2], vp[:, 1:-1], vp[:, 2:]], axis=2)
    scale = dh ** -0.5
    s_loc = jnp.einsum('bnqhgd,bnjhd->bnhgqj', qb, kw).astype(jnp.float32) * scale
    s_ctx = jnp.einsum('bnqhgd,bchd->bnhgqc', qb, kc).astype(jnp.float32) * scale
    q_pos = (jnp.arange(nb) * BLK)[:, None] + jnp.arange(BLK)[None, :]
    k_pos = (jnp.arange(nb) * BLK - BLK)[:, None] + jnp.arange(3 * BLK)[None, :]
    rel = q_pos[:, :, None] - k_pos[:, None, :]
    valid = (jnp.abs(rel) <= WINDOW) & (k_pos[:, None, :] >= 0) & (k_pos[:, None, :] < L)
    s_loc = jnp.where(valid[None, :, None, None], s_loc, -jnp.inf)
    s_sink = jnp.broadcast_to(sink.astype(jnp.float32).reshape(1, 1, hkv, g, 1, 1), s_loc.shape[:-1] + (1,))
    p = jax.nn.softmax(jnp.concatenate([s_loc, s_ctx, s_sink], axis=-1), axis=-1)
    nloc = 3 * BLK
    p_loc = p[..., :nloc].astype(v.dtype)
    p_ctx = p[..., nloc:nloc + lc].astype(v.dtype)
    o = jnp.einsum('bnhgqj,bnjhd->bnqhgd', p_loc, vw) + jnp.einsum('bnhgqc,bchd->bnqhgd', p_ctx, vc)
    return o.reshape(b, L, hq * dh)


def grid_attention(q, k, v, kc, vc):
    b, L, hq, dh = q.shape
    nb = L // BLK
    k_all = jnp.concatenate([k, kc], axis=1)
    v_all = jnp.concatenate([v, vc], axis=1)
    qb = jnp.moveaxis(q.reshape(b, nb, BLK, hq, dh), 1, 0)
    ob = lax.map(lambda qi: dense_attention(qi, k_all, v_all, None), qb)
    return jnp.moveaxis(ob, 0, 1).reshape(b, L, hq * dh)


def ssm_inputs(xbc_raw, dt_raw, conv_w, conv_b, dt_bias):
    b, L, _ = xbc_raw.shape
    xbc = jax.nn.silu(dwconv(xbc_raw, conv_w, conv_b))
    xs = xbc[..., :SSM_INNER].reshape(b, L, SSM_HEADS, SSM_HEADDIM)
    bm = xbc[..., SSM_INNER:SSM_INNER + SSM_BC].reshape(b, L, SSM_GROUPS, SSM_STATE)
    cm = xbc[..., SSM_INNER + SSM_BC:].reshape(b, L, SSM_GROUPS, SSM_STATE)
    dt = jax.nn.softplus(dt_raw.astype(jnp.float32).reshape(b, L, 2, SSM_HEADS) + dt_bias.astype(jnp.float32))
    return xs, bm, cm, dt


def ssd_scan(xh, dt, a_coef, bm, cm, h0, want_y):
    b, L, H, P = xh.shape
    G, N = bm.shape[2], bm.shape[3]
    hg = H // G
    q = SSM_CHUNK
    nc = L // q
    f32 = jnp.float32
    xdt = (xh.astype(f32) * dt[..., None]).reshape(b, nc, q, G, hg, P)
    bc = bm.astype(f32).reshape(b, nc, q, G, N)
    cc = cm.astype(f32).reshape(b, nc, q, G, N)
    acs = jnp.cumsum((dt * a_coef.astype(f32)).reshape(b, nc, q, G, hg), axis=2)
    decay_end = jnp.exp(acs[:, :, -1:] - acs)
    states = jnp.einsum('bcjgn,bcjghp->bcghpn', bc, xdt * decay_end[..., None])
    chunk_decay = jnp.exp(acs[:, :, -1])

    def step(h, inp):
        st, dec = inp
        return h * dec[..., None, None] + st, h

    h_last, h_in = lax.scan(step, h0.reshape(b, G, hg, P, N),
                            (jnp.moveaxis(states, 1, 0), jnp.moveaxis(chunk_decay, 1, 0)))
    h_last = h_last.reshape(b, H, P, N)
    if not want_y:
        return None, h_last
    h_in = jnp.moveaxis(h_in, 0, 1)
    seg = acs[:, :, :, None] - acs[:, :, None, :]
    tri = jnp.tril(jnp.ones((q, q), dtype=bool))
    lmat = jnp.exp(jnp.where(tri[:, :, None, None], seg, -jnp.inf))
    cb = jnp.einsum('bcign,bcjgn->bcijg', cc, bc)
    y_diag = jnp.einsum('bcijgh,bcjghp->bcighp', cb[..., None] * lmat, xdt)
    y_off = jnp.einsum('bcign,bcghpn->bcighp', cc, h_in) * jnp.exp(acs)[..., None]
    y = (y_diag + y_off).reshape(b, L, H, P).astype(xh.dtype)
    return y, h_last


def ssm_output(yf, yb, xs, z, d_skip, norm_g):
    b, L = xs.shape[0], xs.shape[1]
    y = yf + yb + xs * d_skip[:, None].astype(xs.dtype)
    y = y.reshape(b, L, SSM_INNER)
    return rms_norm(y * jax.nn.silu(z), norm_g)


def merge_branches(ya, yb, yc, g_raw, w_oa, w_ob, w_oc, w_out):
    g = jax.nn.sigmoid(g_raw.astype(jnp.float32)).astype(ya.dtype)
    ga, gb, gc = jnp.split(g, 3, axis=-1)
    m = ga * (ya @ w_oa) + gb * (yb @ w_ob) + gc * (yc @ w_oc)
    return m @ w_out


def conv_ffn(h, w_up, w_gate, conv_w, conv_b, w_down):
    up = h @ w_up
    gt = dwconv(h @ w_gate, conv_w, conv_b)
    return (jax.nn.silu(gt) * up) @ w_down


def hybrid_layer(x, xc, c_mod, cc_mod, cos, sin, w_in, norm1, norm2, a_sink, ssm_conv_w, ssm_conv_b,
                 ssm_a_log, ssm_dt_bias, ssm_d, ssm_norm, c_q_norm, c_k_norm, w_oa, w_ob, w_oc, w_out,
                 ffn_w_up, ffn_w_gate, ffn_conv_w, ffn_conv_b, ffn_w_down, need_ctx_out):
    b, L, _ = x.shape
    lc = xc.shape[1]
    sp = _in_spans()
    shift1, scale1, gate1, shift2, scale2, gate2 = jnp.split(c_mod[:, None, :], 6, axis=-1)
    h = modulate(rms_norm(x, norm1), shift1, scale1)
    hc = modulate(rms_norm(xc, norm1), cc_mod[:D_MODEL], cc_mod[D_MODEL:2 * D_MODEL])
    u = h @ w_in

    def lat(name):
        return u[..., sp[name][0]:sp[name][1]]

    def ctxp(name):
        return hc @ w_in[:, sp[name][0]:sp[name][1]]

    qa = apply_rope(lat('a_q').reshape(b, L, A_HEADS, HEAD_DIM), cos, sin)
    ka = apply_rope(lat('a_k').reshape(b, L, A_KV_HEADS, HEAD_DIM), cos, sin)
    va = lat('a_v').reshape(b, L, A_KV_HEADS, HEAD_DIM)
    kac = ctxp('a_k').reshape(b, lc, A_KV_HEADS, HEAD_DIM)
    vac = ctxp('a_v').reshape(b, lc, A_KV_HEADS, HEAD_DIM)
    ya = window_attention(qa, ka, va, kac, vac, a_sink)

    qg = apply_rope(rms_norm(lat('c_q').reshape(b, L, C_HEADS, HEAD_DIM), c_q_norm), cos, sin)
    kg = apply_rope(rms_norm(lat('c_k').reshape(b, L, C_KV_HEADS, HEAD_DIM), c_k_norm), cos, sin)
    vg = lat('c_v').reshape(b, L, C_KV_HEADS, HEAD_DIM)
    kgc = rms_norm(ctxp('c_k').reshape(b, lc, C_KV_HEADS, HEAD_DIM), c_k_norm)
    vgc = ctxp('c_v').reshape(b, lc, C_KV_HEADS, HEAD_DIM)
    yg = grid_attention(qg, kg, vg, kgc, vgc)

    a_coef = -jnp.exp(ssm_a_log.astype(jnp.float32))
    xs_c, bm_c, cm_c, dt_c = ssm_inputs(ctxp('b_xbc'), ctxp('b_dt'), ssm_conv_w, ssm_conv_b, ssm_dt_bias)
    h0 = jnp.zeros((b, SSM_HEADS, SSM_HEADDIM, SSM_STATE), jnp.float32)
    yf_c, hf_c = ssd_scan(xs_c, dt_c[:, :, 0], a_coef[0], bm_c, cm_c, h0, need_ctx_out)
    yb_c, hb_c = ssd_scan(jnp.flip(xs_c, 1), jnp.flip(dt_c[:, :, 1], 1), a_coef[1],
                          jnp.flip(bm_c, 1), jnp.flip(cm_c, 1), h0, need_ctx_out)
    xs, bm, cm, dt = ssm_inputs(lat('b_xbc'), lat('b_dt'), ssm_conv_w, ssm_conv_b, ssm_dt_bias)
    yf, _ = ssd_scan(xs, dt[:, :, 0], a_coef[0], bm, cm, hf_c, True)
    yb, _ = ssd_scan(jnp.flip(xs, 1), jnp.flip(dt[:, :, 1], 1), a_coef[1],
                     jnp.flip(bm, 1), jnp.flip(cm, 1), hb_c, True)
    ys = ssm_output(yf, jnp.flip(yb, 1), xs, lat('b_z'), ssm_d, ssm_norm)

    x = x + gate1 * merge_branches(ya, ys, yg, lat('gates'), w_oa, w_ob, w_oc, w_out)
    h2 = modulate(rms_norm(x, norm2), shift2, scale2)
    x = x + gate2 * conv_ffn(h2, ffn_w_up, ffn_w_gate, ffn_conv_w, ffn_conv_b, ffn_w_down)
    if not need_ctx_out:
        return x, None

    cgate1 = cc_mod[2 * D_MODEL:3 * D_MODEL]
    cshift2 = cc_mod[3 * D_MODEL:4 * D_MODEL]
    cscale2 = cc_mod[4 * D_MODEL:5 * D_MODEL]
    cgate2 = cc_mod[5 * D_MODEL:]
    yac = dense_attention(ctxp('a_q').reshape(b, lc, A_HEADS, HEAD_DIM), kac, vac, a_sink)
    ygc = dense_attention(rms_norm(ctxp('c_q').reshape(b, lc, C_HEADS, HEAD_DIM), c_q_norm), kgc, vgc, None)
    ysc = ssm_output(yf_c, jnp.flip(yb_c, 1), xs_c, ctxp('b_z'), ssm_d, ssm_norm)
    xc = xc + cgate1 * merge_branches(yac, ysc, ygc, ctxp('gates'), w_oa, w_ob, w_oc, w_out)
    h2c = modulate(rms_norm(xc, norm2), cshift2, cscale2)
    xc = xc + cgate2 * conv_ffn(h2c, ffn_w_up, ffn_w_gate, ffn_conv_w, ffn_conv_b, ffn_w_down)
    return x, xc


def setup_inputs(seed: int = 0) -> dict:
    key = jax.random.key(seed)
    ks = jax.random.split(key, 32)
    f32 = jnp.float32

    def nrm(k, shape, scale):
        return jax.random.normal(k, shape, f32) * scale

    d = D_MODEL
    dt0 = jnp.exp(jax.random.uniform(ks[11], (DEPTH, 2, SSM_HEADS), f32, math.log(1e-3), math.log(1e-1)))
    return {
        'x': nrm(ks[0], (BATCH, SEQ, d), 1.0),
        'c': nrm(ks[1], (BATCH, d), 1.0),
        'ctx': nrm(ks[2], (BATCH, CTX_LEN, d), 1.0),
        'c_ctx': nrm(ks[3], (d,), 1.0),
        'w_mod': nrm(ks[4], (DEPTH, d, 6 * d), 0.5 * d ** -0.5),
        'b_mod': nrm(ks[5], (DEPTH, 6 * d), 0.02),
        'norm1': 1.0 + nrm(ks[6], (DEPTH, d), 0.05),
        'norm2': 1.0 + nrm(ks[7], (DEPTH, d), 0.05),
        'w_in': nrm(ks[8], (DEPTH, d, IN_WIDTH), d ** -0.5),
        'a_sink': nrm(ks[9], (DEPTH, A_HEADS), 0.5),
        'ssm_conv_w': nrm(ks[10], (DEPTH, SSM_CONV, SSM_XBC), SSM_CONV ** -0.5),
        'ssm_conv_b': nrm(ks[12], (DEPTH, SSM_XBC), 0.02),
        'ssm_A_log': jnp.log(jax.random.uniform(ks[13], (DEPTH, 2, SSM_HEADS), f32, 1.0, 16.0)),
        'ssm_dt_bias': dt0 + jnp.log(-jnp.expm1(-dt0)),
        'ssm_D': 1.0 + nrm(ks[14], (DEPTH, SSM_HEADS), 0.1),
        'ssm_norm': 1.0 + nrm(ks[15], (DEPTH, SSM_INNER), 0.05),
        'c_q_norm': 1.0 + nrm(ks[16], (DEPTH, HEAD_DIM), 0.05),
        'c_k_norm': 1.0 + nrm(ks[17], (DEPTH, HEAD_DIM), 0.05),
        'w_oa': nrm(ks[18], (DEPTH, A_Q_W, d), A_Q_W ** -0.5),
        'w_ob': nrm(ks[19], (DEPTH, SSM_INNER, d), SSM_INNER ** -0.5),
        'w_oc': nrm(ks[20], (DEPTH, C_Q_W, d), C_Q_W ** -0.5),
        'w_out': nrm(ks[21], (DEPTH, d, d), d ** -0.5),
        'ffn_w_up': nrm(ks[22], (DEPTH, d, D_FF), d ** -0.5),
        'ffn_w_gate': nrm(ks[23], (DEPTH, d, D_FF), d ** -0.5),
        'ffn_conv_w': nrm(ks[24], (DEPTH, FFN_CONV, D_FF), FFN_CONV ** -0.5),
        'ffn_conv_b': nrm(ks[25], (DEPTH, D_FF), 0.02),
        'ffn_w_down': nrm(ks[26], (DEPTH, D_FF, d), D_FF ** -0.5),
        'final_norm': 1.0 + nrm(ks[27], (d,), 0.05),
    }


def reference(x, c, ctx, c_ctx, w_mod, b_mod, norm1, norm2, w_in, a_sink, ssm_conv_w, ssm_conv_b,
              ssm_A_log, ssm_dt_bias, ssm_D, ssm_norm, c_q_norm, c_k_norm, w_oa, w_ob, w_oc, w_out,
              ffn_w_up, ffn_w_gate, ffn_conv_w, ffn_conv_b, ffn_w_down, final_norm):
    rows = x.shape[1] // GRID_W
    cos, sin = grid_rope(rows)
    xc = ctx
    sc = jax.nn.silu(c)
    scc = jax.nn.silu(c_ctx)
    for l in range(DEPTH):
        need_ctx_out = l < DEPTH - 1
        n_cmod = 6 * D_MODEL if need_ctx_out else 2 * D_MODEL
        c_mod = sc @ w_mod[l] + b_mod[l]
        cc_mod = scc @ w_mod[l][:, :n_cmod] + b_mod[l][:n_cmod]
        x, xc = hybrid_layer(x, xc, c_mod, cc_mod, cos, sin, w_in[l], norm1[l], norm2[l], a_sink[l],
                             ssm_conv_w[l], ssm_conv_b[l], ssm_A_log[l], ssm_dt_bias[l], ssm_D[l],
                             ssm_norm[l], c_q_norm[l], c_k_norm[l], w_oa[l], w_ob[l], w_oc[l], w_out[l],
                             ffn_w_up[l], ffn_w_gate[l], ffn_conv_w[l], ffn_conv_b[l], ffn_w_down[l],
                             need_ctx_out)
    return rms_norm(x, final_norm)
```

```python
from contextlib import ExitStack
import os
import numpy as np
import concourse.bass as bass
import concourse.mybir as mybir
from concourse.bass_utils import run_bass_kernel_spmd

F32 = mybir.dt.float32
BF16 = mybir.dt.bfloat16
ALU = mybir.AluOpType
AF = mybir.ActivationFunctionType

D = 1024
L = 4096
LH = 2048
LC = 256
T = LH + LC
NB = T // 128
PAIRS = [[0, 1], [2, 3], [4, 5], [6, 7]]
NOCC = False
DEPTH = 2
DFF = 2816
EPS = 1e-6
HTC = T + 6
HL = 259
HR = 260 + LH
NFM = 56
NTM = 1312


def hcol(i):
    return i + 2 if i < LC else i + 4


class Trk:
    ROT = 30000
    NDMA = 24

    def __init__(self, nc):
        self.nc = nc
        self.eng = {'pe': nc.tensor, 'act': nc.scalar, 'dve': nc.vector, 'pool': nc.gpsimd, 'sp': nc.sync}
        self.semh = []
        self.cur = {}
        self.cnt = {}
        for e in ('pe', 'act', 'dve', 'pool'):
            self.cur[e] = self._newsem(f"s_{e}")
            self.cnt[e] = 0
        self.dsem = [self._newsem(f"s_dma{i}") for i in range(self.NDMA)]
        self.dval = [0] * self.NDMA
        self.drr = 0
        self.known = {e: {} for e in self.eng}
        self.res = {}
        self.ninst = 0
        self.pesems = {self.cur['pe']}
        self.ccsem = None
        self.ccv = 0

    def _newsem(self, name):
        h = self.nc.alloc_semaphore(f"{name}_{len(self.semh)}")
        self.semh.append(h)
        return len(self.semh) - 1

    def _wait(self, e, tok):
        sid, val = tok
        if self.known[e].get(sid, 0) >= val:
            return
        self.eng[e].wait_ge(self.semh[sid], val)
        self.known[e][sid] = val

    def _deps(self, reads, writes):
        deps = {}

        def add(tok):
            if tok is None:
                return
            if deps.get(tok[0], 0) < tok[1]:
                deps[tok[0]] = tok[1]
        for r in reads:
            st = self.res.get(r)
            if st is not None:
                add(st[0])
        for w in writes:
            st = self.res.get(w)
            if st is not None:
                add(st[0])
                for s, v in st[1].items():
                    add((s, v))
        return deps

    def _commit(self, tok, reads, writes):
        for r in reads:
            st = self.res.setdefault(r, [None, {}])
            if st[1].get(tok[0], 0) < tok[1]:
                st[1][tok[0]] = tok[1]
        for w in writes:
            self.res[w] = [tok, {}]

    def op(self, e, fn, reads=(), writes=()):
        deps = self._deps(reads, writes)
        for s, v in deps.items():
            if e == 'pe' and s in self.pesems:
                continue
            self._wait(e, (s, v))
        inst = fn()
        if self.cnt[e] >= self.ROT:
            self.cur[e] = self._newsem(f"s_{e}")
            self.cnt[e] = 0
            if e == 'pe':
                self.pesems.add(self.cur[e])
        self.cnt[e] += 1
        tok = (self.cur[e], self.cnt[e])
        inst.then_inc(self.semh[tok[0]], 1)
        self._commit(tok, reads, writes)
        self.ninst += 1
        return tok

    def dma(self, out, in_, reads=(), writes=(), q='sp', slow=False):
        deps = self._deps(reads, writes)
        i = self.drr
        self.drr = (self.drr + 1) % self.NDMA
        if self.dval[i] > 0:
            deps[self.dsem[i]] = max(deps.get(self.dsem[i], 0), self.dval[i])
        for s, v in deps.items():
            self._wait(q, (s, v))
        if slow:
            inst = self.eng[q].dma_start(out=out, in_=in_, allow_slow_non_contiguous=True)
        else:
            inst = self.eng[q].dma_start(out=out, in_=in_)
        self.dval[i] += 16
        tok = (self.dsem[i], self.dval[i])
        inst.then_inc(self.semh[tok[0]], 16)
        self._commit(tok, reads, writes)
        self.ninst += 1
        return tok

    def cc(self, src, dst, reads=(), writes=()):
        if NOCC:
            return None
        deps = self._deps(reads, writes)
        for s_, v in deps.items():
            self._wait('pool', (s_, v))
        if self.ccsem is None:
            self.ccsem = self._newsem("s_cc")
            self.ccv = 0
        inst = self.nc.gpsimd.collective_compute("AllGather", ALU.bypass, replica_groups=PAIRS, ins=[src], outs=[dst])
        self.ccv += 1
        tok = (self.ccsem, self.ccv)
        inst.then_inc(self.semh[tok[0]])
        self._commit(tok, reads, writes)
        self.ninst += 1
        return tok

    def barrier(self):
        toks = {}
        for e in ('pe', 'act', 'dve', 'pool'):
            if self.cnt[e] > 0:
                toks[self.cur[e]] = self.cnt[e]
        for i in range(self.NDMA):
            if self.dval[i] > 0:
                toks[self.dsem[i]] = self.dval[i]
        if self.ccsem is not None and self.ccv > 0:
            toks[self.ccsem] = self.ccv
        for e in self.eng:
            for s, v in toks.items():
                self._wait(e, (s, v))
        self.res = {}

    def finish(self, toks):
        for t in toks:
            self._wait('sp', t)


class Prog:
    def __init__(self, dbg=(), nlayers=DEPTH, stop_after=None):
        self.dbg = set(dbg)
        self.nlayers = nlayers
        self.stop_after = stop_after
        nc = self.nc = bass.Bass("TRN2", target_bir_lowering=False)
        self.t = Trk(nc)
        self.outs = []
        self._uid = 0
        self.build()

    def din(self, name, shape, dt=F32):
        return self.nc.dram_tensor(name, list(shape), dt, kind="ExternalInput").ap()

    def dscr(self, name, shape, dt):
        if name in self.dbg:
            self.outs.append(name)
            return self.nc.dram_tensor(name, list(shape), dt, kind="ExternalOutput").ap()
        return self.nc.dram_tensor(name, list(shape), dt).ap()

    def sb(self, name, shape, dt):
        return self.nc.alloc_sbuf_tensor("sb_" + name, list(shape), dt).ap()

    def dump(self, name, ap, keys, dt=F32):
        if name in self.dbg:
            d = self.dscr(name, list(ap.shape), dt)
            self.dma(d, ap, keys, [('dump', name)])

    def uid(self):
        self._uid += 1
        return self._uid

    def mm(self, out, lhsT, rhs, start, stop, r, w):
        return self.t.op('pe', lambda: self.nc.tensor.matmul(out, lhsT=lhsT, rhs=rhs, start=start, stop=stop), r, w)

    def tr(self, out, in_, ident, r, w):
        return self.t.op('pe', lambda: self.nc.tensor.transpose(out, in_, ident), r, w)

    def act(self, out, in_, func, r, w, bias=None, scale=1.0, accum_out=None):
        kw = {}
        if bias is not None:
            kw['bias'] = bias
        if accum_out is not None:
            kw['accum_out'] = accum_out
        return self.t.op('act', lambda: self.nc.scalar.activation(out=out, in_=in_, func=func, scale=scale, **kw), r, w)

    def tt(self, e, out, in0, in1, op, r, w):
        eng = self.t.eng[e]
        return self.t.op(e, lambda: eng.tensor_tensor(out=out, in0=in0, in1=in1, op=op), r, w)

    def ts(self, e, out, in0, s1, op0, r, w, s2=None, op1=None):
        eng = self.t.eng[e]
        if op1 is None:
            return self.t.op(e, lambda: eng.tensor_scalar(out=out, in0=in0, scalar1=s1, scalar2=None, op0=op0), r, w)
        return self.t.op(e, lambda: eng.tensor_scalar(out=out, in0=in0, scalar1=s1, scalar2=s2, op0=op0, op1=op1), r, w)

    def stt(self, e, out, in0, scalar, in1, op0, op1, r, w):
        eng = self.t.eng[e]
        return self.t.op(e, lambda: eng.scalar_tensor_tensor(out=out, in0=in0, scalar=scalar, in1=in1, op0=op0, op1=op1), r, w)

    def cp(self, e, out, in_, r, w):
        eng = self.t.eng[e]
        if e == 'act':
            return self.t.op(e, lambda: eng.copy(out=out, in_=in_), r, w)
        return self.t.op(e, lambda: eng.tensor_copy(out=out, in_=in_), r, w)

    def recip(self, out, in_, r, w):
        return self.t.op('dve', lambda: self.nc.vector.reciprocal(out=out, in_=in_), r, w)

    def memset(self, e, ap, v, w):
        eng = self.t.eng[e]
        return self.t.op(e, lambda: eng.memset(ap, v), (), w)

    def dma(self, out, in_, r, w, slow=False, q='sp'):
        return self.t.dma(out, in_, r, w, q=q, slow=slow)

    def build(self):
        nc = self.nc
        self.xT0 = self.din("xT0", [128, 8, T])
        self.cvec = self.din("cvec", [128, 8, 2])
        self.xh0 = self.din("xh0", [128, 8, 2])
        self.hmask = self.din("hmask", [128, 2])
        self.wmod = self.din("wmod", [DEPTH, 48, 128, 8, 128])
        self.bmod = self.din("bmod", [DEPTH, 128, 48])
        self.nrm = self.din("nrm", [DEPTH, 128, 2, 8])
        self.wfm = self.din("wfm", [DEPTH, NFM, 128, 8, 128])
        self.wtm = self.din("wtm", [DEPTH, 128, 8, NTM])
        self.rope = self.din("rope", [2, 128, T])
        self.cgain = self.din("cgain", [DEPTH, 128, 4])
        self.cw = self.din("cw", [DEPTH, 128, 12, 4])
        self.dtb = self.din("dtb", [DEPTH, 128, 32])
        self.alog = self.din("alog", [DEPTH, 128, 32])
        self.dsk = self.din("dsk", [DEPTH, 128, 16])
        self.sng = self.din("sng", [DEPTH, 128, 1024])
        self.sink = self.din("sink", [DEPTH, 128, 8])
        self.woa = self.din("woa", [DEPTH, 128, 4, 1024])
        self.woc = self.din("woc", [DEPTH, 128, 4, 1024])
        self.wob = self.din("wob", [DEPTH, 128, 8, 1024])
        self.wout = self.din("wout", [DEPTH, 128, 8, 1024])
        self.wup = self.din("wup", [DEPTH, 22, 128, 8, 128])
        self.wgt = self.din("wgt", [DEPTH, 22, 128, 8, 128])
        self.fcw = self.din("fcw", [DEPTH, 128, 22, 4])
        self.wdn = self.din("wdn", [DEPTH, 128, 22, 1024])
        self.fng = self.din("fng", [128, 8])
        self.masks = self.din("masks", [6, 128, 128])
        self.ident_in = self.din("ident", [128, 128])
        self.outT = nc.dram_tensor("outT", [128, 8, LH], F32, kind="ExternalOutput").ap()

        self.xT = self.dscr("xT", [128, 8, T], F32)
        self.qaT = self.dscr("qaT", [128, 4, T], BF16)
        self.kaT = self.dscr("kaT", [128, T], BF16)
        self.qcT = self.dscr("qcT", [128, 4, T], BF16)
        self.kcT = self.dscr("kcT", [128, T], BF16)
        self.xbcT = self.dscr("xbcT", [128, 12, T], BF16)
        self.gtT = self.dscr("gtT", [128, 24, T], BF16)
        self.zs = self.dscr("zs", [T, 1024], F32)
        self.vv = self.dscr("vv", [T, 256], BF16)
        self.dts = self.dscr("dts", [T, 32], F32)
        self.yaT = self.dscr("yaT", [128, 4, T], BF16)
        self.ycT = self.dscr("ycT", [128, 4, T], BF16)
        self.ysT = self.dscr("ysT", [128, 8, T], BF16)
        self.hst = self.dscr("hst", [2, NB, 128, 1024], BF16)
        self.actT = self.dscr("actT", [128, 22, T], BF16)
        self.xhal = self.dscr("xhal", [128, 8, 2], F32)
        self.xb_src = self.dscr("xb_src", [128, 16], F32)
        self.xb_dst = self.dscr("xb_dst", [256, 16], F32)
        self.kaL = self.dscr("kaL", [128, LH], BF16)
        self.kcL = self.dscr("kcL", [128, LH], BF16)
        self.kaG = self.dscr("kaG", [256, LH], BF16)
        self.kcG = self.dscr("kcG", [256, LH], BF16)
        self.vvL = self.dscr("vvL", [LH, 256], BF16)
        self.vG = self.dscr("vG", [2 * LH, 256], BF16)
        self.s_src = self.dscr("s_src", [128, 2048], F32)
        self.s_dst = self.dscr("s_dst", [256, 2048], F32)

        self.ps = nc.alloc_psum_tensor("ps", [128, 8, 512], F32).ap()
        self.ident_f = self.sb("ident_f", [128, 128], F32)
        self.ident = self.sb("ident", [128, 128], BF16)
        self.ones_ms = self.sb("ones_ms", [128, 128], BF16)
        self.bd_ms = self.sb("bd_ms", [128, 128], BF16)
        self.ones_f = self.sb("ones_f", [128, 128], F32)
        self.msk_f = self.sb("msk_f", [128, 6, 128], F32)
        self.msk_b = self.sb("msk_b", [128, 6, 128], BF16)
        self.epsc = self.sb("epsc", [128, 1], F32)
        self.onec = self.sb("onec", [128, 1], F32)
        self.modT = self.sb("modT", [128, 48, 2], F32)
        self.gm = self.sb("gm", [128, 2, 8, 2], F32)
        self.hm = self.sb("hm", [128, 2], F32)

        self.consts()
        toks = []
        for l in range(self.nlayers):
            self.layer(l)
            if self.stop_after is not None and l == self.stop_after[0]:
                break
        if self.stop_after is None:
            toks = self.final_norm()
        self.t.barrier()

    def consts(self):
        self.dma(self.ident_f, self.ident_in, (), ['ident_f'])
        self.cp('dve', self.ident, self.ident_f, ['ident_f'], ['ident'])
        self.memset('dve', self.ones_ms, 1.0 / 1024, ['ones_ms'])
        self.memset('dve', self.bd_ms, 0.0, ['bd_ms'])
        self.memset('dve', self.bd_ms[0:64, 0:64], 1.0 / 128, ['bd_ms'])
        self.memset('dve', self.bd_ms[64:128, 64:128], 1.0 / 128, ['bd_ms'])
        self.memset('dve', self.ones_f, 1.0, ['ones_f'])
        self.memset('dve', self.epsc, EPS, ['epsc'])
        self.memset('dve', self.onec, 1.0, ['onec'])
        self.dma(self.msk_f, self.masks.rearrange("m p c -> p m c"), (), ['msk_f'])
        self.cp('dve', self.msk_b, self.msk_f, ['msk_f'], ['msk_b'])
        self.dma(self.hm, self.hmask, (), ['hm'])

    def layer(self, l):
        xsrc = self.xT0 if l == 0 else self.xT
        self.mod_phase(l)
        if self.stop_after == (l, 'mod'):
            return
        with self.nc.sbuf_tensor(f"hT{l}a", [128, 8, HTC], BF16) as hT_h:
            self.hT = hT_h.ap()
            self.memset('dve', self.hT, 0.0, [('hT', g_) for g_ in range(5)])
            self.norm_phase(l, 0, xsrc, self.xh0 if l == 0 else self.xhal)
            if self.stop_after == (l, 'n1'):
                return
            self.inproj_phase(l)
        if self.stop_after == (l, 'ip'):
            return
        self.attn_phase(l, 'A')
        self.attn_phase(l, 'C')
        if self.stop_after == (l, 'at'):
            return
        self.ssm_phase(l)
        if self.stop_after == (l, 'ss'):
            return
        self.merge_phase(l, xsrc)
        self.halo_exchange()
        if self.stop_after == (l, 'mg'):
            return
        with self.nc.sbuf_tensor(f"hT{l}b", [128, 8, HTC], BF16) as hT_h:
            self.hT = hT_h.ap()
            self.memset('dve', self.hT, 0.0, [('hT', g_) for g_ in range(5)])
            self.norm_phase(l, 1, self.xT, self.xhal)
            self.ffn_phase(l)
        if l < DEPTH - 1:
            self.halo_exchange()
        if self.stop_after == (l, 'ff'):
            return

    def mod_phase(self, l):
        nc = self.nc
        t = self.t
        with nc.sbuf_tensor(f"m_c{l}", [128, 8, 2], F32) as c_h, \
                nc.sbuf_tensor(f"m_sc{l}", [128, 8, 2], F32) as sc_h, \
                nc.sbuf_tensor(f"m_w{l}", [128, 4, 8, 128], F32) as w_h, \
                nc.sbuf_tensor(f"m_wb{l}", [128, 2, 8, 128], BF16) as wb_h, \
                nc.sbuf_tensor(f"m_scb{l}", [128, 8, 2], BF16) as scb_h, \
                nc.sbuf_tensor(f"m_b{l}", [128, 48], F32) as b_h, \
                nc.sbuf_tensor(f"m_n{l}", [128, 2, 8], F32) as n_h:
            c_sb, sc_sb, w_sb, b_sb, n_sb = c_h.ap(), sc_h.ap(), w_h.ap(), b_h.ap(), n_h.ap()
            self.dma(c_sb, self.cvec, (), ['m_c'])
            self.dma(b_sb, self.bmod[l], (), ['m_b'])
            self.dma(n_sb, self.nrm[l], (), ['m_n'])
            self.act(sc_sb, c_sb, AF.Silu, ['m_c'], ['m_sc'])
            wb_sb, scb = wb_h.ap(), scb_h.ap()
            self.cp('dve', scb, sc_sb, ['m_sc'], ['m_scb'])
            for j in range(3):
                self.dma(w_sb[:, j % 4], self.wmod[l, j], (), [('m_w', j % 4)])
            for j in range(48):
                s = j % 4
                if j + 3 < 48:
                    self.dma(w_sb[:, (j + 3) % 4], self.wmod[l, j + 3], (), [('m_w', (j + 3) % 4)])
                self.cp('dve', wb_sb[:, j % 2], w_sb[:, s], [('m_w', s)], [('m_wb', j % 2)])
                pt = self.ps[:, j % 4, 0:2]
                for kc in range(8):
                    self.mm(pt, wb_sb[:, j % 2, kc, :], scb[:, kc, :], kc == 0, kc == 7,
                            [('m_wb', j % 2), 'm_scb'], [('ps', j % 4)])
                self.ts('dve', self.modT[:, j, :], pt, b_sb[:, j:j + 1], ALU.add, [('ps', j % 4), 'm_b'], [('modT', j)])
            for n in range(2):
                sc_off = 8 + 24 * n
                for kc in range(8):
                    self.ts('dve', self.gm[:, n, kc, :], self.modT[:, sc_off + kc, :], 1.0, ALU.add, [('modT', sc_off + kc)], ['gm'])
                    self.ts('dve', self.gm[:, n, kc, :], self.gm[:, n, kc, :], n_sb[:, n, kc:kc + 1], ALU.mult,
                            ['gm', 'm_n'], ['gm'])
            if 'modT' in self.dbg:
                d = self.dscr("modT", [128, 48, 2], F32)
                self.dma(d, self.modT, ['modT'], ['d_modT'])
            t.barrier()

    @staticmethod
    def groups():
        g = [(0, LC)]
        for i in range(LH // 512):
            g.append((LC + 512 * i, 512))
        return g

    @staticmethod
    def windows():
        return [(256 * w, 256) for w in range(T // 256)]

    def norm_phase(self, l, n, xsrc, xhsrc):
        nc = self.nc
        sh_off = 0 if n == 0 else 24
        with nc.sbuf_tensor(f"n_x{l}{n}", [128, 2, 8, 512], F32) as x_h, \
                nc.sbuf_tensor(f"n_sq{l}{n}", [128, 2, 8, 512], BF16) as sq_h, \
                nc.sbuf_tensor(f"n_r{l}{n}", [128, 2, 512], F32) as r_h, \
                nc.sbuf_tensor(f"n_t{l}{n}", [128, 2, 512], F32) as t_h:
            x_sb, sq_sb, r_sb, t_sb = x_h.ap(), sq_h.ap(), r_h.ap(), t_h.ap()
            glist = list(enumerate(self.groups())) + [(99, (None, 2))]
            for gi, (t0, n_) in glist:
                halo = gi == 99
                s = gi % 2
                cls = 1 if (not halo and t0 < LC) else 0
                xg = x_sb[:, s, :, 0:n_]
                if halo:
                    self.dma(xg, xhsrc, ['xhal'], [('n_x', s)])
                else:
                    self.dma(xg, xsrc[:, :, t0:t0 + n_], [('xT', gi)], [('n_x', s)])
                self.act(sq_sb[:, s, :, 0:n_], xg, AF.Square, [('n_x', s)], [('n_sq', s)])
                pt = self.ps[:, s, 0:n_]
                for kc in range(8):
                    self.mm(pt, self.ones_ms, sq_sb[:, s, kc, 0:n_], kc == 0, kc == 7,
                            [('n_sq', s), 'ones_ms'], [('ps', s)])
                rr = r_sb[:, s, 0:n_]
                self.act(rr, pt, AF.Ln, [('ps', s), 'epsc'], [('n_r', s)], bias=self.epsc)
                self.act(rr, rr, AF.Exp, [('n_r', s)], [('n_r', s)], scale=-0.5)
                c0 = hcol(t0) if not halo else None
                for kc in range(8):
                    ts_ = kc % 2
                    tmp = t_sb[:, ts_, 0:n_]
                    self.stt('dve', tmp, xg[:, kc, :], self.gm[:, n, kc, cls:cls + 1], rr, ALU.mult, ALU.mult,
                             [('n_x', s), 'gm', ('n_r', s)], [('n_t', ts_)])
                    if halo:
                        self.stt('dve', tmp, tmp, self.modT[:, sh_off + kc, 0:1], self.hm, ALU.add, ALU.mult,
                                 [('n_t', ts_), 'modT', 'hm'], [('n_t', ts_)])
                        self.cp('dve', self.hT[:, kc, HL:HL + 1], tmp[:, 0:1], [('n_t', ts_)], [('hT', 1)])
                        self.cp('dve', self.hT[:, kc, HR:HR + 1], tmp[:, 1:2], [('n_t', ts_)], [('hT', 4)])
                        continue
                    self.act(self.hT[:, kc, c0:c0 + n_], tmp, AF.Identity, [('n_t', ts_), 'modT'], [('hT', gi)],
                             bias=self.modT[:, sh_off + kc, cls:cls + 1])
            if 'hT' in self.dbg:
                d = self.dscr("hT", [128, 8, HTC], BF16)
                self.dma(d, self.hT, [('hT', g_) for g_ in range(5)], ['d_hT'])
            self.t.barrier()

    def halo_exchange(self):
        xk = [('xT', gi) for gi in range(5)]
        self.dma(self.xb_src[:, 0:8], self.xT[:, :, LC:LC + 1].rearrange("p k o -> p (k o)"), xk, ['xb_src'], slow=True)
        self.dma(self.xb_src[:, 8:16], self.xT[:, :, T - 1:T].rearrange("p k o -> p (k o)"), xk, ['xb_src'], slow=True)
        self.t.cc(self.xb_src, self.xb_dst, ['xb_src'], ['xb_dst'])
        self.dma(self.xhal[:, :, 0:1].rearrange("p k o -> p (k o)"), self.xb_dst[0:128, 8:16], ['xb_dst'], ['xhal'], slow=True, q='pool')
        self.dma(self.xhal[:, :, 1:2].rearrange("p k o -> p (k o)"), self.xb_dst[128:256, 0:8], ['xb_dst'], ['xhal'], slow=True, q='pool')

    def hkeys(self, c0, c1):
        ks = []
        for gi, (t0, n_) in enumerate(self.groups()):
            a = hcol(t0) - 2
            b = hcol(t0) + n_ + 2
            if c0 < b and c1 > a:
                ks.append(('hT', gi))
        return ks

    def inproj_phase(self, l):
        nc = self.nc
        with nc.sbuf_tensor(f"j_wf{l}", [128, 2, NTM], F32) as wf_h, nc.sbuf_tensor(f"j_wb{l}", [128, 8, NTM], BF16) as wbt_h:
            self._wtf, self._wtb = wf_h.ap(), wbt_h.ap()
            self._inproj(l)

    def _inproj(self, l):
        nc = self.nc
        with nc.sbuf_tensor(f"i_ws{l}", [128, 4, 8, 128], F32) as ws_h, \
                nc.sbuf_tensor(f"i_wb{l}", [128, 4, 8, 128], BF16) as wb_h, \
                nc.sbuf_tensor(f"i_rp{l}", [128, 2, T], F32) as rp_h, \
                nc.sbuf_tensor(f"i_or{l}", [128, 2, T], BF16) as or_h, \
                nc.sbuf_tensor(f"i_cg{l}", [128, 4], F32) as cg_h, \
                nc.sbuf_tensor(f"i_cw{l}", [128, 12, 4], F32) as cw_h, \
                nc.sbuf_tensor(f"i_t1{l}", [128, 2, 512], F32) as t1_h, \
                nc.sbuf_tensor(f"i_t2{l}", [128, 2, 512], F32) as t2_h, \
                nc.sbuf_tensor(f"i_sq{l}", [128, 2, 2, 512], BF16) as sq_h, \
                nc.sbuf_tensor(f"i_xa{l}", [128, 4, 256], F32) as xa_h, \
                nc.sbuf_tensor(f"i_rs{l}", [128, 2, 512], F32) as rs_h:
            ws, wb, rp, orow = ws_h.ap(), wb_h.ap(), rp_h.ap(), or_h.ap()
            wtf = self._wtf
            wtb = self._wtb
            cg, cw, t1, t2, sq, rs = cg_h.ap(), cw_h.ap(), t1_h.ap(), t2_h.ap(), sq_h.ap(), rs_h.ap()
            xacc = xa_h.ap()
            self.dma(rp, self.rope.rearrange("a p t -> p a t"), (), ['i_rp'])
            self.dma(cg, self.cgain[l], (), ['i_cg'])
            self.dma(cw, self.cw[l], (), ['i_cw'])
            loaded = set()

            def load(ti):
                if ti >= NFM or ti in loaded:
                    return
                loaded.add(ti)
                sl = ti % 4
                self.dma(ws[:, sl], self.wfm[l, ti], (), [('i_ws', sl)])
                self.cp('pool', wb[:, sl], ws[:, sl], [('i_ws', sl)], [('i_wb', sl)])

            load(0); load(1)
            osl = 0
            dests = [self.qaT[:, m, :] for m in range(4)] + [self.kaT] + [self.qcT[:, m, :] for m in range(4)] + [self.kcT]
            for pr in range(10):
                tA, tB = 2 * pr, 2 * pr + 1
                load(tA + 2); load(tB + 2)
                isC = pr >= 5
                gq = 0 if pr < 9 else 2
                for gi, (t0, n_) in enumerate(self.groups()):
                    s = gi % 2
                    bA, bB, bM = 2 * s, 2 * s + 1, 4 + s
                    c0 = hcol(t0)
                    hk = [('hT', gi)]
                    pA, pB = self.ps[:, bA, 0:n_], self.ps[:, bB, 0:n_]
                    for kc in range(8):
                        self.mm(pA, wb[:, tA % 4, kc, :], self.hT[:, kc, c0:c0 + n_], kc == 0, kc == 7,
                                hk + [('i_wb', tA % 4)], [('ps', bA)])
                    for kc in range(8):
                        self.mm(pB, wb[:, tB % 4, kc, :], self.hT[:, kc, c0:c0 + n_], kc == 0, kc == 7,
                                hk + [('i_wb', tB % 4)], [('ps', bB)])
                    T1, T2 = rp[:, 0, t0:t0 + n_], rp[:, 1, t0:t0 + n_]
                    a1, a2 = t1[:, s, 0:n_], t2[:, s, 0:n_]
                    oo = orow[:, osl, t0:t0 + n_]
                    if not isC:
                        self.tt('dve', a1, pA, T1, ALU.mult, [('ps', bA), 'i_rp'], [('i_t1', s)])
                        self.tt('dve', a2, pB, T2, ALU.mult, [('ps', bB), 'i_rp'], [('i_t2', s)])
                        self.tt('pool', oo, a1, a2, ALU.add, [('i_t1', s), ('i_t2', s)], [('i_or', osl)])
                    else:
                        self.act(sq[:, s, 0, 0:n_], pA, AF.Square, [('ps', bA)], [('i_sq', s, 0)])
                        self.act(sq[:, s, 1, 0:n_], pB, AF.Square, [('ps', bB)], [('i_sq', s, 1)])
                        self.stt('dve', a1, pA, cg[:, gq:gq + 1], T1, ALU.mult, ALU.mult,
                                 [('ps', bA), 'i_rp', 'i_cg', ('i_sq', s, 0)], [('i_t1', s)])
                        self.stt('dve', a2, pB, cg[:, gq + 1:gq + 2], T2, ALU.mult, ALU.mult,
                                 [('ps', bB), 'i_rp', 'i_cg', ('i_sq', s, 1)], [('i_t2', s)])
                        pM = self.ps[:, bM, 0:n_]
                        self.mm(pM, self.bd_ms, sq[:, s, 0, 0:n_], True, False, [('i_sq', s, 0), 'bd_ms'], [('ps', bM)])
                        self.mm(pM, self.bd_ms, sq[:, s, 1, 0:n_], False, True, [('i_sq', s, 1), 'bd_ms'], [('ps', bM)])
                        rr = rs[:, s, 0:n_]
                        self.act(rr, pM, AF.Sqrt, [('ps', bM), 'epsc'], [('i_rs', s)], bias=self.epsc)
                        self.recip(rr, rr, [('i_rs', s)], [('i_rs', s)])
                        self.tt('pool', a1, a1, a2, ALU.add, [('i_t1', s), ('i_t2', s)], [('i_t1', s)])
                        self.tt('pool', oo, a1, rr, ALU.mult, [('i_t1', s), ('i_rs', s)], [('i_or', osl)])
                self.dma(dests[pr], orow[:, osl, :], [('i_or', osl)], [('dst_rope', pr)])
                if pr == 4:
                    self.dma(self.kaL, orow[:, osl, LC:T], [('i_or', osl)], ['kaL'])
                if pr == 9:
                    self.dma(self.kcL, orow[:, osl, LC:T], [('i_or', osl)], ['kcL'])
                osl ^= 1
            for j in range(12):
                ti = 20 + j
                load(ti + 1); load(ti + 2)
                pend = None
                for wi, (t0, n_) in enumerate(self.windows()):
                    s = wi % 4
                    bk = s
                    c0 = hcol(t0) - 1
                    hk = self.hkeys(c0, c0 + 258)
                    pt = self.ps[:, bk, 0:258]
                    for kc in range(8):
                        self.mm(pt, wb[:, ti % 4, kc, :], self.hT[:, kc, c0:c0 + 258], kc == 0, kc == 7,
                                hk + [('i_wb', ti % 4)], [('ps', bk)])
                    acc = xacc[:, s, :]
                    self.act(acc, pt[:, 0:256], AF.Identity, [('ps', bk), 'i_cw'], [('i_xa', s)], scale=cw[:, j, 0:1])
                    self.stt('dve', acc, pt[:, 1:257], cw[:, j, 1:2], acc, ALU.mult, ALU.add,
                             [('ps', bk), 'i_cw', ('i_xa', s)], [('i_xa', s)])
                    self.stt('dve', acc, pt[:, 2:258], cw[:, j, 2:3], acc, ALU.mult, ALU.add,
                             [('ps', bk), 'i_cw', ('i_xa', s)], [('i_xa', s)])
                    if pend is not None:
                        pend()
                    pend = (lambda acc=acc, s=s, t0=t0, osl=osl, j=j: self.act(
                        orow[:, osl, t0:t0 + 256], acc, AF.Silu, [('i_xa', s), 'i_cw'], [('i_or', osl)], bias=cw[:, j, 3:4]))
                pend()
                pend = None
                self.dma(self.xbcT[:, j, :], orow[:, osl, :], [('i_or', osl)], [('xbcT', j)])
                osl ^= 1
                if j == 1:
                    self.t.cc(self.kaL, self.kaG, ['kaL'], ['kaG'])
                    self.t.cc(self.kcL, self.kcG, ['kcL'], ['kcG'])
            for j in range(24):
                ti = 32 + j
                load(ti + 1); load(ti + 2)
                if j < 8:
                    self.dma(wtf[:, j % 2], self.wtm[l, :, j, :], (), [('j_wf', j % 2)])
                    self.cp('pool', wtb[:, j, :], wtf[:, j % 2], [('j_wf', j % 2)], [('j_wb', j)])
                for gi, (t0, n_) in enumerate(self.groups()):
                    bk = (j * 5 + gi) % 6
                    c0 = hcol(t0)
                    pt = self.ps[:, bk, 0:n_]
                    for kc in range(8):
                        self.mm(pt, wb[:, ti % 4, kc, :], self.hT[:, kc, c0:c0 + n_], kc == 0, kc == 7,
                                [('hT', gi), ('i_wb', ti % 4)], [('ps', bk)])
                    self.act(orow[:, osl, t0:t0 + n_], pt, AF.Sigmoid, [('ps', bk)], [('i_or', osl)])
                self.dma(self.gtT[:, j, :], orow[:, osl, :], [('i_or', osl)], [('gtT', j)])
                osl ^= 1
            self.t.barrier()
        with nc.sbuf_tensor(f"j_z{l}", [128, 2, 1024], F32) as z_h, \
                nc.sbuf_tensor(f"j_v{l}", [128, 2, 256], BF16) as v_h, \
                nc.sbuf_tensor(f"j_d{l}", [128, NB, 32], F32) as d_h, \
                nc.sbuf_tensor(f"j_db{l}", [128, 32], F32) as db_h:
            wb, zst, vst, dst, dbt = self._wtb, z_h.ap(), v_h.ap(), d_h.ap(), db_h.ap()
            self.dma(dbt, self.dtb[l], (), ['j_db'])
            wk = []
            for tb, cgps in [(tb_, (2,)) for tb_ in range(NB)] + [(-1, ())] + [(tb_, (0, 1)) for tb_ in range(NB)]:
                if tb < 0:
                    self.t.cc(self.vvL, self.vG, [('vvL', t_) for t_ in range(2, NB)], ['vG'])
                    continue
                s = tb % 2
                c0 = hcol(128 * tb)
                hk = self.hkeys(c0, c0 + 128)
                for cgp in cgps:
                    bk = (tb * 3 + cgp) % 4
                    n_ = 512 if cgp < 2 else 256
                    pt = self.ps[:, bk, 0:n_]
                    for kc in range(8):
                        self.mm(pt, self.hT[:, kc, c0:c0 + 128], wb[:, kc, 512 * cgp:512 * cgp + n_], kc == 0, kc == 7,
                                hk + wk, [('ps', bk)])
                    if cgp < 2:
                        self.act(zst[:, s, 512 * cgp:512 * cgp + 512], pt, AF.Silu, [('ps', bk)], [('j_z', s, cgp)])
                    else:
                        self.cp('dve', vst[:, s, :], pt, [('ps', bk)], [('j_v', s)])
                if 0 in cgps:
                    self.dma(self.zs[128 * tb:128 * tb + 128, :], zst[:, s, :], [('j_z', s, 0), ('j_z', s, 1)], [('zs', tb)])
                else:
                    self.dma(self.vv[128 * tb:128 * tb + 128, :], vst[:, s, :], [('j_v', s)], [('vv', tb)])
                    if tb >= 2:
                        self.dma(self.vvL[128 * (tb - 2):128 * (tb - 2) + 128, :], vst[:, s, :], [('j_v', s)], [('vvL', tb)])
            for tb in range(NB):
                bk = 4 + tb % 2
                c0 = hcol(128 * tb)
                hk = self.hkeys(c0, c0 + 128)
                pt = self.ps[:, bk, 0:32]
                for kc in range(8):
                    self.mm(pt, self.hT[:, kc, c0:c0 + 128], wb[:, kc, 1280:1312], kc == 0, kc == 7, hk + wk, [('ps', bk)])
                self.tt('dve', dst[:, tb, :], pt, dbt, ALU.add, [('ps', bk), 'j_db'], [('j_d', tb)])
            dk = [('j_d', tb) for tb in range(NB)]
            self.act(dst, dst, AF.Exp, dk, dk)
            self.act(dst, dst, AF.Ln, dk + ['onec'], dk, bias=self.onec)
            self.dma(self.dts.rearrange("(b p) c -> p b c", p=128), dst, dk, ['dts'])
            self.t.barrier()

    def attn_phase(self, l, which):
        nc = self.nc
        isA = which == 'A'
        do_ctx = l < DEPTH - 1
        qsrc, ksrc, ydst = (self.qaT, self.kaT, self.yaT) if isA else (self.qcT, self.kcT, self.ycT)
        voff = 0 if isA else 128
        nm = f"{which}{l}"
        LOOK = 3
        NS, NP = 4, 6
        with ExitStack() as stk:
            al = lambda name, shape, dt: stk.enter_context(nc.sbuf_tensor(f"{name}{nm}", shape, dt)).ap()
            if isA:
                NKB = 20
            else:
                NKB = 34
            kz = al("a_k", [128, 2, 128 * NKB], BF16)
            Q = al("a_q", [128, 4, T], BF16)
            vx = al("a_v", [128, NKB, 2, 128], BF16)
            P = al("a_p", [128, NP, 512], BF16)
            dsum = al("a_d", [128, 2, 512], F32)
            rden = al("a_r", [64, 2, 512], F32)
            lnd = al("a_l", [128, 2, 512], F32)
            yst = al("a_y", [64, 2, 512], BF16)
            sk = al("a_s", [128, 8], F32)
            es = al("a_e", [128, 2, 512], F32)
            mx = al("a_m", [128, 4, 128], BF16)
            self.memset('pool', kz[64:128, 0, :], 0.0, [('a_k', 0)])
            self.memset('pool', kz[0:64, 1, :], 0.0, [('a_k', 1)])
            self.memset('pool', vx[:, :, :, 64:128], 1.0, ['a_v1'])
            vsrc = lambda t_, a, b: t_[a:b, :].rearrange("(b p) c -> p b c", p=128)
            for g_ in range(2):
                r0, r1 = 64 * g_, 64 * g_ + 64
                vc = slice(voff + 64 * g_, voff + 64 * g_ + 64)
                self.dma(kz[r0:r1, g_, 0:LC], ksrc[r0:r1, 0:LC], (), [('a_k', g_)])
                self.dma(vx[:, 0:2, g_, 0:64], vsrc(self.vv, 0, LC)[:, :, vc], (), [('a_v', g_)])
                if isA:
                    self.dma(kz[r0:r1, g_, 256:384], self.kaG[r0:r1, LH - 128:LH], (), [('a_k', g_)])
                    self.dma(kz[r0:r1, g_, 384:384 + LH], ksrc[r0:r1, LC:T], (), [('a_k', g_)])
                    self.dma(kz[r0:r1, g_, 384 + LH:512 + LH], self.kaG[128 + r0:128 + r1, 0:128], (), [('a_k', g_)])
                    self.dma(vx[:, 2:3, g_, 0:64], vsrc(self.vG, LH - 128, LH)[:, :, vc], (), [('a_v', g_)])
                    self.dma(vx[:, 3:19, g_, 0:64], vsrc(self.vvL, 0, LH)[:, :, vc], (), [('a_v', g_)])
                    self.dma(vx[:, 19:20, g_, 0:64], vsrc(self.vG, LH, LH + 128)[:, :, vc], (), [('a_v', g_)])
                else:
                    self.dma(kz[r0:r1, g_, LC:LC + 2 * LH].rearrange("p (r t) -> p r t", r=2),
                             self.kcG.rearrange("(r p) t -> p r t", r=2)[r0:r1], (), [('a_k', g_)])
                    self.dma(vx[:, 2:34, g_, 0:64], vsrc(self.vG, 0, 2 * LH)[:, :, vc], (), [('a_v', g_)])
            self.dma(Q, qsrc, (), ['a_q'])
            self.cp('dve', mx[:, 0, :], self.msk_b[:, 2, :], ['msk_b'], [('a_m', 0)])
            self.cp('dve', mx[:, 1, :], self.msk_b[:, 0, :], ['msk_b'], [('a_m', 1)])
            self.ts('dve', mx[:, 2, :], self.msk_b[:, 2, :], self.hm[:, 0:1], ALU.mult, ['msk_b', 'hm'], [('a_m', 2)])
            self.ts('dve', mx[:, 3, :], self.msk_b[:, 0, :], self.hm[:, 1:2], ALU.mult, ['msk_b', 'hm'], [('a_m', 3)])
            if isA:
                self.dma(sk, self.sink[l], (), ['a_s'])
                self.act(sk, sk, AF.Exp, ['a_s'], ['a_s'])
                for kvh in range(2):
                    self.cp('dve', es[:, kvh, :].rearrange("p (a b) -> p a b", a=4),
                            sk[:, 4 * kvh:4 * kvh + 4].unsqueeze(2).to_broadcast([128, 4, 128]), ['a_s'], [('a_e', kvh)])
            steps = []
            for qb in range(NB):
                if qb < 2:
                    if not do_ctx:
                        continue
                    kbs = [(0, None), (1, None)]
                elif isA:
                    n = qb - 2
                    kbs = [(n + 2, 2 if n == 0 else 0), (n + 3, None), (n + 4, 3 if n == NB - 3 else 1),
                           (0, None), (1, None)]
                else:
                    kbs = [(kb, None) for kb in range(NKB)]
                for i, (kb, mk) in enumerate(kbs):
                    for kvh in range(2):
                        steps.append(dict(qb=qb, kb=kb, kvh=kvh, mk=mk, first=(i == 0), last=(i == len(kbs) - 1)))
            qseq = {}
            for st in steps:
                qseq.setdefault(st['qb'], len(qseq))
            for i, st in enumerate(steps):
                st['sb'] = i % NS
                st['pb'] = i % NP
                st['ob'] = 4 + 2 * (qseq[st['qb']] % 2) + st['kvh']

            def emit_S(st):
                kvh, qb, kb = st['kvh'], st['qb'], st['kb']
                out = self.ps[:, st['sb'], :].rearrange("p (a b) -> p a b", a=4)
                self.mm(out, kz[:, kvh, 128 * kb:128 * kb + 128], Q[:, :, 128 * qb:128 * qb + 128],
                        True, True, [('a_k', kvh), 'a_q'], [('ps', st['sb'])])
                pp = P[:, st['pb'], :]
                self.act(pp, self.ps[:, st['sb'], :], AF.Exp, [('ps', st['sb'])], [('a_p', st['pb'])], scale=0.125)
                if st['mk'] is not None:
                    self.tt('pool', pp.rearrange("p (a b) -> p a b", a=4), pp.rearrange("p (a b) -> p a b", a=4),
                            mx[:, st['mk'], :].unsqueeze(1).to_broadcast([128, 4, 128]), ALU.mult,
                            [('a_p', st['pb']), ('a_m', st['mk'])], [('a_p', st['pb'])])

            def emit_PV(st):
                kvh, qb, kb, ob = st['kvh'], st['qb'], st['kb'], st['ob']
                self.mm(self.ps[:, ob, :], vx[:, kb, kvh, :], P[:, st['pb'], :], st['first'], st['last'],
                        [('a_p', st['pb']), ('a_v', kvh), 'a_v1'], [('ps', ob)])
                if not st['last']:
                    return
                sl = kvh
                if isA:
                    self.tt('dve', dsum[64:128, sl, :], self.ps[64:128, ob, :], es[64:128, kvh, :], ALU.add,
                            [('ps', ob), ('a_e', kvh)], [('a_d', sl)])
                    self.act(lnd[64:128, sl, :], dsum[64:128, sl, :], AF.Ln, [('a_d', sl)], [('a_l', sl)])
                    self.act(rden[:, sl, :], lnd[64:128, sl, :], AF.Exp, [('a_l', sl)], [('a_r', sl)], scale=-1.0)
                else:
                    self.recip(rden[:, sl, :], self.ps[64:128, ob, :], [('ps', ob)], [('a_r', sl)])
                self.tt('dve', yst[:, sl, :], self.ps[0:64, ob, :], rden[:, sl, :], ALU.mult, [('ps', ob), ('a_r', sl)], [('a_y', sl)])
                ysv = yst[:, sl, :].rearrange("p (a b q) -> p a b q", a=2, b=2)
                for par in range(2):
                    self.dma(ydst[64 * par:64 * par + 64, 2 * kvh:2 * kvh + 2, 128 * qb:128 * qb + 128], ysv[:, :, par, :],
                             [('a_y', sl)], [('yT' + which, qb, kvh, par)])

            for i in range(min(LOOK, len(steps))):
                emit_S(steps[i])
            for i, st in enumerate(steps):
                if i + LOOK < len(steps):
                    emit_S(steps[i + LOOK])
                emit_PV(st)
            self.t.barrier()

    def ssm_phase(self, l):
        nc = self.nc
        do_ctx = l < DEPTH - 1
        sbs = self.dscr(f"sbs{l}", [2, NB, 128, 1024], F32)
        with ExitStack() as stk:
            al = lambda name, shape, dt: stk.enter_context(nc.sbuf_tensor(f"{name}{l}", shape, dt)).ap()
            BT = al("s_bt", [128, 2, T], BF16)
            CT = al("s_ct", [128, 2, T], BF16)
            dt_all = al("s_dt", [128, NB, 32], F32)
            da_all = al("s_da", [128, NB, 32], F32)
            E = al("s_E", [128, NB, 96], F32)
            acf = al("s_ac", [128, 32], F32)
            dsk = al("s_dk", [128, 16], F32)
            gn = al("s_gn", [128, 1024], F32)
            xf = al("s_xf", [128, 2, 8, 128], BF16)
            xf3 = al("s_xf3", [128, 3, 8, 128], BF16)
            xt = al("s_xt", [128, 2, 1024], BF16)
            bm = al("s_bm", [128, 2, 256], BF16)
            w = al("s_w", [128, 2, 32], F32)
            xw = al("s_xw", [128, 2, 2, 1024], BF16)
            H = al("s_H", [128, 2, 1024], F32)
            hb = al("s_hb", [128, 2, 2, 1024], BF16)
            sbt = al("s_sb", [128, 2, 1024], F32)
            R = al("s_R", [128, 2, 1024], F32)
            Lx = al("s_L", [128, 2, 1024], F32)
            M = al("s_M", [128, 4, 1024], BF16)
            ytmp = al("s_yt", [128, 1024], F32)
            ss = al("s_ss", [128, 2], F32)
            yn = al("s_yn", [128, 1024], BF16)
            yst = al("s_ys", [128, 2, 8, 128], BF16)
            xk = [('xbcT', j) for j in range(12)]
            self.dma(BT, self.xbcT[:, 8:10, :], xk, ['s_bt'])
            self.dma(CT, self.xbcT[:, 10:12, :], xk, ['s_ct'])
            self.dma(dt_all, self.dts.rearrange("(b p) c -> p b c", p=128), ['dts'], ['s_dt'])
            self.dma(acf, self.alog[l], (), ['s_ac'])
            self.dma(dsk, self.dsk[l], (), ['s_dk'])
            self.dma(gn, self.sng[l], (), ['s_gn'])
            self.act(acf, acf, AF.Exp, ['s_ac'], ['s_ac'])
            self.ts('dve', acf, acf, -1.0, ALU.mult, ['s_ac'], ['s_ac'])
            self.tt('dve', da_all, dt_all, acf.unsqueeze(1).to_broadcast([128, NB, 32]), ALU.mult, ['s_dt', 's_ac'], ['s_da'])
            self.memset('dve', H, 0.0, [('s_H', 0), ('s_H', 1)])
            psb = lambda b: self.ps[:, b, :].bitcast(BF16)

            def b16(ap):
                return ap.rearrange("p (h d) -> p h d", h=16)

            def bc16(ap):
                return ap.unsqueeze(2).to_broadcast([128, 16, 64])

            def load_xs(c, slot):
                self.dma(xf[:, slot], self.xbcT[:, 0:8, 128 * c:128 * c + 128], xk, [('s_xf', slot)])
                pt = psb(7)
                for f in range(8):
                    self.tr(pt[:, 128 * f:128 * f + 128], xf[:, slot, f, :], self.ident, [('s_xf', slot), 'ident'], [('ps', 7)])
                self.cp('act', xt[:, slot, :], pt, [('ps', 7)], [('s_xt', slot)])

            def p1_load(c):
                if c < NB:
                    self.dma(xf3[:, c % 3], self.xbcT[:, 0:8, 128 * c:128 * c + 128], xk, [('s_xf3', c % 3)])

            def p1_a(c):
                s = c % 2
                pt7 = psb(7)
                for f in range(8):
                    self.tr(pt7[:, 128 * f:128 * f + 128], xf3[:, c % 3, f, :], self.ident, [('s_xf3', c % 3), 'ident'], [('ps', 7)])
                self.cp('act', xt[:, s, :], pt7, [('ps', 7)], [('s_xt', s)])
                pt = psb(0)
                for g in range(2):
                    self.tr(pt[:, 128 * g:128 * g + 128], BT[:, g, 128 * c:128 * c + 128], self.ident, ['s_bt', 'ident'], [('ps', 0)])
                self.cp('dve', bm[:, s, :], pt[:, 0:256], [('ps', 0)], [('s_bm', s)])
                pc = self.ps[:, 1, :]
                for (c0, mi, d0, dn) in ((0, 0, 0, 16), (16, 1, 0, 16), (32, 2, 16, 16), (48, 3, 16, 16), (64, 4, 0, 32)):
                    self.mm(pc[:, c0:c0 + dn], self.msk_f[:, mi, :], da_all[:, c, d0:d0 + dn], True, True,
                            ['s_da', 'msk_f'], [('ps', 1)])
                self.act(E[:, c, :], pc[:, 0:96], AF.Exp, [('ps', 1)], [('s_E', c)])
                self.tt('dve', w[:, s, :].rearrange("p (a b) -> p a b", a=2), dt_all[:, c, :].rearrange("p (a b) -> p a b", a=2),
                        E[:, c, 16:80].rearrange("p (a b) -> p a b", a=2)[:, :, 0:16], ALU.mult, ['s_dt', ('s_E', c)], [('s_w', s)])
                for d in range(2):
                    self.tt('dve' if d == 0 else 'pool', b16(xw[:, s, d, :]), b16(xt[:, s, :]), bc16(w[:, s, 16 * d:16 * d + 16]), ALU.mult,
                            [('s_xt', s), ('s_w', s)], [('s_xw', s, d)])

            def p1_b(c):
                s = c % 2
                for d in range(2):
                    for g in range(2):
                        bk = 2 + 2 * d + g
                        self.mm(self.ps[:, bk, :], bm[:, s, 128 * g:128 * g + 128], xw[:, s, d, 512 * g:512 * g + 512], True, True,
                                [('s_bm', s), ('s_xw', s, d)], [('ps', bk)])
                for d in range(2):
                    self.cp('act', sbt[:, d, :], self.ps[:, 2 + 2 * d:4 + 2 * d, :].rearrange("p a b -> p (a b)"),
                            [('ps', 2 + 2 * d), ('ps', 3 + 2 * d)], [('s_sb', d)])
                    self.dma(sbs[d, c], sbt[:, d, :], [('s_sb', d)], [('sbs', d, c)])

            p1_load(0)
            p1_load(1)
            p1_a(0)
            for c in range(NB):
                p1_load(c + 2)
                if c + 1 < NB:
                    p1_a(c + 1)
                p1_b(c)
            self.dump("dbg_E", E, [('s_E', c_) for c_ in range(NB)])
            rstk = ExitStack()
            alr = lambda name, shape, dt: rstk.enter_context(nc.sbuf_tensor(f"{name}{l}", shape, dt)).ap()
            Hc = alr("s_Hc", [128, 2, 1024], F32)
            Gx = alr("s_Gx", [128, 2, 1024], F32)
            fwd_lat = list(range(2, NB))
            bwd_lat = list(range(NB - 1, 1, -1))

            sbr = alr("s_sbr", [128, 2, 4, 1024], F32)
            rk = [0]

            def recur(orders, store):
                n = len(orders[0])

                def ld(i):
                    if i >= n:
                        return
                    for d in range(2):
                        c = orders[d][i]
                        sl = (rk[0] + i) % 4
                        self.dma(sbr[:, d, sl, :], sbs[d, c], [('sbs', d, c)], [('s_sbr', d, sl)])
                for i in range(3):
                    ld(i)
                for i in range(n):
                    ld(i + 3)
                    for d in range(2):
                        c = orders[d][i]
                        sl = (rk[0] + i) % 4
                        if store:
                            hs = i % 2
                            self.cp('act', hb[:, d, hs, :], H[:, d, :], [('s_H', d)], [('s_hb', d, hs)])
                            self.dma(self.hst[d, c], hb[:, d, hs, :], [('s_hb', d, hs)], [('hst', d, c)])
                        self.tt('dve', b16(H[:, d, :]), b16(H[:, d, :]), bc16(E[:, c, 64 + 16 * d:80 + 16 * d]), ALU.mult,
                                [('s_H', d), ('s_E', c)], [('s_H', d)])
                        self.tt('dve', H[:, d, :], H[:, d, :], sbr[:, d, sl, :], ALU.add, [('s_H', d), ('s_sbr', d, sl)], [('s_H', d)])
                rk[0] += n

            recur(([0, 1], [1, 0]), True)
            for d in range(2):
                self.cp('dve', Hc[:, d, :], H[:, d, :], [('s_H', d)], [('s_Hc', d)])
            recur((fwd_lat, bwd_lat), False)
            self.dma(self.s_src.rearrange("p (d f) -> p d f", d=2), H, [('s_H', 0), ('s_H', 1)], ['s_src'])
            self.t.cc(self.s_src, self.s_dst, ['s_src'], ['s_dst'])
            self.dma(Gx[:, 0, :], self.s_dst[0:128, 0:1024], ['s_dst'], [('s_Gx', 0)])
            self.dma(Gx[:, 1, :], self.s_dst[128:256, 1024:2048], ['s_dst'], [('s_Gx', 1)])
            for d in range(2):
                own, oth = (1, 0) if d == 0 else (0, 1)
                self.ts('dve', Gx[:, d, :], Gx[:, d, :], self.hm[:, oth:oth + 1], ALU.mult, [('s_Gx', d), 'hm'], [('s_Gx', d)])
                self.stt('dve', H[:, d, :], Hc[:, d, :], self.hm[:, own:own + 1], Gx[:, d, :], ALU.mult, ALU.add,
                         [('s_Hc', d), 'hm', ('s_Gx', d)], [('s_H', d)])
            recur((fwd_lat, bwd_lat), True)
            self.t.barrier()
            rstk.close()
            zt3 = al("s_zt3", [128, 3, 1024], F32)
            hb3 = al("s_hb3", [128, 2, 3, 1024], BF16)

            def loads(c):
                s3 = c % 3
                tk_ = slice(128 * c, 128 * c + 128)
                self.dma(xf3[:, s3], self.xbcT[:, 0:8, tk_], xk, [('s_xf3', s3)])
                self.dma(zt3[:, s3, :], self.zs[tk_, :], [('zs', c)], [('s_z3', s3)])
                for d in range(2):
                    self.dma(hb3[:, d, s3, :], self.hst[d, c], [('hst', d, c)], [('s_hb3', d, s3)])
            ya2 = al("s_ya2", [128, 2, 1024], F32)
            yt2 = al("s_yt2", [128, 2, 1024], F32)
            cb2 = al("s_cb2", [128, 2, 2, 2, 128], F32)

            def front(c):
                s = c % 2
                s3 = c % 3
                tk = slice(128 * c, 128 * c + 128)
                pt_ = psb(7)
                for f in range(8):
                    self.tr(pt_[:, 128 * f:128 * f + 128], xf3[:, s3, f, :], self.ident, [('s_xf3', s3), 'ident'], [('ps', 7)])
                self.cp('act', xt[:, s, :], pt_, [('ps', 7)], [('s_xt', s)])
                for d in range(2):
                    self.tt('pool', b16(xw[:, s, d, :]), b16(xt[:, s, :]), bc16(dt_all[:, c, 16 * d:16 * d + 16]), ALU.mult,
                            [('s_xt', s), 's_dt'], [('s_xw', s, d)])
                pcb = self.ps[:, 0, 0:256]
                for g in range(2):
                    self.mm(pcb[:, 128 * g:128 * g + 128], BT[:, g, tk], CT[:, g, tk], True, True, ['s_bt', 's_ct'], [('ps', 0)])
                for d in range(2):
                    self.tt('dve', cb2[:, s, d, :, :], pcb.rearrange("p (g i) -> p g i", g=2),
                            self.msk_f[:, 0 if d == 0 else 2, :].unsqueeze(1).to_broadcast([128, 2, 128]), ALU.mult,
                            [('ps', 0), 'msk_f'], [('s_cb', s, d)])
                R4 = R.rearrange("p a (b f) -> p (a b) f", b=2)
                L4 = Lx.rearrange("p a (b f) -> p (a b) f", b=2)
                its = [(d, g, hf) for d in range(2) for g in range(2) for hf in range(2)]

                def emit_R(k):
                    d, g, hf = its[k]
                    sl = k % 4
                    h0 = 16 * d + 8 * g + 4 * hf
                    for hq in range(4):
                        self.act(R4[:, sl, 128 * hq:128 * hq + 128], self.msk_f[:, 0 if d == 0 else 2, :], AF.Identity,
                                 ['msk_f', 's_da'], [('s_R', sl, hq)], scale=da_all[:, c, h0 + hq:h0 + hq + 1])
                emit_R(0)
                emit_R(1)
                for k, (d, g, hf) in enumerate(its):
                    sl = k % 4
                    bk = 1 + k % 2
                    self.mm(self.ps[:, bk, :], self.msk_f[:, 1 if d == 0 else 3, :], R4[:, sl, :], True, True,
                            [('s_R', sl, hq) for hq in range(4)] + ['msk_f'], [('ps', bk)])
                    self.act(L4[:, sl, :], self.ps[:, bk, :], AF.Exp, [('ps', bk)], [('s_L', sl)])
                    if k + 2 < len(its):
                        emit_R(k + 2)
                    self.tt('dve', M[:, 2 * d + g, 512 * hf:512 * hf + 512].rearrange("p (h i) -> p h i", h=4),
                            L4[:, sl, :].rearrange("p (h i) -> p h i", h=4),
                            cb2[:, s, d, g, :].unsqueeze(1).to_broadcast([128, 4, 128]), ALU.mult,
                            [('s_L', sl), ('s_cb', s, d)], [('s_M', 2 * d + g, hf)])
                for g in range(2):
                    for hg in range(8):
                        hh = 8 * g + hg
                        for d in range(2):
                            self.mm(self.ps[:, 3 + g, 64 * hg:64 * hg + 64], M[:, 2 * d + g, 128 * hg:128 * hg + 128],
                                    xw[:, s, d, 64 * hh:64 * hh + 64], d == 0, d == 1,
                                    [('s_M', 2 * d + g, hg // 4), ('s_xw', s, d)], [('ps', 3 + g)])
                for d in range(2):
                    for g in range(2):
                        bk = 5 + (2 * d + g) % 2
                        self.mm(self.ps[:, bk, :], CT[:, g, tk], hb3[:, d, c % 3, 512 * g:512 * g + 512], True, True,
                                ['s_ct', ('s_hb3', d, c % 3)], [('ps', bk)])
                        dst = (ya2 if d == 0 else yt2)[:, s, 512 * g:512 * g + 512]
                        self.tt('dve', dst.rearrange("p (h e) -> p h e", h=8), self.ps[:, bk, :].rearrange("p (h e) -> p h e", h=8),
                                E[:, c, 32 * d + 8 * g:32 * d + 8 * g + 8].unsqueeze(2).to_broadcast([128, 8, 64]), ALU.mult,
                                [('ps', bk), ('s_E', c)], [('s_ya', s, g) if d == 0 else ('s_yt', s, g)])
                for g in range(2):
                    hs = slice(512 * g, 512 * g + 512)
                    self.tt('dve', ya2[:, s, hs], ya2[:, s, hs], self.ps[:, 3 + g, :], ALU.add,
                            [('s_ya', s, g), ('ps', 3 + g)], [('s_ya', s, g)])

            def tail(c):
                s = c % 2
                tk = slice(128 * c, 128 * c + 128)
                ya_ = ya2[:, s, :]
                yk = [('s_ya', s, 0), ('s_ya', s, 1)]
                self.tt('pool', ya_, ya_, yt2[:, s, :], ALU.add, yk + [('s_yt', s, 0), ('s_yt', s, 1)], yk)
                self.tt('pool', b16(ytmp), b16(xt[:, s, :]), bc16(dsk), ALU.mult, [('s_xt', s), 's_dk'], ['s_y3'])
                self.tt('pool', ya_, ya_, ytmp, ALU.add, yk + ['s_y3'], yk)
                self.tt('pool', ya_, ya_, zt3[:, c % 3, :], ALU.mult, yk + [('s_z3', c % 3)], yk)
                self.act(ytmp, ya_, AF.Square, yk, ['s_y3'])
                self.t.op('dve', lambda: nc.vector.reduce_sum(out=ss[:, 0:1], in_=ytmp, axis=mybir.AxisListType.X), ['s_y3'], ['s_ss'])
                self.act(ss[:, 1:2], ss[:, 0:1], AF.Ln, ['s_ss', 'epsc'], ['s_ss'], bias=self.epsc, scale=1.0 / 1024)
                self.act(ss[:, 1:2], ss[:, 1:2], AF.Exp, ['s_ss'], ['s_ss'], scale=-0.5)
                self.stt('dve', yn, ya_, ss[:, 1:2], gn, ALU.mult, ALU.mult, yk + ['s_ss', 's_gn'], ['s_yn'])
                pt = psb(0)
                for f in range(8):
                    self.tr(pt[:, 128 * f:128 * f + 128], yn[:, 128 * f:128 * f + 128], self.ident, ['s_yn', 'ident'], [('ps', 0)])
                self.cp('act', yst[:, s].rearrange("p f t -> p (f t)"), pt, [('ps', 0)], [('s_ys', s)])
                self.dma(self.ysT[:, :, tk], yst[:, s], [('s_ys', s)], [('ysT', c)])

            chunks = [c for c in range(NB) if not (c < 2 and not do_ctx)]
            loads(chunks[0])
            loads(chunks[1])
            front(chunks[0])
            for i, c in enumerate(chunks):
                if i + 2 < len(chunks):
                    loads(chunks[i + 2])
                if i + 1 < len(chunks):
                    front(chunks[i + 1])
                tail(c)
            self.t.barrier()

    def merge_phase(self, l, xsrc):
        nc = self.nc
        do_ctx = l < DEPTH - 1
        with ExitStack() as stk:
            al = lambda name, shape, dt: stk.enter_context(nc.sbuf_tensor(f"{name}{l}", shape, dt)).ap()
            woa = al("g_woa", [128, 4, 1024], BF16)
            woc = al("g_woc", [128, 4, 1024], BF16)
            wob = al("g_wob", [128, 8, 1024], BF16)
            wout = al("g_wout", [128, 8, 1024], BF16)
            stg = al("g_stg", [128, 4, 1024], F32)
            ya = al("g_ya", [128, 2, 4, 256], BF16)
            yc = al("g_yc", [128, 2, 4, 256], BF16)
            ys = al("g_ys", [128, 2, 8, 256], BF16)
            gt = al("g_gt", [128, 2, 24, 256], BF16)
            xg = al("g_xg", [128, 2, 8, 256], F32)
            mT = al("g_mT", [128, 2, 8, 256], BF16)
            ta = al("g_ta", [128, 2, 256], F32)
            tb = al("g_tb", [128, 2, 256], F32)
            tc_ = al("g_tc", [128, 2, 256], F32)
            k = 0
            for (wsrc, wdst, np_, nk) in ((self.woa[l], woa, 128, 4), (self.woc[l], woc, 128, 4), (self.wob[l], wob, 128, 8),
                                          (self.wout[l], wout, 128, 8)):
                for kc in range(nk):
                    sl = k % 4
                    eng_ = ('dve', 'act', 'dve', 'pool')[k % 4]
                    k += 1
                    self.dma(stg[0:np_, sl, :], wsrc[:, kc, :], (), [('g_stg', sl)])
                    self.cp(eng_, wdst[:, kc, :], stg[0:np_, sl, :], [('g_stg', sl)], [('g_w', id(wdst) % 1000, kc)])
            wk = lambda wdst: [('g_w', id(wdst) % 1000, kc) for kc in range(wdst.shape[1])]
            wins = [(wi, t0, n_) for wi, (t0, n_) in enumerate(self.windows()) if not (wi == 0 and not do_ctx)]

            def mg_loads(wi, t0, n_):
                gi = 0 if wi == 0 else 1 + (wi - 1) // 2
                s = wi % 2
                tk = slice(t0, t0 + n_)
                self.dma(ya[:, s, :, 0:n_], self.yaT[:, :, tk], ['yTA'], [('g_ya', s)])
                self.dma(yc[:, s, :, 0:n_], self.ycT[:, :, tk], ['yTC'], [('g_yc', s)])
                self.dma(ys[:, s, :, 0:n_], self.ysT[:, :, tk], ['ysT'], [('g_ys', s)])
                self.dma(gt[:, s, :, 0:n_], self.gtT[:, :, tk], ['gtT'], [('g_gt', s)])
                self.dma(xg[:, s, :, 0:n_], xsrc[:, :, tk], [('xT', gi)], [('g_xg', s)])

            mg_loads(*wins[0])
            for wpos, (wi, t0, n_) in enumerate(wins):
                if wpos + 1 < len(wins):
                    mg_loads(*wins[wpos + 1])
                gi = 0 if wi == 0 else 1 + (wi - 1) // 2
                s = wi % 2
                cls = 1 if t0 < LC else 0
                tk = slice(t0, t0 + n_)
                for j in range(8):
                    js = j % 2
                    cs = slice(128 * j, 128 * j + 128)
                    bA, bB, bC = 3 * js, 3 * js + 1, 3 * js + 2
                    for h in range(4):
                        self.mm(self.ps[:, bA, 0:n_], woa[:, h, cs], ya[:, s, h, 0:n_], h == 0, h == 3, wk(woa) + [('g_ya', s)], [('ps', bA)])
                    for kc in range(8):
                        self.mm(self.ps[:, bB, 0:n_], wob[:, kc, cs], ys[:, s, kc, 0:n_], kc == 0, kc == 7, wk(wob) + [('g_ys', s)], [('ps', bB)])
                    for h in range(4):
                        self.mm(self.ps[:, bC, 0:n_], woc[:, h, cs], yc[:, s, h, 0:n_], h == 0, h == 3, wk(woc) + [('g_yc', s)], [('ps', bC)])
                    self.tt('dve', ta[:, js, 0:n_], self.ps[:, bA, 0:n_], gt[:, s, j, 0:n_], ALU.mult, [('ps', bA), ('g_gt', s)], [('g_ta', js)])
                    self.tt('dve', tb[:, js, 0:n_], self.ps[:, bB, 0:n_], gt[:, s, 8 + j, 0:n_], ALU.mult, [('ps', bB), ('g_gt', s)], [('g_tb', js)])
                    self.tt('dve', tc_[:, js, 0:n_], self.ps[:, bC, 0:n_], gt[:, s, 16 + j, 0:n_], ALU.mult, [('ps', bC), ('g_gt', s)], [('g_tc', js)])
                    self.tt('pool', ta[:, js, 0:n_], ta[:, js, 0:n_], tb[:, js, 0:n_], ALU.add, [('g_ta', js), ('g_tb', js)], [('g_ta', js)])
                    self.tt('pool', mT[:, s, j, 0:n_], ta[:, js, 0:n_], tc_[:, js, 0:n_], ALU.add, [('g_ta', js), ('g_tc', js)], [('g_mT', s, j)])
                for j in range(8):
                    bk = 6 + j % 2
                    cs = slice(128 * j, 128 * j + 128)
                    for kc in range(8):
                        self.mm(self.ps[:, bk, 0:n_], wout[:, kc, cs], mT[:, s, kc, 0:n_], kc == 0, kc == 7,
                                wk(wout) + [('g_mT', s, kc)], [('ps', bk)])
                    self.stt('dve', xg[:, s, j, 0:n_], self.ps[:, bk, 0:n_], self.modT[:, 16 + j, cls:cls + 1], xg[:, s, j, 0:n_],
                             ALU.mult, ALU.add, [('ps', bk), 'modT', ('g_xg', s)], [('g_xg', s)])
                self.dma(self.xT[:, :, tk], xg[:, s, :, 0:n_], [('g_xg', s)], [('xT', gi)])
            self.t.barrier()

    def ffn_phase(self, l):
        nc = self.nc
        do_ctx = l < DEPTH - 1
        with ExitStack() as stk0:
            al0 = lambda name, shape, dt: stk0.enter_context(nc.sbuf_tensor(f"{name}{l}", shape, dt)).ap()
            wdn = al0("d_w", [128, 22, 1024], BF16)
            dstg = al0("d_stg", [128, 2, 1024], F32)
            self._ffn(l, wdn, dstg)

    def _ffn(self, l, wdn, dstg):
        nc = self.nc
        do_ctx = l < DEPTH - 1
        with ExitStack() as stk:
            al = lambda name, shape, dt: stk.enter_context(nc.sbuf_tensor(f"{name}{l}", shape, dt)).ap()
            ws = al("f_ws", [128, 4, 8, 128], F32)
            wb = al("f_wb", [128, 4, 8, 128], BF16)
            cw = al("f_cw", [128, 22, 4], F32)
            orow = al("f_or", [128, 2, T], BF16)
            acc = al("f_acc", [128, 4, 256], F32)
            sg = al("f_sg", [128, 4, 256], F32)
            self.dma(cw, self.fcw[l], (), ['f_cw'])

            def load(f):
                if f >= 22:
                    return
                for i, src in enumerate((self.wup, self.wgt)):
                    sl = (2 * f + i) % 4
                    self.dma(ws[:, sl], src[l, f], (), [('f_ws', sl)])
                    self.cp('pool', wb[:, sl], ws[:, sl], [('f_ws', sl)], [('f_wb', sl)])
            load(0)
            for f in range(22):
                load(f + 1)
                self.dma(dstg[:, f % 2, :], self.wdn[l, :, f, :], (), [('d_stg', f % 2)])
                self.cp('pool', wdn[:, f, :], dstg[:, f % 2, :], [('d_stg', f % 2)], [('d_w', f)])
                su, sg_ = (2 * f) % 4, (2 * f + 1) % 4
                osl = f % 2
                pend = None
                for wi, (t0, n_) in enumerate(self.windows()):
                    if wi == 0 and not do_ctx:
                        continue
                    s = wi % 4
                    bG, bU = 2 * s, 2 * s + 1
                    c0 = hcol(t0)
                    pG, pU = self.ps[:, bG, 0:258], self.ps[:, bU, 0:256]
                    hk = self.hkeys(c0 - 1, c0 + 257)
                    for kc in range(8):
                        self.mm(pG, wb[:, sg_, kc, :], self.hT[:, kc, c0 - 1:c0 + 257], kc == 0, kc == 7, hk + [('f_wb', sg_)], [('ps', bG)])
                    for kc in range(8):
                        self.mm(pU, wb[:, su, kc, :], self.hT[:, kc, c0:c0 + 256], kc == 0, kc == 7, hk + [('f_wb', su)], [('ps', bU)])
                    a = acc[:, s, :]
                    self.act(a, pG[:, 0:256], AF.Identity, [('ps', bG), 'f_cw'], [('f_acc', s)], scale=cw[:, f, 0:1])
                    self.stt('dve', a, pG[:, 1:257], cw[:, f, 1:2], a, ALU.mult, ALU.add, [('ps', bG), 'f_cw', ('f_acc', s)], [('f_acc', s)])
                    self.stt('dve', a, pG[:, 2:258], cw[:, f, 2:3], a, ALU.mult, ALU.add, [('ps', bG), 'f_cw', ('f_acc', s)], [('f_acc', s)])
                    if pend is not None:
                        pend()

                    def pend(a=a, s=s, t0=t0, osl=osl, f=f, pU=pU, bU=bU):
                        self.act(sg[:, s, :], a, AF.Silu, [('f_acc', s), 'f_cw'], [('f_sg', s)], bias=cw[:, f, 3:4])
                        self.tt('dve', orow[:, osl, t0:t0 + 256], sg[:, s, :], pU, ALU.mult, [('f_sg', s), ('ps', bU)], [('f_or', osl)])
                pend()
                pend = None
                self.dma(self.actT[:, f, :], orow[:, osl, :], [('f_or', osl)], [('actT', f)])
            self.t.barrier()
        with ExitStack() as stk:
            al = lambda name, shape, dt: stk.enter_context(nc.sbuf_tensor(f"{name}{l}", shape, dt)).ap()
            at = al("d_at", [128, 2, 22, 512], BF16)
            xg = al("d_xg", [128, 2, 8, 512], F32)
            wk = []
            ak = [('actT', f) for f in range(22)]
            grps = [(gi, t0, n_) for gi, (t0, n_) in enumerate(self.groups()) if not (gi == 0 and not do_ctx)]

            def dn_loads(gi, t0, n_):
                s = gi % 2
                tk = slice(t0, t0 + n_)
                self.dma(at[:, s, :, 0:n_], self.actT[:, :, tk], ak, [('d_at', s)])
                self.dma(xg[:, s, :, 0:n_], self.xT[:, :, tk], [('xT', gi)], [('d_xg', s)])

            dn_loads(*grps[0])
            for gpos, (gi, t0, n_) in enumerate(grps):
                if gpos + 1 < len(grps):
                    dn_loads(*grps[gpos + 1])
                s = gi % 2
                cls = 1 if t0 < LC else 0
                tk = slice(t0, t0 + n_)
                for j in range(8):
                    bk = j % 4
                    cs = slice(128 * j, 128 * j + 128)
                    for kc in range(22):
                        self.mm(self.ps[:, bk, 0:n_], wdn[:, kc, cs], at[:, s, kc, 0:n_], kc == 0, kc == 21, wk + [('d_at', s)], [('ps', bk)])
                    self.stt('dve', xg[:, s, j, 0:n_], self.ps[:, bk, 0:n_], self.modT[:, 40 + j, cls:cls + 1], xg[:, s, j, 0:n_],
                             ALU.mult, ALU.add, [('ps', bk), 'modT', ('d_xg', s)], [('d_xg', s)])
                self.dma(self.xT[:, :, tk], xg[:, s, :, 0:n_], [('d_xg', s)], [('xT', gi)])
            self.t.barrier()

    def final_norm(self):
        nc = self.nc
        with ExitStack() as stk:
            al = lambda name, shape, dt: stk.enter_context(nc.sbuf_tensor(name, shape, dt)).ap()
            x_sb = al("fn_x", [128, 2, 8, 512], F32)
            sq_sb = al("fn_sq", [128, 2, 8, 512], BF16)
            r_sb = al("fn_r", [128, 2, 512], F32)
            o_sb = al("fn_o", [128, 2, 8, 512], F32)
            g_sb = al("fn_g", [128, 8], F32)
            self.dma(g_sb, self.fng, (), ['fn_g'])
            fgr = [(gi, t0, n_) for gi, (t0, n_) in enumerate(self.groups()) if gi > 0]

            def fn_load(gi, t0, n_):
                self.dma(x_sb[:, gi % 2, :, 0:n_], self.xT[:, :, t0:t0 + n_], [('xT', gi)], [('fn_x', gi % 2)])

            fn_load(*fgr[0])
            for gpos, (gi, t0, n_) in enumerate(fgr):
                if gpos + 1 < len(fgr):
                    fn_load(*fgr[gpos + 1])
                s = gi % 2
                xg = x_sb[:, s, :, 0:n_]
                self.act(sq_sb[:, s, :, 0:n_], xg, AF.Square, [('fn_x', s)], [('fn_sq', s)])
                pt = self.ps[:, s, 0:n_]
                for kc in range(8):
                    self.mm(pt, self.ones_ms, sq_sb[:, s, kc, 0:n_], kc == 0, kc == 7, [('fn_sq', s), 'ones_ms'], [('ps', s)])
                rr = r_sb[:, s, 0:n_]
                self.act(rr, pt, AF.Ln, [('ps', s), 'epsc'], [('fn_r', s)], bias=self.epsc)
                self.act(rr, rr, AF.Exp, [('fn_r', s)], [('fn_r', s)], scale=-0.5)
                for kc in range(8):
                    self.stt('dve', o_sb[:, s, kc, 0:n_], xg[:, kc, :], g_sb[:, kc:kc + 1], rr, ALU.mult, ALU.mult,
                             [('fn_x', s), 'fn_g', ('fn_r', s)], [('fn_o', s, kc)])
                self.dma(self.outT[:, :, t0 - LC:t0 - LC + n_], o_sb[:, s, :, 0:n_], [('fn_o', s, kc) for kc in range(8)], [('outT', gi)])
            self.t.barrier()
        return []


def _fm(w):
    K, C = w.shape
    return np.ascontiguousarray(w.reshape(K // 128, 128, C // 128, 128).transpose(2, 1, 0, 3))


def _rowsp(v, kc):
    return np.ascontiguousarray(v.reshape(kc, 128).T)


def _rope_table(hf):
    rows = L // 64
    t_row = np.repeat(np.arange(rows), 64).astype(np.float32)
    t_col = np.tile(np.arange(64), rows).astype(np.float32)
    n = 16
    inv = (10000.0 ** (-np.arange(n, dtype=np.float32) / n)).astype(np.float32)
    ang = np.concatenate([t_row[:, None] * inv, t_col[:, None] * inv], axis=-1)
    ang = ang[hf * LH:(hf + 1) * LH]
    cos = np.concatenate([np.ones((LC, 32), np.float32), np.cos(ang).astype(np.float32)], 0).T
    sin = np.concatenate([np.zeros((LC, 32), np.float32), np.sin(ang).astype(np.float32)], 0).T
    t1 = np.concatenate([cos, sin, cos, sin], 0)
    t2 = np.concatenate([-sin, cos, -sin, cos], 0)
    return np.ascontiguousarray(np.stack([t1, t2], 0)).astype(np.float32)


def _const_tables():
    k = np.arange(128)[:, None]
    i = np.arange(128)[None, :]
    masks = np.stack([(k <= i), (k > i), (k >= i), (k < i), np.ones((128, 128), bool), (k == i)], 0).astype(np.float32)
    return masks


def _in_cols():
    o = {}
    s = 0
    for name, n in (('a_q', 512), ('a_k', 128), ('a_v', 128), ('b_z', 1024), ('b_xbc', 1536), ('b_dt', 32),
                    ('c_q', 512), ('c_k', 128), ('c_v', 128), ('gates', 3072)):
        o[name] = s
        s += n
    ev = np.arange(0, 64, 2)
    od = np.arange(1, 64, 2)
    tiles = []

    def rope_pair(base, h0, h1):
        a = np.concatenate([base + h0 * 64 + ev, base + h0 * 64 + ev, base + h1 * 64 + ev, base + h1 * 64 + ev])
        b = np.concatenate([base + h0 * 64 + od, base + h0 * 64 + od, base + h1 * 64 + od, base + h1 * 64 + od])
        tiles.append(a)
        tiles.append(b)
    for m in range(4):
        rope_pair(o['a_q'], m, m + 4)
    rope_pair(o['a_k'], 0, 1)
    for m in range(4):
        rope_pair(o['c_q'], m, m + 4)
    rope_pair(o['c_k'], 0, 1)
    for j in range(12):
        tiles.append(o['b_xbc'] + j * 128 + np.arange(128))
    for j in range(24):
        tiles.append(o['gates'] + j * 128 + np.arange(128))
    fm = np.concatenate(tiles)
    tm = np.concatenate([o['b_z'] + np.arange(1024), o['a_v'] + np.arange(128), o['c_v'] + np.arange(128),
                         o['b_dt'] + np.arange(32)])
    return fm, tm


def prep_shared(inp):
    f = lambda a: np.ascontiguousarray(a, dtype=np.float32)
    masks = _const_tables()
    fmc, tmc = _in_cols()
    ev = np.arange(0, 64, 2)
    od = np.arange(1, 64, 2)
    sh = {}
    sh["wmod"] = f(np.stack([_fm(inp["w_mod"][l]) for l in range(DEPTH)]))
    sh["bmod"] = f(np.stack([_rowsp(inp["b_mod"][l], 48) for l in range(DEPTH)]))
    sh["nrm"] = f(np.stack([np.stack([_rowsp(inp["norm1"][l], 8), _rowsp(inp["norm2"][l], 8)], 1) for l in range(DEPTH)]))
    sh["wfm"] = f(np.stack([_fm(inp["w_in"][l][:, fmc]) for l in range(DEPTH)]))
    sh["wtm"] = f(np.stack([inp["w_in"][l][:, tmc].reshape(8, 128, NTM).transpose(1, 0, 2) for l in range(DEPTH)]))
    cg = []
    for l in range(DEPTH):
        q, k = inp["c_q_norm"][l], inp["c_k_norm"][l]
        cg.append(np.stack([np.tile(q[ev], 4), np.tile(q[od], 4), np.tile(k[ev], 4), np.tile(k[od], 4)], 1))
    sh["cgain"] = f(np.stack(cg))
    sh["cw"] = f(np.stack([np.concatenate([inp["ssm_conv_w"][l], inp["ssm_conv_b"][l][None]], 0).reshape(4, 12, 128).transpose(2, 1, 0)
                           for l in range(DEPTH)]))
    sh["dtb"] = f(np.stack([np.broadcast_to(inp["ssm_dt_bias"][l].reshape(1, 32), (128, 32)) for l in range(DEPTH)]))
    sh["alog"] = f(np.stack([np.broadcast_to(inp["ssm_A_log"][l].reshape(1, 32), (128, 32)) for l in range(DEPTH)]))
    sh["dsk"] = f(np.stack([np.broadcast_to(inp["ssm_D"][l].reshape(1, 16), (128, 16)) for l in range(DEPTH)]))
    sh["sng"] = f(np.stack([np.broadcast_to(inp["ssm_norm"][l].reshape(1, 1024), (128, 1024)) for l in range(DEPTH)]))
    sh["sink"] = f(np.stack([np.broadcast_to(inp["a_sink"][l].reshape(1, 8), (128, 8)) for l in range(DEPTH)]))
    sh["woa"] = f(np.stack([inp["w_oa"][l].reshape(4, 128, 1024).transpose(1, 0, 2) for l in range(DEPTH)]))
    sh["woc"] = f(np.stack([inp["w_oc"][l].reshape(4, 128, 1024).transpose(1, 0, 2) for l in range(DEPTH)]))
    sh["wob"] = f(np.stack([inp["w_ob"][l].reshape(8, 128, 1024).transpose(1, 0, 2) for l in range(DEPTH)]))
    sh["wout"] = f(np.stack([inp["w_out"][l].reshape(8, 128, 1024).transpose(1, 0, 2) for l in range(DEPTH)]))
    sh["wup"] = f(np.stack([_fm(inp["ffn_w_up"][l]) for l in range(DEPTH)]))
    sh["wgt"] = f(np.stack([_fm(inp["ffn_w_gate"][l]) for l in range(DEPTH)]))
    sh["fcw"] = f(np.stack([np.concatenate([inp["ffn_conv_w"][l], inp["ffn_conv_b"][l][None]], 0).reshape(4, 22, 128).transpose(2, 1, 0)
                            for l in range(DEPTH)]))
    sh["wdn"] = f(np.stack([inp["ffn_w_down"][l].reshape(22, 128, 1024).transpose(1, 0, 2) for l in range(DEPTH)]))
    sh["fng"] = f(_rowsp(inp["final_norm"], 8))
    sh["masks"] = masks
    sh["ident"] = np.eye(128, dtype=np.float32)
    return sh


def prep_core(inp, c):
    b, hf = c // 2, c % 2
    xl = inp["x"][b]
    xa = np.concatenate([inp["ctx"][b], xl[hf * LH:(hf + 1) * LH]], 0)
    xT0 = np.ascontiguousarray(xa.T.reshape(8, 128, T).transpose(1, 0, 2), dtype=np.float32)
    cv = np.stack([_rowsp(inp["c"][b], 8), _rowsp(inp["c_ctx"], 8)], -1)
    zero = np.zeros((D,), np.float32)
    left = xl[hf * LH - 1] if hf == 1 else zero
    right = xl[(hf + 1) * LH] if hf == 0 else zero
    xh0 = np.stack([_rowsp(left, 8), _rowsp(right, 8)], -1)
    hmask = np.broadcast_to(np.array([[float(hf), float(1 - hf)]], np.float32), (128, 2))
    return {"xT0": xT0, "cvec": np.ascontiguousarray(cv, dtype=np.float32), "xh0": np.ascontiguousarray(xh0, dtype=np.float32),
            "hmask": np.ascontiguousarray(hmask, dtype=np.float32), "rope": _rope_table(hf)}


_PROG = None


def kernel(**inp):
    global _PROG
    inp = {k: np.asarray(v) for k, v in inp.items()}
    if _PROG is None:
        _PROG = Prog()
    p = _PROG
    sh = prep_shared(inp)
    in_maps = []
    for c in range(8):
        m = dict(sh)
        m.update(prep_core(inp, c))
        in_maps.append(m)
    res = run_bass_kernel_spmd(p.nc, in_maps, core_ids=list(range(8)))
    out = np.empty((4, L, D), np.float32)
    for c in range(8):
        oT = res.results[c]["outT"]
        out[c // 2, (c % 2) * LH:(c % 2 + 1) * LH] = oT.transpose(2, 1, 0).reshape(LH, D)
    return out
```

```python
from contextlib import ExitStack
import os
import numpy as np
import concourse.bass as bass
import concourse.mybir as mybir
from concourse.bass_utils import run_bass_kernel_spmd

F32 = mybir.dt.float32
BF16 = mybir.dt.bfloat16
ALU = mybir.AluOpType
AF = mybir.ActivationFunctionType

D = 1024
L = 4096
LH = 2048
LC = 256
T = LH + LC
NB = T // 128
PAIRS = [[0, 1], [2, 3], [4, 5], [6, 7]]
NOCC = False
DEPTH = 2
DFF = 2816
EPS = 1e-6
HTC = T + 6
HL = 259
HR = 260 + LH
NFM = 56
NTM = 1312


def hcol(i):
    return i + 2 if i < LC else i + 4


class Trk:
    ROT = 30000
    NDMA = 24

    def __init__(self, nc):
        self.nc = nc
        self.eng = {'pe': nc.tensor, 'act': nc.scalar, 'dve': nc.vector, 'pool': nc.gpsimd, 'sp': nc.sync}
        self.semh = []
        self.cur = {}
        self.cnt = {}
        for e in ('pe', 'act', 'dve', 'pool'):
            self.cur[e] = self._newsem(f"s_{e}")
            self.cnt[e] = 0
        self.dsem = [self._newsem(f"s_dma{i}") for i in range(self.NDMA)]
        self.dval = [0] * self.NDMA
        self.drr = 0
        self.known = {e: {} for e in self.eng}
        self.res = {}
        self.ninst = 0
        self.pesems = {self.cur['pe']}
        self.ccsem = None
        self.ccv = 0

    def _newsem(self, name):
        h = self.nc.alloc_semaphore(f"{name}_{len(self.semh)}")
        self.semh.append(h)
        return len(self.semh) - 1

    def _wait(self, e, tok):
        sid, val = tok
        if self.known[e].get(sid, 0) >= val:
            return
        self.eng[e].wait_ge(self.semh[sid], val)
        self.known[e][sid] = val

    def _deps(self, reads, writes):
        deps = {}

        def add(tok):
            if tok is None:
                return
            if deps.get(tok[0], 0) < tok[1]:
                deps[tok[0]] = tok[1]
        for r in reads:
            st = self.res.get(r)
            if st is not None:
                add(st[0])
        for w in writes:
            st = self.res.get(w)
            if st is not None:
                add(st[0])
                for s, v in st[1].items():
                    add((s, v))
        return deps

    def _commit(self, tok, reads, writes):
        for r in reads:
            st = self.res.setdefault(r, [None, {}])
            if st[1].get(tok[0], 0) < tok[1]:
                st[1][tok[0]] = tok[1]
        for w in writes:
            self.res[w] = [tok, {}]

    def op(self, e, fn, reads=(), writes=()):
        deps = self._deps(reads, writes)
        for s, v in deps.items():
            if e == 'pe' and s in self.pesems:
                continue
            self._wait(e, (s, v))
        inst = fn()
        if self.cnt[e] >= self.ROT:
            self.cur[e] = self._newsem(f"s_{e}")
            self.cnt[e] = 0
            if e == 'pe':
                self.pesems.add(self.cur[e])
        self.cnt[e] += 1
        tok = (self.cur[e], self.cnt[e])
        inst.then_inc(self.semh[tok[0]], 1)
        self._commit(tok, reads, writes)
        self.ninst += 1
        return tok

    def dma(self, out, in_, reads=(), writes=(), q='sp', slow=False):
        deps = self._deps(reads, writes)
        i = self.drr
        self.drr = (self.drr + 1) % self.NDMA
        if self.dval[i] > 0:
            deps[self.dsem[i]] = max(deps.get(self.dsem[i], 0), self.dval[i])
        for s, v in deps.items():
            self._wait(q, (s, v))
        if slow:
            inst = self.eng[q].dma_start(out=out, in_=in_, allow_slow_non_contiguous=True)
        else:
            inst = self.eng[q].dma_start(out=out, in_=in_)
        self.dval[i] += 16
        tok = (self.dsem[i], self.dval[i])
        inst.then_inc(self.semh[tok[0]], 16)
        self._commit(tok, reads, writes)
        self.ninst += 1
        return tok

    def cc(self, src, dst, reads=(), writes=()):
        if NOCC:
            return None
        deps = self._deps(reads, writes)
        for s_, v in deps.items():
            self._wait('pool', (s_, v))
        if self.ccsem is None:
            self.ccsem = self._newsem("s_cc")
            self.ccv = 0
        inst = self.nc.gpsimd.collective_compute("AllGather", ALU.bypass, replica_groups=PAIRS, ins=[src], outs=[dst])
        self.ccv += 1
        tok = (self.ccsem, self.ccv)
        inst.then_inc(self.semh[tok[0]])
        self._commit(tok, reads, writes)
        self.ninst += 1
        return tok

    def barrier(self):
        toks = {}
        for e in ('pe', 'act', 'dve', 'pool'):
            if self.cnt[e] > 0:
                toks[self.cur[e]] = self.cnt[e]
        for i in range(self.NDMA):
            if self.dval[i] > 0:
                toks[self.dsem[i]] = self.dval[i]
        if self.ccsem is not None and self.ccv > 0:
            toks[self.ccsem] = self.ccv
        for e in self.eng:
            for s, v in toks.items():
                self._wait(e, (s, v))
        self.res = {}

    def finish(self, toks):
        for t in toks:
            self._wait('sp', t)


class Prog:
    def __init__(self, dbg=(), nlayers=DEPTH, stop_after=None):
        self.dbg = set(dbg)
        self.nlayers = nlayers
        self.stop_after = stop_after
        nc = self.nc = bass.Bass("TRN2", target_bir_lowering=False)
        self.t = Trk(nc)
        self.outs = []
        self._uid = 0
        self.build()

    def din(self, name, shape, dt=F32):
        return self.nc.dram_tensor(name, list(shape), dt, kind="ExternalInput").ap()

    def dscr(self, name, shape, dt):
        if name in self.dbg:
            self.outs.append(name)
            return self.nc.dram_tensor(name, list(shape), dt, kind="ExternalOutput").ap()
        return self.nc.dram_tensor(name, list(shape), dt).ap()

    def sb(self, name, shape, dt):
        return self.nc.alloc_sbuf_tensor("sb_" + name, list(shape), dt).ap()

    def dump(self, name, ap, keys, dt=F32):
        if name in self.dbg:
            d = self.dscr(name, list(ap.shape), dt)
            self.dma(d, ap, keys, [('dump', name)])

    def uid(self):
        self._uid += 1
        return self._uid

    def mm(self, out, lhsT, rhs, start, stop, r, w):
        return self.t.op('pe', lambda: self.nc.tensor.matmul(out, lhsT=lhsT, rhs=rhs, start=start, stop=stop), r, w)

    def tr(self, out, in_, ident, r, w):
        return self.t.op('pe', lambda: self.nc.tensor.transpose(out, in_, ident), r, w)

    def act(self, out, in_, func, r, w, bias=None, scale=1.0, accum_out=None):
        kw = {}
        if bias is not None:
            kw['bias'] = bias
        if accum_out is not None:
            kw['accum_out'] = accum_out
        return self.t.op('act', lambda: self.nc.scalar.activation(out=out, in_=in_, func=func, scale=scale, **kw), r, w)

    def tt(self, e, out, in0, in1, op, r, w):
        eng = self.t.eng[e]
        return self.t.op(e, lambda: eng.tensor_tensor(out=out, in0=in0, in1=in1, op=op), r, w)

    def ts(self, e, out, in0, s1, op0, r, w, s2=None, op1=None):
        eng = self.t.eng[e]
        if op1 is None:
            return self.t.op(e, lambda: eng.tensor_scalar(out=out, in0=in0, scalar1=s1, scalar2=None, op0=op0), r, w)
        return self.t.op(e, lambda: eng.tensor_scalar(out=out, in0=in0, scalar1=s1, scalar2=s2, op0=op0, op1=op1), r, w)

    def stt(self, e, out, in0, scalar, in1, op0, op1, r, w):
        eng = self.t.eng[e]
        return self.t.op(e, lambda: eng.scalar_tensor_tensor(out=out, in0=in0, scalar=scalar, in1=in1, op0=op0, op1=op1), r, w)

    def cp(self, e, out, in_, r, w):
        eng = self.t.eng[e]
        if e == 'act':
            return self.t.op(e, lambda: eng.copy(out=out, in_=in_), r, w)
        return self.t.op(e, lambda: eng.tensor_copy(out=out, in_=in_), r, w)

    def recip(self, out, in_, r, w):
        return self.t.op('dve', lambda: self.nc.vector.reciprocal(out=out, in_=in_), r, w)

    def memset(self, e, ap, v, w):
        eng = self.t.eng[e]
        return self.t.op(e, lambda: eng.memset(ap, v), (), w)

    def dma(self, out, in_, r, w, slow=False, q='sp'):
        return self.t.dma(out, in_, r, w, q=q, slow=slow)

    def build(self):
        nc = self.nc
        self.xT0 = self.din("xT0", [128, 8, T])
        self.cvec = self.din("cvec", [128, 8, 2])
        self.xh0 = self.din("xh0", [128, 8, 2])
        self.hmask = self.din("hmask", [128, 2])
        self.wmod = self.din("wmod", [DEPTH, 48, 128, 8, 128])
        self.bmod = self.din("bmod", [DEPTH, 128, 48])
        self.nrm = self.din("nrm", [DEPTH, 128, 2, 8])
        self.wfm = self.din("wfm", [DEPTH, NFM, 128, 8, 128])
        self.wtm = self.din("wtm", [DEPTH, 128, 8, NTM])
        self.rope = self.din("rope", [2, 128, T])
        self.cgain = self.din("cgain", [DEPTH, 128, 4])
        self.cw = self.din("cw", [DEPTH, 128, 12, 4])
        self.dtb = self.din("dtb", [DEPTH, 128, 32])
        self.alog = self.din("alog", [DEPTH, 128, 32])
        self.dsk = self.din("dsk", [DEPTH, 128, 16])
        self.sng = self.din("sng", [DEPTH, 128, 1024])
        self.sink = self.din("sink", [DEPTH, 128, 8])
        self.woa = self.din("woa", [DEPTH, 128, 4, 1024])
        self.woc = self.din("woc", [DEPTH, 128, 4, 1024])
        self.wob = self.din("wob", [DEPTH, 128, 8, 1024])
        self.wout = self.din("wout", [DEPTH, 128, 8, 1024])
        self.wup = self.din("wup", [DEPTH, 22, 128, 8, 128])
        self.wgt = self.din("wgt", [DEPTH, 22, 128, 8, 128])
        self.fcw = self.din("fcw", [DEPTH, 128, 22, 4])
        self.wdn = self.din("wdn", [DEPTH, 128, 22, 1024])
        self.fng = self.din("fng", [128, 8])
        self.masks = self.din("masks", [6, 128, 128])
        self.ident_in = self.din("ident", [128, 128])
        self.outT = nc.dram_tensor("outT", [128, 8, LH], F32, kind="ExternalOutput").ap()

        self.xT = self.dscr("xT", [128, 8, T], F32)
        self.qaT = self.dscr("qaT", [128, 4, T], BF16)
        self.kaT = self.dscr("kaT", [128, T], BF16)
        self.qcT = self.dscr("qcT", [128, 4, T], BF16)
        self.kcT = self.dscr("kcT", [128, T], BF16)
        self.xbcT = self.dscr("xbcT", [128, 12, T], BF16)
        self.gtT = self.dscr("gtT", [128, 24, T], BF16)
        self.zs = self.dscr("zs", [T, 1024], F32)
        self.vv = self.dscr("vv", [T, 256], BF16)
        self.dts = self.dscr("dts", [T, 32], F32)
        self.yaT = self.dscr("yaT", [128, 4, T], BF16)
        self.ycT = self.dscr("ycT", [128, 4, T], BF16)
        self.ysT = self.dscr("ysT", [128, 8, T], BF16)
        self.hst = self.dscr("hst", [2, NB, 128, 1024], BF16)
        self.actT = self.dscr("actT", [128, 22, T], BF16)
        self.xhal = self.dscr("xhal", [128, 8, 2], F32)
        self.xb_src = self.dscr("xb_src", [128, 16], F32)
        self.xb_dst = self.dscr("xb_dst", [256, 16], F32)
        self.kaL = self.dscr("kaL", [128, LH], BF16)
        self.kcL = self.dscr("kcL", [128, LH], BF16)
        self.kaG = self.dscr("kaG", [256, LH], BF16)
        self.kcG = self.dscr("kcG", [256, LH], BF16)
        self.vvL = self.dscr("vvL", [LH, 256], BF16)
        self.vG = self.dscr("vG", [2 * LH, 256], BF16)
        self.s_src = self.dscr("s_src", [128, 2048], F32)
        self.s_dst = self.dscr("s_dst", [256, 2048], F32)

        self.ps = nc.alloc_psum_tensor("ps", [128, 8, 512], F32).ap()
        self.ident_f = self.sb("ident_f", [128, 128], F32)
        self.ident = self.sb("ident", [128, 128], BF16)
        self.ones_ms = self.sb("ones_ms", [128, 128], BF16)
        self.bd_ms = self.sb("bd_ms", [128, 128], BF16)
        self.ones_f = self.sb("ones_f", [128, 128], F32)
        self.msk_f = self.sb("msk_f", [128, 6, 128], F32)
        self.msk_b = self.sb("msk_b", [128, 6, 128], BF16)
        self.epsc = self.sb("epsc", [128, 1], F32)
        self.onec = self.sb("onec", [128, 1], F32)
        self.modT = self.sb("modT", [128, 48, 2], F32)
        self.gm = self.sb("gm", [128, 2, 8, 2], F32)
        self.hm = self.sb("hm", [128, 2], F32)

        self.consts()
        toks = []
        for l in range(self.nlayers):
            self.layer(l)
            if self.stop_after is not None and l == self.stop_after[0]:
                break
        if self.stop_after is None:
            toks = self.final_norm()
        self.t.barrier()

    def consts(self):
        self.dma(self.ident_f, self.ident_in, (), ['ident_f'])
        self.cp('dve', self.ident, self.ident_f, ['ident_f'], ['ident'])
        self.memset('dve', self.ones_ms, 1.0 / 1024, ['ones_ms'])
        self.memset('dve', self.bd_ms, 0.0, ['bd_ms'])
        self.memset('dve', self.bd_ms[0:64, 0:64], 1.0 / 128, ['bd_ms'])
        self.memset('dve', self.bd_ms[64:128, 64:128], 1.0 / 128, ['bd_ms'])
        self.memset('dve', self.ones_f, 1.0, ['ones_f'])
        self.memset('dve', self.epsc, EPS, ['epsc'])
        self.memset('dve', self.onec, 1.0, ['onec'])
        self.dma(self.msk_f, self.masks.rearrange("m p c -> p m c"), (), ['msk_f'])
        self.cp('dve', self.msk_b, self.msk_f, ['msk_f'], ['msk_b'])
        self.dma(self.hm, self.hmask, (), ['hm'])

    def layer(self, l):
        xsrc = self.xT0 if l == 0 else self.xT
        self.mod_phase(l)
        if self.stop_after == (l, 'mod'):
            return
        with self.nc.sbuf_tensor(f"hT{l}a", [128, 8, HTC], BF16) as hT_h:
            self.hT = hT_h.ap()
            self.memset('dve', self.hT, 0.0, [('hT', g_) for g_ in range(5)])
            self.norm_phase(l, 0, xsrc, self.xh0 if l == 0 else self.xhal)
            if self.stop_after == (l, 'n1'):
                return
            self.inproj_phase(l)
        if self.stop_after == (l, 'ip'):
            return
        self.attn_phase(l, 'A')
        self.attn_phase(l, 'C')
        if self.stop_after == (l, 'at'):
            return
        self.ssm_phase(l)
        if self.stop_after == (l, 'ss'):
            return
        self.merge_phase(l, xsrc)
        self.halo_exchange()
        if self.stop_after == (l, 'mg'):
            return
        with self.nc.sbuf_tensor(f"hT{l}b", [128, 8, HTC], BF16) as hT_h:
            self.hT = hT_h.ap()
            self.memset('dve', self.hT, 0.0, [('hT', g_) for g_ in range(5)])
            self.norm_phase(l, 1, self.xT, self.xhal)
            self.ffn_phase(l)
        if l < DEPTH - 1:
            self.halo_exchange()
        if self.stop_after == (l, 'ff'):
            return

    def mod_phase(self, l):
        nc = self.nc
        t = self.t
        with nc.sbuf_tensor(f"m_c{l}", [128, 8, 2], F32) as c_h, \
                nc.sbuf_tensor(f"m_sc{l}", [128, 8, 2], F32) as sc_h, \
                nc.sbuf_tensor(f"m_w{l}", [128, 4, 8, 128], F32) as w_h, \
                nc.sbuf_tensor(f"m_wb{l}", [128, 2, 8, 128], BF16) as wb_h, \
                nc.sbuf_tensor(f"m_scb{l}", [128, 8, 2], BF16) as scb_h, \
                nc.sbuf_tensor(f"m_b{l}", [128, 48], F32) as b_h, \
                nc.sbuf_tensor(f"m_n{l}", [128, 2, 8], F32) as n_h:
            c_sb, sc_sb, w_sb, b_sb, n_sb = c_h.ap(), sc_h.ap(), w_h.ap(), b_h.ap(), n_h.ap()
            self.dma(c_sb, self.cvec, (), ['m_c'])
            self.dma(b_sb, self.bmod[l], (), ['m_b'])
            self.dma(n_sb, self.nrm[l], (), ['m_n'])
            self.act(sc_sb, c_sb, AF.Silu, ['m_c'], ['m_sc'])
            wb_sb, scb = wb_h.ap(), scb_h.ap()
            self.cp('dve', scb, sc_sb, ['m_sc'], ['m_scb'])
            for j in range(3):
                self.dma(w_sb[:, j % 4], self.wmod[l, j], (), [('m_w', j % 4)])
            for j in range(48):
                s = j % 4
                if j + 3 < 48:
                    self.dma(w_sb[:, (j + 3) % 4], self.wmod[l, j + 3], (), [('m_w', (j + 3) % 4)])
                self.cp('dve', wb_sb[:, j % 2], w_sb[:, s], [('m_w', s)], [('m_wb', j % 2)])
                pt = self.ps[:, j % 4, 0:2]
                for kc in range(8):
                    self.mm(pt, wb_sb[:, j % 2, kc, :], scb[:, kc, :], kc == 0, kc == 7,
                            [('m_wb', j % 2), 'm_scb'], [('ps', j % 4)])
                self.ts('dve', self.modT[:, j, :], pt, b_sb[:, j:j + 1], ALU.add, [('ps', j % 4), 'm_b'], [('modT', j)])
            for n in range(2):
                sc_off = 8 + 24 * n
                for kc in range(8):
                    self.ts('dve', self.gm[:, n, kc, :], self.modT[:, sc_off + kc, :], 1.0, ALU.add, [('modT', sc_off + kc)], ['gm'])
                    self.ts('dve', self.gm[:, n, kc, :], self.gm[:, n, kc, :], n_sb[:, n, kc:kc + 1], ALU.mult,
                            ['gm', 'm_n'], ['gm'])
            if 'modT' in self.dbg:
                d = self.dscr("modT", [128, 48, 2], F32)
                self.dma(d, self.modT, ['modT'], ['d_modT'])
            t.barrier()

    @staticmethod
    def groups():
        g = [(0, LC)]
        for i in range(LH // 512):
            g.append((LC + 512 * i, 512))
        return g

    @staticmethod
    def windows():
        return [(256 * w, 256) for w in range(T // 256)]

    def norm_phase(self, l, n, xsrc, xhsrc):
        nc = self.nc
        sh_off = 0 if n == 0 else 24
        with nc.sbuf_tensor(f"n_x{l}{n}", [128, 2, 8, 512], F32) as x_h, \
                nc.sbuf_tensor(f"n_sq{l}{n}", [128, 2, 8, 512], BF16) as sq_h, \
                nc.sbuf_tensor(f"n_r{l}{n}", [128, 2, 512], F32) as r_h, \
                nc.sbuf_tensor(f"n_t{l}{n}", [128, 2, 512], F32) as t_h:
            x_sb, sq_sb, r_sb, t_sb = x_h.ap(), sq_h.ap(), r_h.ap(), t_h.ap()
            glist = list(enumerate(self.groups())) + [(99, (None, 2))]
            for gi, (t0, n_) in glist:
                halo = gi == 99
                s = gi % 2
                cls = 1 if (not halo and t0 < LC) else 0
                xg = x_sb[:, s, :, 0:n_]
                if halo:
                    self.dma(xg, xhsrc, ['xhal'], [('n_x', s)])
                else:
                    self.dma(xg, xsrc[:, :, t0:t0 + n_], [('xT', gi)], [('n_x', s)])
                self.act(sq_sb[:, s, :, 0:n_], xg, AF.Square, [('n_x', s)], [('n_sq', s)])
                pt = self.ps[:, s, 0:n_]
                for kc in range(8):
                    self.mm(pt, self.ones_ms, sq_sb[:, s, kc, 0:n_], kc == 0, kc == 7,
                            [('n_sq', s), 'ones_ms'], [('ps', s)])
                rr = r_sb[:, s, 0:n_]
                self.act(rr, pt, AF.Ln, [('ps', s), 'epsc'], [('n_r', s)], bias=self.epsc)
                self.act(rr, rr, AF.Exp, [('n_r', s)], [('n_r', s)], scale=-0.5)
                c0 = hcol(t0) if not halo else None
                for kc in range(8):
                    ts_ = kc % 2
                    tmp = t_sb[:, ts_, 0:n_]
                    self.stt('dve', tmp, xg[:, kc, :], self.gm[:, n, kc, cls:cls + 1], rr, ALU.mult, ALU.mult,
                             [('n_x', s), 'gm', ('n_r', s)], [('n_t', ts_)])
                    if halo:
                        self.stt('dve', tmp, tmp, self.modT[:, sh_off + kc, 0:1], self.hm, ALU.add, ALU.mult,
                                 [('n_t', ts_), 'modT', 'hm'], [('n_t', ts_)])
                        self.cp('dve', self.hT[:, kc, HL:HL + 1], tmp[:, 0:1], [('n_t', ts_)], [('hT', 1)])
                        self.cp('dve', self.hT[:, kc, HR:HR + 1], tmp[:, 1:2], [('n_t', ts_)], [('hT', 4)])
                        continue
                    self.act(self.hT[:, kc, c0:c0 + n_], tmp, AF.Identity, [('n_t', ts_), 'modT'], [('hT', gi)],
                             bias=self.modT[:, sh_off + kc, cls:cls + 1])
            if 'hT' in self.dbg:
                d = self.dscr("hT", [128, 8, HTC], BF16)
                self.dma(d, self.hT, [('hT', g_) for g_ in range(5)], ['d_hT'])
            self.t.barrier()

    def halo_exchange(self):
        xk = [('xT', gi) for gi in range(5)]
        self.dma(self.xb_src[:, 0:8], self.xT[:, :, LC:LC + 1].rearrange("p k o -> p (k o)"), xk, ['xb_src'], slow=True)
        self.dma(self.xb_src[:, 8:16], self.xT[:, :, T - 1:T].rearrange("p k o -> p (k o)"), xk, ['xb_src'], slow=True)
        self.t.cc(self.xb_src, self.xb_dst, ['xb_src'], ['xb_dst'])
        self.dma(self.xhal[:, :, 0:1].rearrange("p k o -> p (k o)"), self.xb_dst[0:128, 8:16], ['xb_dst'], ['xhal'], slow=True, q='pool')
        self.dma(self.xhal[:, :, 1:2].rearrange("p k o -> p (k o)"), self.xb_dst[128:256, 0:8], ['xb_dst'], ['xhal'], slow=True, q='pool')

    def hkeys(self, c0, c1):
        ks = []
        for gi, (t0, n_) in enumerate(self.groups()):
            a = hcol(t0) - 2
            b = hcol(t0) + n_ + 2
            if c0 < b and c1 > a:
                ks.append(('hT', gi))
        return ks

    def inproj_phase(self, l):
        nc = self.nc
        with nc.sbuf_tensor(f"j_wf{l}", [128, 2, NTM], F32) as wf_h, nc.sbuf_tensor(f"j_wb{l}", [128, 8, NTM], BF16) as wbt_h:
            self._wtf, self._wtb = wf_h.ap(), wbt_h.ap()
            self._inproj(l)

    def _inproj(self, l):
        nc = self.nc
        with nc.sbuf_tensor(f"i_ws{l}", [128, 4, 8, 128], F32) as ws_h, \
                nc.sbuf_tensor(f"i_wb{l}", [128, 4, 8, 128], BF16) as wb_h, \
                nc.sbuf_tensor(f"i_rp{l}", [128, 2, T], F32) as rp_h, \
                nc.sbuf_tensor(f"i_or{l}", [128, 2, T], BF16) as or_h, \
                nc.sbuf_tensor(f"i_cg{l}", [128, 4], F32) as cg_h, \
                nc.sbuf_tensor(f"i_cw{l}", [128, 12, 4], F32) as cw_h, \
                nc.sbuf_tensor(f"i_t1{l}", [128, 2, 512], F32) as t1_h, \
                nc.sbuf_tensor(f"i_t2{l}", [128, 2, 512], F32) as t2_h, \
                nc.sbuf_tensor(f"i_sq{l}", [128, 2, 2, 512], BF16) as sq_h, \
                nc.sbuf_tensor(f"i_xa{l}", [128, 4, 256], F32) as xa_h, \
                nc.sbuf_tensor(f"i_rs{l}", [128, 2, 512], F32) as rs_h:
            ws, wb, rp, orow = ws_h.ap(), wb_h.ap(), rp_h.ap(), or_h.ap()
            wtf = self._wtf
            wtb = self._wtb
            cg, cw, t1, t2, sq, rs = cg_h.ap(), cw_h.ap(), t1_h.ap(), t2_h.ap(), sq_h.ap(), rs_h.ap()
            xacc = xa_h.ap()
            self.dma(rp, self.rope.rearrange("a p t -> p a t"), (), ['i_rp'])
            self.dma(cg, self.cgain[l], (), ['i_cg'])
            self.dma(cw, self.cw[l], (), ['i_cw'])
            loaded = set()

            def load(ti):
                if ti >= NFM or ti in loaded:
                    return
                loaded.add(ti)
                sl = ti % 4
                self.dma(ws[:, sl], self.wfm[l, ti], (), [('i_ws', sl)])
                self.cp('pool', wb[:, sl], ws[:, sl], [('i_ws', sl)], [('i_wb', sl)])

            load(0); load(1)
            osl = 0
            dests = [self.qaT[:, m, :] for m in range(4)] + [self.kaT] + [self.qcT[:, m, :] for m in range(4)] + [self.kcT]
            for pr in range(10):
                tA, tB = 2 * pr, 2 * pr + 1
                load(tA + 2); load(tB + 2)
                isC = pr >= 5
                gq = 0 if pr < 9 else 2
                for gi, (t0, n_) in enumerate(self.groups()):
                    s = gi % 2
                    bA, bB, bM = 2 * s, 2 * s + 1, 4 + s
                    c0 = hcol(t0)
                    hk = [('hT', gi)]
                    pA, pB = self.ps[:, bA, 0:n_], self.ps[:, bB, 0:n_]
                    for kc in range(8):
                        self.mm(pA, wb[:, tA % 4, kc, :], self.hT[:, kc, c0:c0 + n_], kc == 0, kc == 7,
                                hk + [('i_wb', tA % 4)], [('ps', bA)])
                    for kc in range(8):
                        self.mm(pB, wb[:, tB % 4, kc, :], self.hT[:, kc, c0:c0 + n_], kc == 0, kc == 7,
                                hk + [('i_wb', tB % 4)], [('ps', bB)])
                    T1, T2 = rp[:, 0, t0:t0 + n_], rp[:, 1, t0:t0 + n_]
                    a1, a2 = t1[:, s, 0:n_], t2[:, s, 0:n_]
                    oo = orow[:, osl, t0:t0 + n_]
                    if not isC:
                        self.tt('dve', a1, pA, T1, ALU.mult, [('ps', bA), 'i_rp'], [('i_t1', s)])
                        self.tt('dve', a2, pB, T2, ALU.mult, [('ps', bB), 'i_rp'], [('i_t2', s)])
                        self.tt('pool', oo, a1, a2, ALU.add, [('i_t1', s), ('i_t2', s)], [('i_or', osl)])
                    else:
                        self.act(sq[:, s, 0, 0:n_], pA, AF.Square, [('ps', bA)], [('i_sq', s, 0)])
                        self.act(sq[:, s, 1, 0:n_], pB, AF.Square, [('ps', bB)], [('i_sq', s, 1)])
                        self.stt('dve', a1, pA, cg[:, gq:gq + 1], T1, ALU.mult, ALU.mult,
                                 [('ps', bA), 'i_rp', 'i_cg', ('i_sq', s, 0)], [('i_t1', s)])
                        self.stt('dve', a2, pB, cg[:, gq + 1:gq + 2], T2, ALU.mult, ALU.mult,
                                 [('ps', bB), 'i_rp', 'i_cg', ('i_sq', s, 1)], [('i_t2', s)])
                        pM = self.ps[:, bM, 0:n_]
                        self.mm(pM, self.bd_ms, sq[:, s, 0, 0:n_], True, False, [('i_sq', s, 0), 'bd_ms'], [('ps', bM)])
                        self.mm(pM, self.bd_ms, sq[:, s, 1, 0:n_], False, True, [('i_sq', s, 1), 'bd_ms'], [('ps', bM)])
                        rr = rs[:, s, 0:n_]
                        self.act(rr, pM, AF.Ln, [('ps', bM), 'epsc'], [('i_rs', s)], bias=self.epsc)
                        self.act(rr, rr, AF.Exp, [('i_rs', s)], [('i_rs', s)], scale=-0.5)
                        self.tt('pool', a1, a1, a2, ALU.add, [('i_t1', s), ('i_t2', s)], [('i_t1', s)])
                        self.tt('pool', oo, a1, rr, ALU.mult, [('i_t1', s), ('i_rs', s)], [('i_or', osl)])
                self.dma(dests[pr], orow[:, osl, :], [('i_or', osl)], [('dst_rope', pr)])
                if pr == 4:
                    self.dma(self.kaL, orow[:, osl, LC:T], [('i_or', osl)], ['kaL'])
                if pr == 9:
                    self.dma(self.kcL, orow[:, osl, LC:T], [('i_or', osl)], ['kcL'])
                osl ^= 1
            for j in range(12):
                ti = 20 + j
                load(ti + 1); load(ti + 2)
                pend = None
                for wi, (t0, n_) in enumerate(self.windows()):
                    s = wi % 4
                    bk = s
                    c0 = hcol(t0) - 1
                    hk = self.hkeys(c0, c0 + 258)
                    pt = self.ps[:, bk, 0:258]
                    for kc in range(8):
                        self.mm(pt, wb[:, ti % 4, kc, :], self.hT[:, kc, c0:c0 + 258], kc == 0, kc == 7,
                                hk + [('i_wb', ti % 4)], [('ps', bk)])
                    acc = xacc[:, s, :]
                    self.act(acc, pt[:, 0:256], AF.Identity, [('ps', bk), 'i_cw'], [('i_xa', s)], scale=cw[:, j, 0:1])
                    self.stt('dve', acc, pt[:, 1:257], cw[:, j, 1:2], acc, ALU.mult, ALU.add,
                             [('ps', bk), 'i_cw', ('i_xa', s)], [('i_xa', s)])
                    self.stt('dve', acc, pt[:, 2:258], cw[:, j, 2:3], acc, ALU.mult, ALU.add,
                             [('ps', bk), 'i_cw', ('i_xa', s)], [('i_xa', s)])
                    if pend is not None:
                        pend()
                    pend = (lambda acc=acc, s=s, t0=t0, osl=osl, j=j: self.act(
                        orow[:, osl, t0:t0 + 256], acc, AF.Silu, [('i_xa', s), 'i_cw'], [('i_or', osl)], bias=cw[:, j, 3:4]))
                pend()
                pend = None
                self.dma(self.xbcT[:, j, :], orow[:, osl, :], [('i_or', osl)], [('xbcT', j)])
                osl ^= 1
                if j == 1:
                    self.t.cc(self.kaL, self.kaG, ['kaL'], ['kaG'])
                    self.t.cc(self.kcL, self.kcG, ['kcL'], ['kcG'])
            for j in range(24):
                ti = 32 + j
                load(ti + 1); load(ti + 2)
                if j < 8:
                    self.dma(wtf[:, j % 2], self.wtm[l, :, j, :], (), [('j_wf', j % 2)])
                    self.cp('pool', wtb[:, j, :], wtf[:, j % 2], [('j_wf', j % 2)], [('j_wb', j)])
                for gi, (t0, n_) in enumerate(self.groups()):
                    bk = (j * 5 + gi) % 6
                    c0 = hcol(t0)
                    pt = self.ps[:, bk, 0:n_]
                    for kc in range(8):
                        self.mm(pt, wb[:, ti % 4, kc, :], self.hT[:, kc, c0:c0 + n_], kc == 0, kc == 7,
                                [('hT', gi), ('i_wb', ti % 4)], [('ps', bk)])
                    self.act(orow[:, osl, t0:t0 + n_], pt, AF.Sigmoid, [('ps', bk)], [('i_or', osl)])
                self.dma(self.gtT[:, j, :], orow[:, osl, :], [('i_or', osl)], [('gtT', j)])
                osl ^= 1
            self.t.barrier()
        with nc.sbuf_tensor(f"j_z{l}", [128, 2, 1024], F32) as z_h, \
                nc.sbuf_tensor(f"j_v{l}", [128, 2, 256], BF16) as v_h, \
                nc.sbuf_tensor(f"j_d{l}", [128, NB, 32], F32) as d_h, \
                nc.sbuf_tensor(f"j_db{l}", [128, 32], F32) as db_h:
            wb, zst, vst, dst, dbt = self._wtb, z_h.ap(), v_h.ap(), d_h.ap(), db_h.ap()
            self.dma(dbt, self.dtb[l], (), ['j_db'])
            wk = []
            for tb, cgps in [(tb_, (2,)) for tb_ in range(NB)] + [(-1, ())] + [(tb_, (0, 1)) for tb_ in range(NB)]:
                if tb < 0:
                    self.t.cc(self.vvL, self.vG, [('vvL', t_) for t_ in range(2, NB)], ['vG'])
                    continue
                s = tb % 2
                c0 = hcol(128 * tb)
                hk = self.hkeys(c0, c0 + 128)
                for cgp in cgps:
                    bk = (tb * 3 + cgp) % 4
                    n_ = 512 if cgp < 2 else 256
                    pt = self.ps[:, bk, 0:n_]
                    for kc in range(8):
                        self.mm(pt, self.hT[:, kc, c0:c0 + 128], wb[:, kc, 512 * cgp:512 * cgp + n_], kc == 0, kc == 7,
                                hk + wk, [('ps', bk)])
                    if cgp < 2:
                        self.act(zst[:, s, 512 * cgp:512 * cgp + 512], pt, AF.Silu, [('ps', bk)], [('j_z', s, cgp)])
                    else:
                        self.cp('dve', vst[:, s, :], pt, [('ps', bk)], [('j_v', s)])
                if 0 in cgps:
                    self.dma(self.zs[128 * tb:128 * tb + 128, :], zst[:, s, :], [('j_z', s, 0), ('j_z', s, 1)], [('zs', tb)])
                else:
                    self.dma(self.vv[128 * tb:128 * tb + 128, :], vst[:, s, :], [('j_v', s)], [('vv', tb)])
                    if tb >= 2:
                        self.dma(self.vvL[128 * (tb - 2):128 * (tb - 2) + 128, :], vst[:, s, :], [('j_v', s)], [('vvL', tb)])
            for tb in range(NB):
                bk = 4 + tb % 2
                c0 = hcol(128 * tb)
                hk = self.hkeys(c0, c0 + 128)
                pt = self.ps[:, bk, 0:32]
                for kc in range(8):
                    self.mm(pt, self.hT[:, kc, c0:c0 + 128], wb[:, kc, 1280:1312], kc == 0, kc == 7, hk + wk, [('ps', bk)])
                self.tt('dve', dst[:, tb, :], pt, dbt, ALU.add, [('ps', bk), 'j_db'], [('j_d', tb)])
            dk = [('j_d', tb) for tb in range(NB)]
            self.act(dst, dst, AF.Exp, dk, dk)
            self.act(dst, dst, AF.Ln, dk + ['onec'], dk, bias=self.onec)
            self.dma(self.dts.rearrange("(b p) c -> p b c", p=128), dst, dk, ['dts'])
            self.t.barrier()

    def attn_phase(self, l, which):
        nc = self.nc
        isA = which == 'A'
        do_ctx = l < DEPTH - 1
        qsrc, ksrc, ydst = (self.qaT, self.kaT, self.yaT) if isA else (self.qcT, self.kcT, self.ycT)
        voff = 0 if isA else 128
        nm = f"{which}{l}"
        LOOK = 3
        NS, NP = 4, 6
        with ExitStack() as stk:
            al = lambda name, shape, dt: stk.enter_context(nc.sbuf_tensor(f"{name}{nm}", shape, dt)).ap()
            if isA:
                NKB = 20
            else:
                NKB = 34
            kz = al("a_k", [128, 2, 128 * NKB], BF16)
            Q = al("a_q", [128, 4, T], BF16)
            vx = al("a_v", [128, NKB, 2, 128], BF16)
            P = al("a_p", [128, NP, 512], BF16)
            dsum = al("a_d", [128, 2, 512], F32)
            rden = al("a_r", [64, 2, 512], F32)
            lnd = al("a_l", [128, 2, 512], F32)
            yst = al("a_y", [64, 2, 512], BF16)
            sk = al("a_s", [128, 8], F32)
            es = al("a_e", [128, 2, 512], F32)
            mx = al("a_m", [128, 4, 128], BF16)
            self.memset('pool', kz[64:128, 0, :], 0.0, [('a_k', 0)])
            self.memset('pool', kz[0:64, 1, :], 0.0, [('a_k', 1)])
            self.memset('pool', vx[:, :, :, 64:128], 1.0, ['a_v1'])
            vsrc = lambda t_, a, b: t_[a:b, :].rearrange("(b p) c -> p b c", p=128)
            for g_ in range(2):
                r0, r1 = 64 * g_, 64 * g_ + 64
                vc = slice(voff + 64 * g_, voff + 64 * g_ + 64)
                self.dma(kz[r0:r1, g_, 0:LC], ksrc[r0:r1, 0:LC], (), [('a_k', g_)])
                self.dma(vx[:, 0:2, g_, 0:64], vsrc(self.vv, 0, LC)[:, :, vc], (), [('a_v', g_)])
                if isA:
                    self.dma(kz[r0:r1, g_, 256:384], self.kaG[r0:r1, LH - 128:LH], (), [('a_k', g_)])
                    self.dma(kz[r0:r1, g_, 384:384 + LH], ksrc[r0:r1, LC:T], (), [('a_k', g_)])
                    self.dma(kz[r0:r1, g_, 384 + LH:512 + LH], self.kaG[128 + r0:128 + r1, 0:128], (), [('a_k', g_)])
                    self.dma(vx[:, 2:3, g_, 0:64], vsrc(self.vG, LH - 128, LH)[:, :, vc], (), [('a_v', g_)])
                    self.dma(vx[:, 3:19, g_, 0:64], vsrc(self.vvL, 0, LH)[:, :, vc], (), [('a_v', g_)])
                    self.dma(vx[:, 19:20, g_, 0:64], vsrc(self.vG, LH, LH + 128)[:, :, vc], (), [('a_v', g_)])
                else:
                    self.dma(kz[r0:r1, g_, LC:LC + 2 * LH].rearrange("p (r t) -> p r t", r=2),
                             self.kcG.rearrange("(r p) t -> p r t", r=2)[r0:r1], (), [('a_k', g_)])
                    self.dma(vx[:, 2:34, g_, 0:64], vsrc(self.vG, 0, 2 * LH)[:, :, vc], (), [('a_v', g_)])
            self.dma(Q, qsrc, (), ['a_q'])
            self.cp('dve', mx[:, 0, :], self.msk_b[:, 2, :], ['msk_b'], [('a_m', 0)])
            self.cp('dve', mx[:, 1, :], self.msk_b[:, 0, :], ['msk_b'], [('a_m', 1)])
            self.ts('dve', mx[:, 2, :], self.msk_b[:, 2, :], self.hm[:, 0:1], ALU.mult, ['msk_b', 'hm'], [('a_m', 2)])
            self.ts('dve', mx[:, 3, :], self.msk_b[:, 0, :], self.hm[:, 1:2], ALU.mult, ['msk_b', 'hm'], [('a_m', 3)])
            if isA:
                self.dma(sk, self.sink[l], (), ['a_s'])
                self.act(sk, sk, AF.Exp, ['a_s'], ['a_s'])
                for kvh in range(2):
                    self.cp('dve', es[:, kvh, :].rearrange("p (a b) -> p a b", a=4),
                            sk[:, 4 * kvh:4 * kvh + 4].unsqueeze(2).to_broadcast([128, 4, 128]), ['a_s'], [('a_e', kvh)])
            steps = []
            for qb in range(NB):
                if qb < 2:
                    if not do_ctx:
                        continue
                    kbs = [(0, None), (1, None)]
                elif isA:
                    n = qb - 2
                    kbs = [(n + 2, 2 if n == 0 else 0), (n + 3, None), (n + 4, 3 if n == NB - 3 else 1),
                           (0, None), (1, None)]
                else:
                    kbs = [(kb, None) for kb in range(NKB)]
                for i, (kb, mk) in enumerate(kbs):
                    for kvh in range(2):
                        steps.append(dict(qb=qb, kb=kb, kvh=kvh, mk=mk, first=(i == 0), last=(i == len(kbs) - 1)))
            qseq = {}
            for st in steps:
                qseq.setdefault(st['qb'], len(qseq))
            for i, st in enumerate(steps):
                st['sb'] = i % NS
                st['pb'] = i % NP
                st['ob'] = 4 + 2 * (qseq[st['qb']] % 2) + st['kvh']

            def emit_S(st):
                kvh, qb, kb = st['kvh'], st['qb'], st['kb']
                out = self.ps[:, st['sb'], :].rearrange("p (a b) -> p a b", a=4)
                self.mm(out, kz[:, kvh, 128 * kb:128 * kb + 128], Q[:, :, 128 * qb:128 * qb + 128],
                        True, True, [('a_k', kvh), 'a_q'], [('ps', st['sb'])])
                pp = P[:, st['pb'], :]
                self.act(pp, self.ps[:, st['sb'], :], AF.Exp, [('ps', st['sb'])], [('a_p', st['pb'])], scale=0.125)
                if st['mk'] is not None:
                    self.tt('pool', pp.rearrange("p (a b) -> p a b", a=4), pp.rearrange("p (a b) -> p a b", a=4),
                            mx[:, st['mk'], :].unsqueeze(1).to_broadcast([128, 4, 128]), ALU.mult,
                            [('a_p', st['pb']), ('a_m', st['mk'])], [('a_p', st['pb'])])

            def emit_PV(st):
                kvh, qb, kb, ob = st['kvh'], st['qb'], st['kb'], st['ob']
                self.mm(self.ps[:, ob, :], vx[:, kb, kvh, :], P[:, st['pb'], :], st['first'], st['last'],
                        [('a_p', st['pb']), ('a_v', kvh), 'a_v1'], [('ps', ob)])
                if not st['last']:
                    return
                sl = kvh
                if isA:
                    self.tt('dve', dsum[64:128, sl, :], self.ps[64:128, ob, :], es[64:128, kvh, :], ALU.add,
                            [('ps', ob), ('a_e', kvh)], [('a_d', sl)])
                    self.act(lnd[64:128, sl, :], dsum[64:128, sl, :], AF.Ln, [('a_d', sl)], [('a_l', sl)])
                    self.act(rden[:, sl, :], lnd[64:128, sl, :], AF.Exp, [('a_l', sl)], [('a_r', sl)], scale=-1.0)
                else:
                    self.recip(rden[:, sl, :], self.ps[64:128, ob, :], [('ps', ob)], [('a_r', sl)])
                self.tt('dve', yst[:, sl, :], self.ps[0:64, ob, :], rden[:, sl, :], ALU.mult, [('ps', ob), ('a_r', sl)], [('a_y', sl)])
                ysv = yst[:, sl, :].rearrange("p (a b q) -> p a b q", a=2, b=2)
                for par in range(2):
                    self.dma(ydst[64 * par:64 * par + 64, 2 * kvh:2 * kvh + 2, 128 * qb:128 * qb + 128], ysv[:, :, par, :],
                             [('a_y', sl)], [('yT' + which, qb, kvh, par)])

            for i in range(min(LOOK, len(steps))):
                emit_S(steps[i])
            for i, st in enumerate(steps):
                if i + LOOK < len(steps):
                    emit_S(steps[i + LOOK])
                emit_PV(st)
            self.t.barrier()

    def ssm_phase(self, l):
        nc = self.nc
        do_ctx = l < DEPTH - 1
        sbs = self.dscr(f"sbs{l}", [2, NB, 128, 1024], F32)
        with ExitStack() as stk:
            al = lambda name, shape, dt: stk.enter_context(nc.sbuf_tensor(f"{name}{l}", shape, dt)).ap()
            BT = al("s_bt", [128, 2, T], BF16)
            CT = al("s_ct", [128, 2, T], BF16)
            dt_all = al("s_dt", [128, NB, 32], F32)
            da_all = al("s_da", [128, NB, 32], F32)
            E = al("s_E", [128, NB, 96], F32)
            acf = al("s_ac", [128, 32], F32)
            dsk = al("s_dk", [128, 16], F32)
            gn = al("s_gn", [128, 1024], F32)
            xf = al("s_xf", [128, 2, 8, 128], BF16)
            xf3 = al("s_xf3", [128, 3, 8, 128], BF16)
            xt = al("s_xt", [128, 2, 1024], BF16)
            bm = al("s_bm", [128, 2, 256], BF16)
            w = al("s_w", [128, 2, 32], F32)
            xw = al("s_xw", [128, 2, 2, 1024], BF16)
            H = al("s_H", [128, 2, 1024], F32)
            hb = al("s_hb", [128, 2, 2, 1024], BF16)
            sbt = al("s_sb", [128, 2, 1024], F32)
            R = al("s_R", [128, 2, 1024], F32)
            Lx = al("s_L", [128, 2, 1024], F32)
            M = al("s_M", [128, 4, 1024], BF16)
            ytmp = al("s_yt", [128, 1024], F32)
            ss = al("s_ss", [128, 2], F32)
            yn = al("s_yn", [128, 1024], BF16)
            yst = al("s_ys", [128, 2, 8, 128], BF16)
            xk = [('xbcT', j) for j in range(12)]
            self.dma(BT, self.xbcT[:, 8:10, :], xk, ['s_bt'])
            self.dma(CT, self.xbcT[:, 10:12, :], xk, ['s_ct'])
            self.dma(dt_all, self.dts.rearrange("(b p) c -> p b c", p=128), ['dts'], ['s_dt'])
            self.dma(acf, self.alog[l], (), ['s_ac'])
            self.dma(dsk, self.dsk[l], (), ['s_dk'])
            self.dma(gn, self.sng[l], (), ['s_gn'])
            self.act(acf, acf, AF.Exp, ['s_ac'], ['s_ac'])
            self.ts('dve', acf, acf, -1.0, ALU.mult, ['s_ac'], ['s_ac'])
            self.tt('dve', da_all, dt_all, acf.unsqueeze(1).to_broadcast([128, NB, 32]), ALU.mult, ['s_dt', 's_ac'], ['s_da'])
            self.memset('dve', H, 0.0, [('s_H', 0), ('s_H', 1)])
            psb = lambda b: self.ps[:, b, :].bitcast(BF16)

            def b16(ap):
                return ap.rearrange("p (h d) -> p h d", h=16)

            def bc16(ap):
                return ap.unsqueeze(2).to_broadcast([128, 16, 64])

            def load_xs(c, slot):
                self.dma(xf[:, slot], self.xbcT[:, 0:8, 128 * c:128 * c + 128], xk, [('s_xf', slot)])
                pt = psb(7)
                for f in range(8):
                    self.tr(pt[:, 128 * f:128 * f + 128], xf[:, slot, f, :], self.ident, [('s_xf', slot), 'ident'], [('ps', 7)])
                self.cp('act', xt[:, slot, :], pt, [('ps', 7)], [('s_xt', slot)])

            def p1_load(c):
                if c < NB:
                    self.dma(xf3[:, c % 3], self.xbcT[:, 0:8, 128 * c:128 * c + 128], xk, [('s_xf3', c % 3)])

            def p1_a(c):
                s = c % 2
                pt7 = psb(7)
                for f in range(8):
                    self.tr(pt7[:, 128 * f:128 * f + 128], xf3[:, c % 3, f, :], self.ident, [('s_xf3', c % 3), 'ident'], [('ps', 7)])
                self.cp('act', xt[:, s, :], pt7, [('ps', 7)], [('s_xt', s)])
                pt = psb(0)
                for g in range(2):
                    self.tr(pt[:, 128 * g:128 * g + 128], BT[:, g, 128 * c:128 * c + 128], self.ident, ['s_bt', 'ident'], [('ps', 0)])
                self.cp('dve', bm[:, s, :], pt[:, 0:256], [('ps', 0)], [('s_bm', s)])
                pc = self.ps[:, 1, :]
                for (c0, mi, d0, dn) in ((0, 0, 0, 16), (16, 1, 0, 16), (32, 2, 16, 16), (48, 3, 16, 16), (64, 4, 0, 32)):
                    self.mm(pc[:, c0:c0 + dn], self.msk_f[:, mi, :], da_all[:, c, d0:d0 + dn], True, True,
                            ['s_da', 'msk_f'], [('ps', 1)])
                self.act(E[:, c, :], pc[:, 0:96], AF.Exp, [('ps', 1)], [('s_E', c)])
                self.tt('dve', w[:, s, :].rearrange("p (a b) -> p a b", a=2), dt_all[:, c, :].rearrange("p (a b) -> p a b", a=2),
                        E[:, c, 16:80].rearrange("p (a b) -> p a b", a=2)[:, :, 0:16], ALU.mult, ['s_dt', ('s_E', c)], [('s_w', s)])
                for d in range(2):
                    self.tt('dve' if d == 0 else 'pool', b16(xw[:, s, d, :]), b16(xt[:, s, :]), bc16(w[:, s, 16 * d:16 * d + 16]), ALU.mult,
                            [('s_xt', s), ('s_w', s)], [('s_xw', s, d)])

            def p1_b(c):
                s = c % 2
                for d in range(2):
                    for g in range(2):
                        bk = 2 + 2 * d + g
                        self.mm(self.ps[:, bk, :], bm[:, s, 128 * g:128 * g + 128], xw[:, s, d, 512 * g:512 * g + 512], True, True,
                                [('s_bm', s), ('s_xw', s, d)], [('ps', bk)])
                for d in range(2):
                    self.cp('act', sbt[:, d, :], self.ps[:, 2 + 2 * d:4 + 2 * d, :].rearrange("p a b -> p (a b)"),
                            [('ps', 2 + 2 * d), ('ps', 3 + 2 * d)], [('s_sb', d)])
                    self.dma(sbs[d, c], sbt[:, d, :], [('s_sb', d)], [('sbs', d, c)])

            p1_load(0)
            p1_load(1)
            p1_a(0)
            for c in range(NB):
                p1_load(c + 2)
                if c + 1 < NB:
                    p1_a(c + 1)
                p1_b(c)
            self.dump("dbg_E", E, [('s_E', c_) for c_ in range(NB)])
            rstk = ExitStack()
            alr = lambda name, shape, dt: rstk.enter_context(nc.sbuf_tensor(f"{name}{l}", shape, dt)).ap()
            Hc = alr("s_Hc", [128, 2, 1024], F32)
            Gx = alr("s_Gx", [128, 2, 1024], F32)
            fwd_lat = list(range(2, NB))
            bwd_lat = list(range(NB - 1, 1, -1))

            sbr = alr("s_sbr", [128, 2, 4, 1024], F32)
            rk = [0]

            def recur(orders, store):
                n = len(orders[0])

                def ld(i):
                    if i >= n:
                        return
                    for d in range(2):
                        c = orders[d][i]
                        sl = (rk[0] + i) % 4
                        self.dma(sbr[:, d, sl, :], sbs[d, c], [('sbs', d, c)], [('s_sbr', d, sl)])
                for i in range(3):
                    ld(i)
                for i in range(n):
                    ld(i + 3)
                    for d in range(2):
                        c = orders[d][i]
                        sl = (rk[0] + i) % 4
                        if store:
                            hs = i % 2
                            self.cp('act', hb[:, d, hs, :], H[:, d, :], [('s_H', d)], [('s_hb', d, hs)])
                            self.dma(self.hst[d, c], hb[:, d, hs, :], [('s_hb', d, hs)], [('hst', d, c)])
                        self.tt('dve', b16(H[:, d, :]), b16(H[:, d, :]), bc16(E[:, c, 64 + 16 * d:80 + 16 * d]), ALU.mult,
                                [('s_H', d), ('s_E', c)], [('s_H', d)])
                        self.tt('dve', H[:, d, :], H[:, d, :], sbr[:, d, sl, :], ALU.add, [('s_H', d), ('s_sbr', d, sl)], [('s_H', d)])
                rk[0] += n

            recur(([0, 1], [1, 0]), True)
            for d in range(2):
                self.cp('dve', Hc[:, d, :], H[:, d, :], [('s_H', d)], [('s_Hc', d)])
            recur((fwd_lat, bwd_lat), False)
            self.dma(self.s_src.rearrange("p (d f) -> p d f", d=2), H, [('s_H', 0), ('s_H', 1)], ['s_src'])
            self.t.cc(self.s_src, self.s_dst, ['s_src'], ['s_dst'])
            self.dma(Gx[:, 0, :], self.s_dst[0:128, 0:1024], ['s_dst'], [('s_Gx', 0)])
            self.dma(Gx[:, 1, :], self.s_dst[128:256, 1024:2048], ['s_dst'], [('s_Gx', 1)])
            for d in range(2):
                own, oth = (1, 0) if d == 0 else (0, 1)
                self.ts('dve', Gx[:, d, :], Gx[:, d, :], self.hm[:, oth:oth + 1], ALU.mult, [('s_Gx', d), 'hm'], [('s_Gx', d)])
                self.stt('dve', H[:, d, :], Hc[:, d, :], self.hm[:, own:own + 1], Gx[:, d, :], ALU.mult, ALU.add,
                         [('s_Hc', d), 'hm', ('s_Gx', d)], [('s_H', d)])
            recur((fwd_lat, bwd_lat), True)
            self.t.barrier()
            rstk.close()
            zt3 = al("s_zt3", [128, 3, 1024], F32)
            hb3 = al("s_hb3", [128, 2, 3, 1024], BF16)

            def loads(c):
                s3 = c % 3
                tk_ = slice(128 * c, 128 * c + 128)
                self.dma(xf3[:, s3], self.xbcT[:, 0:8, tk_], xk, [('s_xf3', s3)])
                self.dma(zt3[:, s3, :], self.zs[tk_, :], [('zs', c)], [('s_z3', s3)])
                for d in range(2):
                    self.dma(hb3[:, d, s3, :], self.hst[d, c], [('hst', d, c)], [('s_hb3', d, s3)])
            ya2 = al("s_ya2", [128, 2, 1024], F32)
            yt2 = al("s_yt2", [128, 2, 1024], F32)
            cb2 = al("s_cb2", [128, 2, 2, 2, 128], F32)

            def front(c):
                s = c % 2
                s3 = c % 3
                tk = slice(128 * c, 128 * c + 128)
                pt_ = psb(7)
                for f in range(8):
                    self.tr(pt_[:, 128 * f:128 * f + 128], xf3[:, s3, f, :], self.ident, [('s_xf3', s3), 'ident'], [('ps', 7)])
                self.cp('act', xt[:, s, :], pt_, [('ps', 7)], [('s_xt', s)])
                for d in range(2):
                    self.tt('pool', b16(xw[:, s, d, :]), b16(xt[:, s, :]), bc16(dt_all[:, c, 16 * d:16 * d + 16]), ALU.mult,
                            [('s_xt', s), 's_dt'], [('s_xw', s, d)])
                pcb = self.ps[:, 0, 0:256]
                for g in range(2):
                    self.mm(pcb[:, 128 * g:128 * g + 128], BT[:, g, tk], CT[:, g, tk], True, True, ['s_bt', 's_ct'], [('ps', 0)])
                for d in range(2):
                    self.tt('dve', cb2[:, s, d, :, :], pcb.rearrange("p (g i) -> p g i", g=2),
                            self.msk_f[:, 0 if d == 0 else 2, :].unsqueeze(1).to_broadcast([128, 2, 128]), ALU.mult,
                            [('ps', 0), 'msk_f'], [('s_cb', s, d)])
                R4 = R.rearrange("p a (b f) -> p (a b) f", b=2)
                L4 = Lx.rearrange("p a (b f) -> p (a b) f", b=2)
                its = [(d, g, hf) for d in range(2) for g in range(2) for hf in range(2)]

                def emit_R(k):
                    d, g, hf = its[k]
                    sl = k % 4
                    h0 = 16 * d + 8 * g + 4 * hf
                    for hq in range(4):
                        self.act(R4[:, sl, 128 * hq:128 * hq + 128], self.msk_f[:, 0 if d == 0 else 2, :], AF.Identity,
                                 ['msk_f', 's_da'], [('s_R', sl, hq)], scale=da_all[:, c, h0 + hq:h0 + hq + 1])
                emit_R(0)
                emit_R(1)
                for k, (d, g, hf) in enumerate(its):
                    sl = k % 4
                    bk = 1 + k % 2
                    self.mm(self.ps[:, bk, :], self.msk_f[:, 1 if d == 0 else 3, :], R4[:, sl, :], True, True,
                            [('s_R', sl, hq) for hq in range(4)] + ['msk_f'], [('ps', bk)])
                    self.act(L4[:, sl, :], self.ps[:, bk, :], AF.Exp, [('ps', bk)], [('s_L', sl)])
                    if k + 2 < len(its):
                        emit_R(k + 2)
                    self.tt('dve', M[:, 2 * d + g, 512 * hf:512 * hf + 512].rearrange("p (h i) -> p h i", h=4),
                            L4[:, sl, :].rearrange("p (h i) -> p h i", h=4),
                            cb2[:, s, d, g, :].unsqueeze(1).to_broadcast([128, 4, 128]), ALU.mult,
                            [('s_L', sl), ('s_cb', s, d)], [('s_M', 2 * d + g, hf)])
                for g in range(2):
                    for hg in range(8):
                        hh = 8 * g + hg
                        for d in range(2):
                            self.mm(self.ps[:, 3 + g, 64 * hg:64 * hg + 64], M[:, 2 * d + g, 128 * hg:128 * hg + 128],
                                    xw[:, s, d, 64 * hh:64 * hh + 64], d == 0, d == 1,
                                    [('s_M', 2 * d + g, hg // 4), ('s_xw', s, d)], [('ps', 3 + g)])
                for d in range(2):
                    for g in range(2):
                        bk = 5 + (2 * d + g) % 2
                        self.mm(self.ps[:, bk, :], CT[:, g, tk], hb3[:, d, c % 3, 512 * g:512 * g + 512], True, True,
                                ['s_ct', ('s_hb3', d, c % 3)], [('ps', bk)])
                        dst = (ya2 if d == 0 else yt2)[:, s, 512 * g:512 * g + 512]
                        self.tt('dve', dst.rearrange("p (h e) -> p h e", h=8), self.ps[:, bk, :].rearrange("p (h e) -> p h e", h=8),
                                E[:, c, 32 * d + 8 * g:32 * d + 8 * g + 8].unsqueeze(2).to_broadcast([128, 8, 64]), ALU.mult,
                                [('ps', bk), ('s_E', c)], [('s_ya', s, g) if d == 0 else ('s_yt', s, g)])
                for g in range(2):
                    hs = slice(512 * g, 512 * g + 512)
                    self.tt('dve', ya2[:, s, hs], ya2[:, s, hs], self.ps[:, 3 + g, :], ALU.add,
                            [('s_ya', s, g), ('ps', 3 + g)], [('s_ya', s, g)])

            def tail(c):
                s = c % 2
                tk = slice(128 * c, 128 * c + 128)
                ya_ = ya2[:, s, :]
                yk = [('s_ya', s, 0), ('s_ya', s, 1)]
                self.tt('pool', ya_, ya_, yt2[:, s, :], ALU.add, yk + [('s_yt', s, 0), ('s_yt', s, 1)], yk)
                self.tt('pool', b16(ytmp), b16(xt[:, s, :]), bc16(dsk), ALU.mult, [('s_xt', s), 's_dk'], ['s_y3'])
                self.tt('pool', ya_, ya_, ytmp, ALU.add, yk + ['s_y3'], yk)
                self.tt('pool', ya_, ya_, zt3[:, c % 3, :], ALU.mult, yk + [('s_z3', c % 3)], yk)
                self.act(ytmp, ya_, AF.Square, yk, ['s_y3'])
                self.t.op('dve', lambda: nc.vector.reduce_sum(out=ss[:, 0:1], in_=ytmp, axis=mybir.AxisListType.X), ['s_y3'], ['s_ss'])
                self.act(ss[:, 1:2], ss[:, 0:1], AF.Ln, ['s_ss', 'epsc'], ['s_ss'], bias=self.epsc, scale=1.0 / 1024)
                self.act(ss[:, 1:2], ss[:, 1:2], AF.Exp, ['s_ss'], ['s_ss'], scale=-0.5)
                self.stt('dve', yn, ya_, ss[:, 1:2], gn, ALU.mult, ALU.mult, yk + ['s_ss', 's_gn'], ['s_yn'])
                pt = psb(0)
                for f in range(8):
                    self.tr(pt[:, 128 * f:128 * f + 128], yn[:, 128 * f:128 * f + 128], self.ident, ['s_yn', 'ident'], [('ps', 0)])
                self.cp('act', yst[:, s].rearrange("p f t -> p (f t)"), pt, [('ps', 0)], [('s_ys', s)])
                self.dma(self.ysT[:, :, tk], yst[:, s], [('s_ys', s)], [('ysT', c)])

            chunks = [c for c in range(NB) if not (c < 2 and not do_ctx)]
            loads(chunks[0])
            loads(chunks[1])
            front(chunks[0])
            for i, c in enumerate(chunks):
                if i + 2 < len(chunks):
                    loads(chunks[i + 2])
                if i + 1 < len(chunks):
                    front(chunks[i + 1])
                tail(c)
            self.t.barrier()

    def merge_phase(self, l, xsrc):
        nc = self.nc
        do_ctx = l < DEPTH - 1
        with ExitStack() as stk:
            al = lambda name, shape, dt: stk.enter_context(nc.sbuf_tensor(f"{name}{l}", shape, dt)).ap()
            woa = al("g_woa", [128, 4, 1024], BF16)
            woc = al("g_woc", [128, 4, 1024], BF16)
            wob = al("g_wob", [128, 8, 1024], BF16)
            wout = al("g_wout", [128, 8, 1024], BF16)
            stg = al("g_stg", [128, 4, 1024], F32)
            ya = al("g_ya", [128, 2, 4, 256], BF16)
            yc = al("g_yc", [128, 2, 4, 256], BF16)
            ys = al("g_ys", [128, 2, 8, 256], BF16)
            gt = al("g_gt", [128, 2, 24, 256], BF16)
            xg = al("g_xg", [128, 2, 8, 256], F32)
            mT = al("g_mT", [128, 2, 8, 256], BF16)
            ta = al("g_ta", [128, 2, 256], F32)
            tb = al("g_tb", [128, 2, 256], F32)
            tc_ = al("g_tc", [128, 2, 256], F32)
            k = 0
            for (wsrc, wdst, np_, nk) in ((self.woa[l], woa, 128, 4), (self.woc[l], woc, 128, 4), (self.wob[l], wob, 128, 8),
                                          (self.wout[l], wout, 128, 8)):
                for kc in range(nk):
                    sl = k % 4
                    eng_ = ('dve', 'act', 'dve', 'pool')[k % 4]
                    k += 1
                    self.dma(stg[0:np_, sl, :], wsrc[:, kc, :], (), [('g_stg', sl)])
                    self.cp(eng_, wdst[:, kc, :], stg[0:np_, sl, :], [('g_stg', sl)], [('g_w', id(wdst) % 1000, kc)])
            wk = lambda wdst: [('g_w', id(wdst) % 1000, kc) for kc in range(wdst.shape[1])]
            wins = [(wi, t0, n_) for wi, (t0, n_) in enumerate(self.windows()) if not (wi == 0 and not do_ctx)]

            def mg_loads(wi, t0, n_):
                gi = 0 if wi == 0 else 1 + (wi - 1) // 2
                s = wi % 2
                tk = slice(t0, t0 + n_)
                self.dma(ya[:, s, :, 0:n_], self.yaT[:, :, tk], ['yTA'], [('g_ya', s)])
                self.dma(yc[:, s, :, 0:n_], self.ycT[:, :, tk], ['yTC'], [('g_yc', s)])
                self.dma(ys[:, s, :, 0:n_], self.ysT[:, :, tk], ['ysT'], [('g_ys', s)])
                self.dma(gt[:, s, :, 0:n_], self.gtT[:, :, tk], ['gtT'], [('g_gt', s)])
                self.dma(xg[:, s, :, 0:n_], xsrc[:, :, tk], [('xT', gi)], [('g_xg', s)])

            mg_loads(*wins[0])
            for wpos, (wi, t0, n_) in enumerate(wins):
                if wpos + 1 < len(wins):
                    mg_loads(*wins[wpos + 1])
                gi = 0 if wi == 0 else 1 + (wi - 1) // 2
                s = wi % 2
                cls = 1 if t0 < LC else 0
                tk = slice(t0, t0 + n_)
                for j in range(8):
                    js = j % 2
                    cs = slice(128 * j, 128 * j + 128)
                    bA, bB, bC = 3 * js, 3 * js + 1, 3 * js + 2
                    for h in range(4):
                        self.mm(self.ps[:, bA, 0:n_], woa[:, h, cs], ya[:, s, h, 0:n_], h == 0, h == 3, wk(woa) + [('g_ya', s)], [('ps', bA)])
                    for kc in range(8):
                        self.mm(self.ps[:, bB, 0:n_], wob[:, kc, cs], ys[:, s, kc, 0:n_], kc == 0, kc == 7, wk(wob) + [('g_ys', s)], [('ps', bB)])
                    for h in range(4):
                        self.mm(self.ps[:, bC, 0:n_], woc[:, h, cs], yc[:, s, h, 0:n_], h == 0, h == 3, wk(woc) + [('g_yc', s)], [('ps', bC)])
                    self.tt('dve', ta[:, js, 0:n_], self.ps[:, bA, 0:n_], gt[:, s, j, 0:n_], ALU.mult, [('ps', bA), ('g_gt', s)], [('g_ta', js)])
                    self.tt('dve', tb[:, js, 0:n_], self.ps[:, bB, 0:n_], gt[:, s, 8 + j, 0:n_], ALU.mult, [('ps', bB), ('g_gt', s)], [('g_tb', js)])
                    self.tt('dve', tc_[:, js, 0:n_], self.ps[:, bC, 0:n_], gt[:, s, 16 + j, 0:n_], ALU.mult, [('ps', bC), ('g_gt', s)], [('g_tc', js)])
                    self.tt('pool', ta[:, js, 0:n_], ta[:, js, 0:n_], tb[:, js, 0:n_], ALU.add, [('g_ta', js), ('g_tb', js)], [('g_ta', js)])
                    self.tt('pool', mT[:, s, j, 0:n_], ta[:, js, 0:n_], tc_[:, js, 0:n_], ALU.add, [('g_ta', js), ('g_tc', js)], [('g_mT', s, j)])
                for j in range(8):
                    bk = 6 + j % 2
                    cs = slice(128 * j, 128 * j + 128)
                    for kc in range(8):
                        self.mm(self.ps[:, bk, 0:n_], wout[:, kc, cs], mT[:, s, kc, 0:n_], kc == 0, kc == 7,
                                wk(wout) + [('g_mT', s, kc)], [('ps', bk)])
                    self.stt('dve', xg[:, s, j, 0:n_], self.ps[:, bk, 0:n_], self.modT[:, 16 + j, cls:cls + 1], xg[:, s, j, 0:n_],
                             ALU.mult, ALU.add, [('ps', bk), 'modT', ('g_xg', s)], [('g_xg', s)])
                self.dma(self.xT[:, :, tk], xg[:, s, :, 0:n_], [('g_xg', s)], [('xT', gi)])
            self.t.barrier()

    def ffn_phase(self, l):
        nc = self.nc
        do_ctx = l < DEPTH - 1
        with ExitStack() as stk0:
            al0 = lambda name, shape, dt: stk0.enter_context(nc.sbuf_tensor(f"{name}{l}", shape, dt)).ap()
            wdn = al0("d_w", [128, 22, 1024], BF16)
            dstg = al0("d_stg", [128, 2, 1024], F32)
            self._ffn(l, wdn, dstg)

    def _ffn(self, l, wdn, dstg):
        nc = self.nc
        do_ctx = l < DEPTH - 1
        with ExitStack() as stk:
            al = lambda name, shape, dt: stk.enter_context(nc.sbuf_tensor(f"{name}{l}", shape, dt)).ap()
            ws = al("f_ws", [128, 4, 8, 128], F32)
            wb = al("f_wb", [128, 4, 8, 128], BF16)
            cw = al("f_cw", [128, 22, 4], F32)
            orow = al("f_or", [128, 2, T], BF16)
            acc = al("f_acc", [128, 4, 256], F32)
            sg = al("f_sg", [128, 4, 256], F32)
            self.dma(cw, self.fcw[l], (), ['f_cw'])

            def load(f):
                if f >= 22:
                    return
                for i, src in enumerate((self.wup, self.wgt)):
                    sl = (2 * f + i) % 4
                    self.dma(ws[:, sl], src[l, f], (), [('f_ws', sl)])
                    self.cp('pool', wb[:, sl], ws[:, sl], [('f_ws', sl)], [('f_wb', sl)])
            load(0)
            for f in range(22):
                load(f + 1)
                self.dma(dstg[:, f % 2, :], self.wdn[l, :, f, :], (), [('d_stg', f % 2)])
                self.cp('pool', wdn[:, f, :], dstg[:, f % 2, :], [('d_stg', f % 2)], [('d_w', f)])
                su, sg_ = (2 * f) % 4, (2 * f + 1) % 4
                osl = f % 2
                pend = None
                for wi, (t0, n_) in enumerate(self.windows()):
                    if wi == 0 and not do_ctx:
                        continue
                    s = wi % 4
                    bG, bU = 2 * s, 2 * s + 1
                    c0 = hcol(t0)
                    pG, pU = self.ps[:, bG, 0:258], self.ps[:, bU, 0:256]
                    hk = self.hkeys(c0 - 1, c0 + 257)
                    for kc in range(8):
                        self.mm(pG, wb[:, sg_, kc, :], self.hT[:, kc, c0 - 1:c0 + 257], kc == 0, kc == 7, hk + [('f_wb', sg_)], [('ps', bG)])
                    for kc in range(8):
                        self.mm(pU, wb[:, su, kc, :], self.hT[:, kc, c0:c0 + 256], kc == 0, kc == 7, hk + [('f_wb', su)], [('ps', bU)])
                    a = acc[:, s, :]
                    self.act(a, pG[:, 0:256], AF.Identity, [('ps', bG), 'f_cw'], [('f_acc', s)], scale=cw[:, f, 0:1])
                    self.stt('dve', a, pG[:, 1:257], cw[:, f, 1:2], a, ALU.mult, ALU.add, [('ps', bG), 'f_cw', ('f_acc', s)], [('f_acc', s)])
                    self.stt('dve', a, pG[:, 2:258], cw[:, f, 2:3], a, ALU.mult, ALU.add, [('ps', bG), 'f_cw', ('f_acc', s)], [('f_acc', s)])
                    if pend is not None:
                        pend()

                    def pend(a=a, s=s, t0=t0, osl=osl, f=f, pU=pU, bU=bU):
                        self.act(sg[:, s, :], a, AF.Silu, [('f_acc', s), 'f_cw'], [('f_sg', s)], bias=cw[:, f, 3:4])
                        self.tt('dve', orow[:, osl, t0:t0 + 256], sg[:, s, :], pU, ALU.mult, [('f_sg', s), ('ps', bU)], [('f_or', osl)])
                pend()
                pend = None
                self.dma(self.actT[:, f, :], orow[:, osl, :], [('f_or', osl)], [('actT', f)])
            self.t.barrier()
        with ExitStack() as stk:
            al = lambda name, shape, dt: stk.enter_context(nc.sbuf_tensor(f"{name}{l}", shape, dt)).ap()
            at = al("d_at", [128, 2, 22, 512], BF16)
            xg = al("d_xg", [128, 2, 8, 512], F32)
            wk = []
            ak = [('actT', f) for f in range(22)]
            grps = [(gi, t0, n_) for gi, (t0, n_) in enumerate(self.groups()) if not (gi == 0 and not do_ctx)]

            def dn_loads(gi, t0, n_):
                s = gi % 2
                tk = slice(t0, t0 + n_)
                self.dma(at[:, s, :, 0:n_], self.actT[:, :, tk], ak, [('d_at', s)])
                self.dma(xg[:, s, :, 0:n_], self.xT[:, :, tk], [('xT', gi)], [('d_xg', s)])

            dn_loads(*grps[0])
            for gpos, (gi, t0, n_) in enumerate(grps):
                if gpos + 1 < len(grps):
                    dn_loads(*grps[gpos + 1])
                s = gi % 2
                cls = 1 if t0 < LC else 0
                tk = slice(t0, t0 + n_)
                for j in range(8):
                    bk = j % 4
                    cs = slice(128 * j, 128 * j + 128)
                    for kc in range(22):
                        self.mm(self.ps[:, bk, 0:n_], wdn[:, kc, cs], at[:, s, kc, 0:n_], kc == 0, kc == 21, wk + [('d_at', s)], [('ps', bk)])
                    self.stt('dve', xg[:, s, j, 0:n_], self.ps[:, bk, 0:n_], self.modT[:, 40 + j, cls:cls + 1], xg[:, s, j, 0:n_],
                             ALU.mult, ALU.add, [('ps', bk), 'modT', ('d_xg', s)], [('d_xg', s)])
                self.dma(self.xT[:, :, tk], xg[:, s, :, 0:n_], [('d_xg', s)], [('xT', gi)])
            self.t.barrier()

    def final_norm(self):
        nc = self.nc
        with ExitStack() as stk:
            al = lambda name, shape, dt: stk.enter_context(nc.sbuf_tensor(name, shape, dt)).ap()
            x_sb = al("fn_x", [128, 2, 8, 512], F32)
            sq_sb = al("fn_sq", [128, 2, 8, 512], BF16)
            r_sb = al("fn_r", [128, 2, 512], F32)
            o_sb = al("fn_o", [128, 2, 8, 512], F32)
            g_sb = al("fn_g", [128, 8], F32)
            self.dma(g_sb, self.fng, (), ['fn_g'])
            fgr = [(gi, t0, n_) for gi, (t0, n_) in enumerate(self.groups()) if gi > 0]

            def fn_load(gi, t0, n_):
                self.dma(x_sb[:, gi % 2, :, 0:n_], self.xT[:, :, t0:t0 + n_], [('xT', gi)], [('fn_x', gi % 2)])

            fn_load(*fgr[0])
            for gpos, (gi, t0, n_) in enumerate(fgr):
                if gpos + 1 < len(fgr):
                    fn_load(*fgr[gpos + 1])
                s = gi % 2
                xg = x_sb[:, s, :, 0:n_]
                self.act(sq_sb[:, s, :, 0:n_], xg, AF.Square, [('fn_x', s)], [('fn_sq', s)])
                pt = self.ps[:, s, 0:n_]
                for kc in range(8):
                    self.mm(pt, self.ones_ms, sq_sb[:, s, kc, 0:n_], kc == 0, kc == 7, [('fn_sq', s), 'ones_ms'], [('ps', s)])
                rr = r_sb[:, s, 0:n_]
                self.act(rr, pt, AF.Ln, [('ps', s), 'epsc'], [('fn_r', s)], bias=self.epsc)
                self.act(rr, rr, AF.Exp, [('fn_r', s)], [('fn_r', s)], scale=-0.5)
                for kc in range(8):
                    self.stt('dve', o_sb[:, s, kc, 0:n_], xg[:, kc, :], g_sb[:, kc:kc + 1], rr, ALU.mult, ALU.mult,
                             [('fn_x', s), 'fn_g', ('fn_r', s)], [('fn_o', s, kc)])
                self.dma(self.outT[:, :, t0 - LC:t0 - LC + n_], o_sb[:, s, :, 0:n_], [('fn_o', s, kc) for kc in range(8)], [('outT', gi)])
            self.t.barrier()
        return []


def _fm(w):
    K, C = w.shape
    return np.ascontiguousarray(w.reshape(K // 128, 128, C // 128, 128).transpose(2, 1, 0, 3))


def _rowsp(v, kc):
    return np.ascontiguousarray(v.reshape(kc, 128).T)


def _rope_table(hf):
    rows = L // 64
    t_row = np.repeat(np.arange(rows), 64).astype(np.float32)
    t_col = np.tile(np.arange(64), rows).astype(np.float32)
    n = 16
    inv = (10000.0 ** (-np.arange(n, dtype=np.float32) / n)).astype(np.float32)
    ang = np.concatenate([t_row[:, None] * inv, t_col[:, None] * inv], axis=-1)
    ang = ang[hf * LH:(hf + 1) * LH]
    cos = np.concatenate([np.ones((LC, 32), np.float32), np.cos(ang).astype(np.float32)], 0).T
    sin = np.concatenate([np.zeros((LC, 32), np.float32), np.sin(ang).astype(np.float32)], 0).T
    t1 = np.concatenate([cos, sin, cos, sin], 0)
    t2 = np.concatenate([-sin, cos, -sin, cos], 0)
    return np.ascontiguousarray(np.stack([t1, t2], 0)).astype(np.float32)


def _const_tables():
    k = np.arange(128)[:, None]
    i = np.arange(128)[None, :]
    masks = np.stack([(k <= i), (k > i), (k >= i), (k < i), np.ones((128, 128), bool), (k == i)], 0).astype(np.float32)
    return masks


def _in_cols():
    o = {}
    s = 0
    for name, n in (('a_q', 512), ('a_k', 128), ('a_v', 128), ('b_z', 1024), ('b_xbc', 1536), ('b_dt', 32),
                    ('c_q', 512), ('c_k', 128), ('c_v', 128), ('gates', 3072)):
        o[name] = s
        s += n
    ev = np.arange(0, 64, 2)
    od = np.arange(1, 64, 2)
    tiles = []

    def rope_pair(base, h0, h1):
        a = np.concatenate([base + h0 * 64 + ev, base + h0 * 64 + ev, base + h1 * 64 + ev, base + h1 * 64 + ev])
        b = np.concatenate([base + h0 * 64 + od, base + h0 * 64 + od, base + h1 * 64 + od, base + h1 * 64 + od])
        tiles.append(a)
        tiles.append(b)
    for m in range(4):
        rope_pair(o['a_q'], m, m + 4)
    rope_pair(o['a_k'], 0, 1)
    for m in range(4):
        rope_pair(o['c_q'], m, m + 4)
    rope_pair(o['c_k'], 0, 1)
    for j in range(12):
        tiles.append(o['b_xbc'] + j * 128 + np.arange(128))
    for j in range(24):
        tiles.append(o['gates'] + j * 128 + np.arange(128))
    fm = np.concatenate(tiles)
    tm = np.concatenate([o['b_z'] + np.arange(1024), o['a_v'] + np.arange(128), o['c_v'] + np.arange(128),
                         o['b_dt'] + np.arange(32)])
    return fm, tm


def prep_shared(inp):
    f = lambda a: np.ascontiguousarray(a, dtype=np.float32)
    masks = _const_tables()
    fmc, tmc = _in_cols()
    ev = np.arange(0, 64, 2)
    od = np.arange(1, 64, 2)
    sh = {}
    sh["wmod"] = f(np.stack([_fm(inp["w_mod"][l]) for l in range(DEPTH)]))
    sh["bmod"] = f(np.stack([_rowsp(inp["b_mod"][l], 48) for l in range(DEPTH)]))
    sh["nrm"] = f(np.stack([np.stack([_rowsp(inp["norm1"][l], 8), _rowsp(inp["norm2"][l], 8)], 1) for l in range(DEPTH)]))
    sh["wfm"] = f(np.stack([_fm(inp["w_in"][l][:, fmc]) for l in range(DEPTH)]))
    sh["wtm"] = f(np.stack([inp["w_in"][l][:, tmc].reshape(8, 128, NTM).transpose(1, 0, 2) for l in range(DEPTH)]))
    cg = []
    for l in range(DEPTH):
        q, k = inp["c_q_norm"][l], inp["c_k_norm"][l]
        cg.append(np.stack([np.tile(q[ev], 4), np.tile(q[od], 4), np.tile(k[ev], 4), np.tile(k[od], 4)], 1))
    sh["cgain"] = f(np.stack(cg))
    sh["cw"] = f(np.stack([np.concatenate([inp["ssm_conv_w"][l], inp["ssm_conv_b"][l][None]], 0).reshape(4, 12, 128).transpose(2, 1, 0)
                           for l in range(DEPTH)]))
    sh["dtb"] = f(np.stack([np.broadcast_to(inp["ssm_dt_bias"][l].reshape(1, 32), (128, 32)) for l in range(DEPTH)]))
    sh["alog"] = f(np.stack([np.broadcast_to(inp["ssm_A_log"][l].reshape(1, 32), (128, 32)) for l in range(DEPTH)]))
    sh["dsk"] = f(np.stack([np.broadcast_to(inp["ssm_D"][l].reshape(1, 16), (128, 16)) for l in range(DEPTH)]))
    sh["sng"] = f(np.stack([np.broadcast_to(inp["ssm_norm"][l].reshape(1, 1024), (128, 1024)) for l in range(DEPTH)]))
    sh["sink"] = f(np.stack([np.broadcast_to(inp["a_sink"][l].reshape(1, 8), (128, 8)) for l in range(DEPTH)]))
    sh["woa"] = f(np.stack([inp["w_oa"][l].reshape(4, 128, 1024).transpose(1, 0, 2) for l in range(DEPTH)]))
    sh["woc"] = f(np.stack([inp["w_oc"][l].reshape(4, 128, 1024).transpose(1, 0, 2) for l in range(DEPTH)]))
    sh["wob"] = f(np.stack([inp["w_ob"][l].reshape(8, 128, 1024).transpose(1, 0, 2) for l in range(DEPTH)]))
    sh["wout"] = f(np.stack([inp["w_out"][l].reshape(8, 128, 1024).transpose(1, 0, 2) for l in range(DEPTH)]))
    sh["wup"] = f(np.stack([_fm(inp["ffn_w_up"][l]) for l in range(DEPTH)]))
    sh["wgt"] = f(np.stack([_fm(inp["ffn_w_gate"][l]) for l in range(DEPTH)]))
    sh["fcw"] = f(np.stack([np.concatenate([inp["ffn_conv_w"][l], inp["ffn_conv_b"][l][None]], 0).reshape(4, 22, 128).transpose(2, 1, 0)
                            for l in range(DEPTH)]))
    sh["wdn"] = f(np.stack([inp["ffn_w_down"][l].reshape(22, 128, 1024).transpose(1, 0, 2) for l in range(DEPTH)]))
    sh["fng"] = f(_rowsp(inp["final_norm"], 8))
    sh["masks"] = masks
    sh["ident"] = np.eye(128, dtype=np.float32)
    return sh


def prep_core(inp, c):
    b, hf = c // 2, c % 2
    xl = inp["x"][b]
    xa = np.concatenate([inp["ctx"][b], xl[hf * LH:(hf + 1) * LH]], 0)
    xT0 = np.ascontiguousarray(xa.T.reshape(8, 128, T).transpose(1, 0, 2), dtype=np.float32)
    cv = np.stack([_rowsp(inp["c"][b], 8), _rowsp(inp["c_ctx"], 8)], -1)
    zero = np.zeros((D,), np.float32)
    left = xl[hf * LH - 1] if hf == 1 else zero
    right = xl[(hf + 1) * LH] if hf == 0 else zero
    xh0 = np.stack([_rowsp(left, 8), _rowsp(right, 8)], -1)
    hmask = np.broadcast_to(np.array([[float(hf), float(1 - hf)]], np.float32), (128, 2))
    return {"xT0": xT0, "cvec": np.ascontiguousarray(cv, dtype=np.float32), "xh0": np.ascontiguousarray(xh0, dtype=np.float32),
            "hmask": np.ascontiguousarray(hmask, dtype=np.float32), "rope": _rope_table(hf)}


_PROG = None


def kernel(**inp):
    global _PROG
    inp = {k: np.asarray(v) for k, v in inp.items()}
    if _PROG is None:
        _PROG = Prog()
    p = _PROG
    sh = prep_shared(inp)
    in_maps = []
    for c in range(8):
        m = dict(sh)
        m.update(prep_core(inp, c))
        in_maps.append(m)
    res = run_bass_kernel_spmd(p.nc, in_maps, core_ids=list(range(8)))
    out = np.empty((4, L, D), np.float32)
    for c in range(8):
        oT = res.results[c]["outT"]
        out[c // 2, (c % 2) * LH:(c % 2 + 1) * LH] = oT.transpose(2, 1, 0).reshape(LH, D)
    return out
```

```python
from contextlib import ExitStack
import os
import numpy as np
import concourse.bass as bass
import concourse.mybir as mybir
from concourse.bass_utils import run_bass_kernel_spmd

F32 = mybir.dt.float32
BF16 = mybir.dt.bfloat16
ALU = mybir.AluOpType
AF = mybir.ActivationFunctionType

D = 1024
L = 4096
LH = 2048
LC = 256
T = LH + LC
NB = T // 128
PAIRS = [[0, 1], [2, 3], [4, 5], [6, 7]]
NOCC = False
DEPTH = 2
DFF = 2816
EPS = 1e-6
HTC = T + 6
HL = 259
HR = 260 + LH
NFM = 56
NTM = 1312


def hcol(i):
    return i + 2 if i < LC else i + 4


class Trk:
    ROT = 30000
    NDMA = 40

    def __init__(self, nc):
        self.nc = nc
        self.eng = {'pe': nc.tensor, 'act': nc.scalar, 'dve': nc.vector, 'pool': nc.gpsimd, 'sp': nc.sync}
        self.semh = []
        self.cur = {}
        self.cnt = {}
        for e in ('pe', 'act', 'dve', 'pool'):
            self.cur[e] = self._newsem(f"s_{e}")
            self.cnt[e] = 0
        self.dsem = [self._newsem(f"s_dma{i}") for i in range(self.NDMA)]
        self.dval = [0] * self.NDMA
        self.drr = 0
        self.known = {e: {} for e in self.eng}
        self.res = {}
        self.ninst = 0
        self.pesems = {self.cur['pe']}
        self.ccsem = None
        self.ccv = 0

    def _newsem(self, name):
        h = self.nc.alloc_semaphore(f"{name}_{len(self.semh)}")
        self.semh.append(h)
        return len(self.semh) - 1

    def _wait(self, e, tok):
        sid, val = tok
        if self.known[e].get(sid, 0) >= val:
            return
        self.eng[e].wait_ge(self.semh[sid], val)
        self.known[e][sid] = val

    def _deps(self, reads, writes):
        deps = {}

        def add(tok):
            if tok is None:
                return
            if deps.get(tok[0], 0) < tok[1]:
                deps[tok[0]] = tok[1]
        for r in reads:
            st = self.res.get(r)
            if st is not None:
                add(st[0])
        for w in writes:
            st = self.res.get(w)
            if st is not None:
                add(st[0])
                for s, v in st[1].items():
                    add((s, v))
        return deps

    def _commit(self, tok, reads, writes):
        for r in reads:
            st = self.res.setdefault(r, [None, {}])
            if st[1].get(tok[0], 0) < tok[1]:
                st[1][tok[0]] = tok[1]
        for w in writes:
            self.res[w] = [tok, {}]

    def op(self, e, fn, reads=(), writes=()):
        deps = self._deps(reads, writes)
        for s, v in deps.items():
            if e == 'pe' and s in self.pesems:
                continue
            self._wait(e, (s, v))
        inst = fn()
        if self.cnt[e] >= self.ROT:
            self.cur[e] = self._newsem(f"s_{e}")
            self.cnt[e] = 0
            if e == 'pe':
                self.pesems.add(self.cur[e])
        self.cnt[e] += 1
        tok = (self.cur[e], self.cnt[e])
        inst.then_inc(self.semh[tok[0]], 1)
        self._commit(tok, reads, writes)
        self.ninst += 1
        return tok

    def dma(self, out, in_, reads=(), writes=(), q='sp', slow=False):
        deps = self._deps(reads, writes)
        i = self.drr
        self.drr = (self.drr + 1) % self.NDMA
        if self.dval[i] > 0:
            deps[self.dsem[i]] = max(deps.get(self.dsem[i], 0), self.dval[i])
        for s, v in deps.items():
            self._wait(q, (s, v))
        if slow:
            inst = self.eng[q].dma_start(out=out, in_=in_, allow_slow_non_contiguous=True)
        else:
            inst = self.eng[q].dma_start(out=out, in_=in_)
        self.dval[i] += 16
        tok = (self.dsem[i], self.dval[i])
        inst.then_inc(self.semh[tok[0]], 16)
        self._commit(tok, reads, writes)
        self.ninst += 1
        return tok

    def cc(self, src, dst, reads=(), writes=()):
        if NOCC:
            return None
        deps = self._deps(reads, writes)
        for s_, v in deps.items():
            self._wait('pool', (s_, v))
        if self.ccsem is None:
            self.ccsem = self._newsem("s_cc")
            self.ccv = 0
        inst = self.nc.gpsimd.collective_compute("AllGather", ALU.bypass, replica_groups=PAIRS, ins=[src], outs=[dst])
        self.ccv += 1
        tok = (self.ccsem, self.ccv)
        inst.then_inc(self.semh[tok[0]])
        self._commit(tok, reads, writes)
        self.ninst += 1
        return tok

    def barrier(self):
        toks = {}
        for e in ('pe', 'act', 'dve', 'pool'):
            if self.cnt[e] > 0:
                toks[self.cur[e]] = self.cnt[e]
        for i in range(self.NDMA):
            if self.dval[i] > 0:
                toks[self.dsem[i]] = self.dval[i]
        if self.ccsem is not None and self.ccv > 0:
            toks[self.ccsem] = self.ccv
        for e in self.eng:
            for s, v in toks.items():
                self._wait(e, (s, v))
        self.res = {}

    def finish(self, toks):
        for t in toks:
            self._wait('sp', t)


class Prog:
    def __init__(self, dbg=(), nlayers=DEPTH, stop_after=None):
        self.dbg = set(dbg)
        self.nlayers = nlayers
        self.stop_after = stop_after
        nc = self.nc = bass.Bass("TRN2", target_bir_lowering=False)
        self.t = Trk(nc)
        self.outs = []
        self._uid = 0
        self.build()

    def din(self, name, shape, dt=F32):
        return self.nc.dram_tensor(name, list(shape), dt, kind="ExternalInput").ap()

    def dscr(self, name, shape, dt):
        if name in self.dbg:
            self.outs.append(name)
            return self.nc.dram_tensor(name, list(shape), dt, kind="ExternalOutput").ap()
        return self.nc.dram_tensor(name, list(shape), dt).ap()

    def sb(self, name, shape, dt):
        return self.nc.alloc_sbuf_tensor("sb_" + name, list(shape), dt).ap()

    def dump(self, name, ap, keys, dt=F32):
        if name in self.dbg:
            d = self.dscr(name, list(ap.shape), dt)
            self.dma(d, ap, keys, [('dump', name)])

    def uid(self):
        self._uid += 1
        return self._uid

    def mm(self, out, lhsT, rhs, start, stop, r, w):
        return self.t.op('pe', lambda: self.nc.tensor.matmul(out, lhsT=lhsT, rhs=rhs, start=start, stop=stop), r, w)

    def tr(self, out, in_, ident, r, w):
        return self.t.op('pe', lambda: self.nc.tensor.transpose(out, in_, ident), r, w)

    def act(self, out, in_, func, r, w, bias=None, scale=1.0, accum_out=None):
        kw = {}
        if bias is not None:
            kw['bias'] = bias
        if accum_out is not None:
            kw['accum_out'] = accum_out
        return self.t.op('act', lambda: self.nc.scalar.activation(out=out, in_=in_, func=func, scale=scale, **kw), r, w)

    def tt(self, e, out, in0, in1, op, r, w):
        eng = self.t.eng[e]
        return self.t.op(e, lambda: eng.tensor_tensor(out=out, in0=in0, in1=in1, op=op), r, w)

    def ts(self, e, out, in0, s1, op0, r, w, s2=None, op1=None):
        eng = self.t.eng[e]
        if op1 is None:
            return self.t.op(e, lambda: eng.tensor_scalar(out=out, in0=in0, scalar1=s1, scalar2=None, op0=op0), r, w)
        return self.t.op(e, lambda: eng.tensor_scalar(out=out, in0=in0, scalar1=s1, scalar2=s2, op0=op0, op1=op1), r, w)

    def stt(self, e, out, in0, scalar, in1, op0, op1, r, w):
        eng = self.t.eng[e]
        return self.t.op(e, lambda: eng.scalar_tensor_tensor(out=out, in0=in0, scalar=scalar, in1=in1, op0=op0, op1=op1), r, w)

    def cp(self, e, out, in_, r, w):
        eng = self.t.eng[e]
        if e == 'act':
            return self.t.op(e, lambda: eng.copy(out=out, in_=in_), r, w)
        return self.t.op(e, lambda: eng.tensor_copy(out=out, in_=in_), r, w)

    def recip(self, out, in_, r, w):
        return self.t.op('dve', lambda: self.nc.vector.reciprocal(out=out, in_=in_), r, w)

    def memset(self, e, ap, v, w):
        eng = self.t.eng[e]
        return self.t.op(e, lambda: eng.memset(ap, v), (), w)

    def dma(self, out, in_, r, w, slow=False, q='sp'):
        return self.t.dma(out, in_, r, w, q=q, slow=slow)

    def build(self):
        nc = self.nc
        self.xT0 = self.din("xT0", [128, 8, T])
        self.cvec = self.din("cvec", [128, 8, 2])
        self.xh0 = self.din("xh0", [128, 8, 2])
        self.hmask = self.din("hmask", [128, 2])
        self.wmod = self.din("wmod", [DEPTH, 48, 128, 8, 128])
        self.bmod = self.din("bmod", [DEPTH, 128, 48])
        self.nrm = self.din("nrm", [DEPTH, 128, 2, 8])
        self.wfm = self.din("wfm", [DEPTH, NFM, 128, 8, 128])
        self.wtm = self.din("wtm", [DEPTH, 128, 8, NTM])
        self.rope = self.din("rope", [2, 128, T])
        self.cgain = self.din("cgain", [DEPTH, 128, 4])
        self.cw = self.din("cw", [DEPTH, 128, 12, 4])
        self.dtb = self.din("dtb", [DEPTH, 128, 32])
        self.alog = self.din("alog", [DEPTH, 128, 32])
        self.dsk = self.din("dsk", [DEPTH, 128, 16])
        self.sng = self.din("sng", [DEPTH, 128, 1024])
        self.sink = self.din("sink", [DEPTH, 128, 8])
        self.woa = self.din("woa", [DEPTH, 128, 4, 1024])
        self.woc = self.din("woc", [DEPTH, 128, 4, 1024])
        self.wob = self.din("wob", [DEPTH, 128, 8, 1024])
        self.wout = self.din("wout", [DEPTH, 128, 8, 1024])
        self.wup = self.din("wup", [DEPTH, 22, 128, 8, 128])
        self.wgt = self.din("wgt", [DEPTH, 22, 128, 8, 128])
        self.fcw = self.din("fcw", [DEPTH, 128, 22, 4])
        self.wdn = self.din("wdn", [DEPTH, 128, 22, 1024])
        self.fng = self.din("fng", [128, 8])
        self.masks = self.din("masks", [6, 128, 128])
        self.ident_in = self.din("ident", [128, 128])
        self.outT = nc.dram_tensor("outT", [128, 8, LH], F32, kind="ExternalOutput").ap()

        self.xT = self.dscr("xT", [128, 8, T], F32)
        self.qaT = self.dscr("qaT", [128, 4, T], BF16)
        self.kaT = self.dscr("kaT", [128, T], BF16)
        self.qcT = self.dscr("qcT", [128, 4, T], BF16)
        self.kcT = self.dscr("kcT", [128, T], BF16)
        self.xbcT = self.dscr("xbcT", [128, 12, T], BF16)
        self.gtT = self.dscr("gtT", [128, 24, T], BF16)
        self.zs = self.dscr("zs", [T, 1024], F32)
        self.vv = self.dscr("vv", [T, 256], BF16)
        self.dts = self.dscr("dts", [T, 32], F32)
        self.yaT = self.dscr("yaT", [128, 4, T], BF16)
        self.ycT = self.dscr("ycT", [128, 4, T], BF16)
        self.ysT = self.dscr("ysT", [128, 8, T], BF16)
        self.hst = self.dscr("hst", [2, NB, 128, 1024], BF16)
        self.actT = self.dscr("actT", [128, 22, T], BF16)
        self.xhal = self.dscr("xhal", [128, 8, 2], F32)
        self.xb_src = self.dscr("xb_src", [128, 16], F32)
        self.xb_dst = self.dscr("xb_dst", [256, 16], F32)
        self.kaL = self.dscr("kaL", [128, LH], BF16)
        self.kcL = self.dscr("kcL", [128, LH], BF16)
        self.kaG = self.dscr("kaG", [256, LH], BF16)
        self.kcG = self.dscr("kcG", [256, LH], BF16)
        self.vvL = self.dscr("vvL", [LH, 256], BF16)
        self.vG = self.dscr("vG", [2 * LH, 256], BF16)
        self.s_src = self.dscr("s_src", [128, 2048], F32)
        self.s_dst = self.dscr("s_dst", [256, 2048], F32)

        self.ps = nc.alloc_psum_tensor("ps", [128, 8, 512], F32).ap()
        self.ident_f = self.sb("ident_f", [128, 128], F32)
        self.ident = self.sb("ident", [128, 128], BF16)
        self.ones_ms = self.sb("ones_ms", [128, 128], BF16)
        self.bd_ms = self.sb("bd_ms", [128, 128], BF16)
        self.ones_f = self.sb("ones_f", [128, 128], F32)
        self.msk_f = self.sb("msk_f", [128, 6, 128], F32)
        self.msk_b = self.sb("msk_b", [128, 6, 128], BF16)
        self.epsc = self.sb("epsc", [128, 1], F32)
        self.onec = self.sb("onec", [128, 1], F32)
        self.modT = self.sb("modT", [128, 48, 2], F32)
        self.gm = self.sb("gm", [128, 2, 8, 2], F32)
        self.hm = self.sb("hm", [128, 2], F32)

        self.consts()
        toks = []
        for l in range(self.nlayers):
            self.layer(l)
            if self.stop_after is not None and l == self.stop_after[0]:
                break
        if self.stop_after is None:
            toks = self.final_norm()
        self.t.barrier()

    def consts(self):
        self.dma(self.ident_f, self.ident_in, (), ['ident_f'])
        self.cp('dve', self.ident, self.ident_f, ['ident_f'], ['ident'])
        self.memset('dve', self.ones_ms, 1.0 / 1024, ['ones_ms'])
        self.memset('dve', self.bd_ms, 0.0, ['bd_ms'])
        self.memset('dve', self.bd_ms[0:64, 0:64], 1.0 / 128, ['bd_ms'])
        self.memset('dve', self.bd_ms[64:128, 64:128], 1.0 / 128, ['bd_ms'])
        self.memset('dve', self.ones_f, 1.0, ['ones_f'])
        self.memset('dve', self.epsc, EPS, ['epsc'])
        self.memset('dve', self.onec, 1.0, ['onec'])
        self.dma(self.msk_f, self.masks.rearrange("m p c -> p m c"), (), ['msk_f'])
        self.cp('dve', self.msk_b, self.msk_f, ['msk_f'], ['msk_b'])
        self.dma(self.hm, self.hmask, (), ['hm'])

    def layer(self, l):
        xsrc = self.xT0 if l == 0 else self.xT
        self.mod_phase(l)
        if self.stop_after == (l, 'mod'):
            return
        with self.nc.sbuf_tensor(f"hT{l}a", [128, 8, HTC], BF16) as hT_h:
            self.hT = hT_h.ap()
            self.memset('dve', self.hT, 0.0, [('hT', g_) for g_ in range(5)])
            self.norm_phase(l, 0, xsrc, self.xh0 if l == 0 else self.xhal)
            if self.stop_after == (l, 'n1'):
                return
            self.inproj_phase(l)
        if self.stop_after == (l, 'ip'):
            return
        self.attn_phase(l, 'A')
        self.attn_phase(l, 'C')
        if self.stop_after == (l, 'at'):
            return
        self.ssm_phase(l)
        if self.stop_after == (l, 'ss'):
            return
        self.merge_phase(l, xsrc)
        self.halo_exchange()
        if self.stop_after == (l, 'mg'):
            return
        with self.nc.sbuf_tensor(f"hT{l}b", [128, 8, HTC], BF16) as hT_h:
            self.hT = hT_h.ap()
            self.memset('dve', self.hT, 0.0, [('hT', g_) for g_ in range(5)])
            self.norm_phase(l, 1, self.xT, self.xhal)
            self.ffn_phase(l)
        if l < DEPTH - 1:
            self.halo_exchange()
        if self.stop_after == (l, 'ff'):
            return

    def mod_phase(self, l):
        nc = self.nc
        t = self.t
        with nc.sbuf_tensor(f"m_c{l}", [128, 8, 2], F32) as c_h, \
                nc.sbuf_tensor(f"m_sc{l}", [128, 8, 2], F32) as sc_h, \
                nc.sbuf_tensor(f"m_w{l}", [128, 4, 8, 128], F32) as w_h, \
                nc.sbuf_tensor(f"m_wb{l}", [128, 2, 8, 128], BF16) as wb_h, \
                nc.sbuf_tensor(f"m_scb{l}", [128, 8, 2], BF16) as scb_h, \
                nc.sbuf_tensor(f"m_b{l}", [128, 48], F32) as b_h, \
                nc.sbuf_tensor(f"m_n{l}", [128, 2, 8], F32) as n_h:
            c_sb, sc_sb, w_sb, b_sb, n_sb = c_h.ap(), sc_h.ap(), w_h.ap(), b_h.ap(), n_h.ap()
            self.dma(c_sb, self.cvec, (), ['m_c'])
            self.dma(b_sb, self.bmod[l], (), ['m_b'])
            self.dma(n_sb, self.nrm[l], (), ['m_n'])
            self.act(sc_sb, c_sb, AF.Silu, ['m_c'], ['m_sc'])
            wb_sb, scb = wb_h.ap(), scb_h.ap()
            self.cp('dve', scb, sc_sb, ['m_sc'], ['m_scb'])
            for j in range(3):
                self.dma(w_sb[:, j % 4], self.wmod[l, j], (), [('m_w', j % 4)])
            for j in range(48):
                s = j % 4
                if j + 3 < 48:
                    self.dma(w_sb[:, (j + 3) % 4], self.wmod[l, j + 3], (), [('m_w', (j + 3) % 4)])
                self.cp('dve', wb_sb[:, j % 2], w_sb[:, s], [('m_w', s)], [('m_wb', j % 2)])
                pt = self.ps[:, j % 4, 0:2]
                for kc in range(8):
                    self.mm(pt, wb_sb[:, j % 2, kc, :], scb[:, kc, :], kc == 0, kc == 7,
                            [('m_wb', j % 2), 'm_scb'], [('ps', j % 4)])
                self.ts('dve', self.modT[:, j, :], pt, b_sb[:, j:j + 1], ALU.add, [('ps', j % 4), 'm_b'], [('modT', j)])
            for n in range(2):
                sc_off = 8 + 24 * n
                for kc in range(8):
                    self.ts('dve', self.gm[:, n, kc, :], self.modT[:, sc_off + kc, :], 1.0, ALU.add, [('modT', sc_off + kc)], ['gm'])
                    self.ts('dve', self.gm[:, n, kc, :], self.gm[:, n, kc, :], n_sb[:, n, kc:kc + 1], ALU.mult,
                            ['gm', 'm_n'], ['gm'])
            if 'modT' in self.dbg:
                d = self.dscr("modT", [128, 48, 2], F32)
                self.dma(d, self.modT, ['modT'], ['d_modT'])
            t.barrier()

    @staticmethod
    def groups():
        g = [(0, LC)]
        for i in range(LH // 512):
            g.append((LC + 512 * i, 512))
        return g

    @staticmethod
    def windows():
        return [(256 * w, 256) for w in range(T // 256)]

    def norm_phase(self, l, n, xsrc, xhsrc):
        nc = self.nc
        sh_off = 0 if n == 0 else 24
        with nc.sbuf_tensor(f"n_x{l}{n}", [128, 2, 8, 512], F32) as x_h, \
                nc.sbuf_tensor(f"n_sq{l}{n}", [128, 2, 8, 512], BF16) as sq_h, \
                nc.sbuf_tensor(f"n_r{l}{n}", [128, 2, 512], F32) as r_h, \
                nc.sbuf_tensor(f"n_t{l}{n}", [128, 2, 512], F32) as t_h:
            x_sb, sq_sb, r_sb, t_sb = x_h.ap(), sq_h.ap(), r_h.ap(), t_h.ap()
            glist = list(enumerate(self.groups())) + [(99, (None, 2))]
            for gi, (t0, n_) in glist:
                halo = gi == 99
                s = gi % 2
                cls = 1 if (not halo and t0 < LC) else 0
                xg = x_sb[:, s, :, 0:n_]
                if halo:
                    self.dma(xg, xhsrc, ['xhal'], [('n_x', s)])
                else:
                    self.dma(xg, xsrc[:, :, t0:t0 + n_], [('xT', gi)], [('n_x', s)])
                self.act(sq_sb[:, s, :, 0:n_], xg, AF.Square, [('n_x', s)], [('n_sq', s)])
                pt = self.ps[:, s, 0:n_]
                for kc in range(8):
                    self.mm(pt, self.ones_ms, sq_sb[:, s, kc, 0:n_], kc == 0, kc == 7,
                            [('n_sq', s), 'ones_ms'], [('ps', s)])
                rr = r_sb[:, s, 0:n_]
                self.act(rr, pt, AF.Ln, [('ps', s), 'epsc'], [('n_r', s)], bias=self.epsc)
                self.act(rr, rr, AF.Exp, [('n_r', s)], [('n_r', s)], scale=-0.5)
                c0 = hcol(t0) if not halo else None
                for kc in range(8):
                    ts_ = kc % 2
                    tmp = t_sb[:, ts_, 0:n_]
                    self.stt('dve', tmp, xg[:, kc, :], self.gm[:, n, kc, cls:cls + 1], rr, ALU.mult, ALU.mult,
                             [('n_x', s), 'gm', ('n_r', s)], [('n_t', ts_)])
                    if halo:
                        self.stt('dve', tmp, tmp, self.modT[:, sh_off + kc, 0:1], self.hm, ALU.add, ALU.mult,
                                 [('n_t', ts_), 'modT', 'hm'], [('n_t', ts_)])
                        self.cp('dve', self.hT[:, kc, HL:HL + 1], tmp[:, 0:1], [('n_t', ts_)], [('hT', 1)])
                        self.cp('dve', self.hT[:, kc, HR:HR + 1], tmp[:, 1:2], [('n_t', ts_)], [('hT', 4)])
                        continue
                    self.act(self.hT[:, kc, c0:c0 + n_], tmp, AF.Identity, [('n_t', ts_), 'modT'], [('hT', gi)],
                             bias=self.modT[:, sh_off + kc, cls:cls + 1])
            if 'hT' in self.dbg:
                d = self.dscr("hT", [128, 8, HTC], BF16)
                self.dma(d, self.hT, [('hT', g_) for g_ in range(5)], ['d_hT'])
            self.t.barrier()

    def halo_exchange(self):
        xk = [('xT', gi) for gi in range(5)]
        self.dma(self.xb_src[:, 0:8], self.xT[:, :, LC:LC + 1].rearrange("p k o -> p (k o)"), xk, ['xb_src'], slow=True)
        self.dma(self.xb_src[:, 8:16], self.xT[:, :, T - 1:T].rearrange("p k o -> p (k o)"), xk, ['xb_src'], slow=True)
        self.t.cc(self.xb_src, self.xb_dst, ['xb_src'], ['xb_dst'])
        self.dma(self.xhal[:, :, 0:1].rearrange("p k o -> p (k o)"), self.xb_dst[0:128, 8:16], ['xb_dst'], ['xhal'], slow=True, q='pool')
        self.dma(self.xhal[:, :, 1:2].rearrange("p k o -> p (k o)"), self.xb_dst[128:256, 0:8], ['xb_dst'], ['xhal'], slow=True, q='pool')

    def hkeys(self, c0, c1):
        ks = []
        for gi, (t0, n_) in enumerate(self.groups()):
            a = hcol(t0) - 2
            b = hcol(t0) + n_ + 2
            if c0 < b and c1 > a:
                ks.append(('hT', gi))
        return ks

    def inproj_phase(self, l):
        nc = self.nc
        with nc.sbuf_tensor(f"j_wf{l}", [128, 2, NTM], F32) as wf_h, nc.sbuf_tensor(f"j_wb{l}", [128, 8, NTM], BF16) as wbt_h:
            self._wtf, self._wtb = wf_h.ap(), wbt_h.ap()
            self._inproj(l)

    def _inproj(self, l):
        nc = self.nc
        with nc.sbuf_tensor(f"i_ws{l}", [128, 4, 8, 128], F32) as ws_h, \
                nc.sbuf_tensor(f"i_wb{l}", [128, 4, 8, 128], BF16) as wb_h, \
                nc.sbuf_tensor(f"i_rp{l}", [128, 2, T], F32) as rp_h, \
                nc.sbuf_tensor(f"i_or{l}", [128, 2, T], BF16) as or_h, \
                nc.sbuf_tensor(f"i_cg{l}", [128, 4], F32) as cg_h, \
                nc.sbuf_tensor(f"i_cw{l}", [128, 12, 4], F32) as cw_h, \
                nc.sbuf_tensor(f"i_t1{l}", [128, 2, 512], F32) as t1_h, \
                nc.sbuf_tensor(f"i_t2{l}", [128, 2, 512], F32) as t2_h, \
                nc.sbuf_tensor(f"i_sq{l}", [128, 2, 2, 512], BF16) as sq_h, \
                nc.sbuf_tensor(f"i_xa{l}", [128, 4, 256], F32) as xa_h, \
                nc.sbuf_tensor(f"i_rs{l}", [128, 2, 512], F32) as rs_h:
            ws, wb, rp, orow = ws_h.ap(), wb_h.ap(), rp_h.ap(), or_h.ap()
            wtf = self._wtf
            wtb = self._wtb
            cg, cw, t1, t2, sq, rs = cg_h.ap(), cw_h.ap(), t1_h.ap(), t2_h.ap(), sq_h.ap(), rs_h.ap()
            xacc = xa_h.ap()
            self.dma(rp, self.rope.rearrange("a p t -> p a t"), (), ['i_rp'])
            self.dma(cg, self.cgain[l], (), ['i_cg'])
            self.dma(cw, self.cw[l], (), ['i_cw'])
            loaded = set()

            def load(ti):
                if ti >= NFM or ti in loaded:
                    return
                loaded.add(ti)
                sl = ti % 4
                self.dma(ws[:, sl], self.wfm[l, ti], (), [('i_ws', sl)])
                self.cp('pool', wb[:, sl], ws[:, sl], [('i_ws', sl)], [('i_wb', sl)])

            load(0); load(1)
            osl = 0
            dests = [self.qaT[:, m, :] for m in range(4)] + [self.kaT] + [self.qcT[:, m, :] for m in range(4)] + [self.kcT]
            for pr in range(10):
                tA, tB = 2 * pr, 2 * pr + 1
                load(tA + 2); load(tB + 2)
                isC = pr >= 5
                gq = 0 if pr < 9 else 2
                for gi, (t0, n_) in enumerate(self.groups()):
                    s = gi % 2
                    bA, bB, bM = 2 * s, 2 * s + 1, 4 + s
                    c0 = hcol(t0)
                    hk = [('hT', gi)]
                    pA, pB = self.ps[:, bA, 0:n_], self.ps[:, bB, 0:n_]
                    for kc in range(8):
                        self.mm(pA, wb[:, tA % 4, kc, :], self.hT[:, kc, c0:c0 + n_], kc == 0, kc == 7,
                                hk + [('i_wb', tA % 4)], [('ps', bA)])
                    for kc in range(8):
                        self.mm(pB, wb[:, tB % 4, kc, :], self.hT[:, kc, c0:c0 + n_], kc == 0, kc == 7,
                                hk + [('i_wb', tB % 4)], [('ps', bB)])
                    T1, T2 = rp[:, 0, t0:t0 + n_], rp[:, 1, t0:t0 + n_]
                    a1, a2 = t1[:, s, 0:n_], t2[:, s, 0:n_]
                    oo = orow[:, osl, t0:t0 + n_]
                    if not isC:
                        self.tt('dve', a1, pA, T1, ALU.mult, [('ps', bA), 'i_rp'], [('i_t1', s)])
                        self.tt('dve', a2, pB, T2, ALU.mult, [('ps', bB), 'i_rp'], [('i_t2', s)])
                        self.tt('pool', oo, a1, a2, ALU.add, [('i_t1', s), ('i_t2', s)], [('i_or', osl)])
                    else:
                        self.act(sq[:, s, 0, 0:n_], pA, AF.Square, [('ps', bA)], [('i_sq', s, 0)])
                        self.act(sq[:, s, 1, 0:n_], pB, AF.Square, [('ps', bB)], [('i_sq', s, 1)])
                        self.stt('dve', a1, pA, cg[:, gq:gq + 1], T1, ALU.mult, ALU.mult,
                                 [('ps', bA), 'i_rp', 'i_cg', ('i_sq', s, 0)], [('i_t1', s)])
                        self.stt('dve', a2, pB, cg[:, gq + 1:gq + 2], T2, ALU.mult, ALU.mult,
                                 [('ps', bB), 'i_rp', 'i_cg', ('i_sq', s, 1)], [('i_t2', s)])
                        pM = self.ps[:, bM, 0:n_]
                        self.mm(pM, self.bd_ms, sq[:, s, 0, 0:n_], True, False, [('i_sq', s, 0), 'bd_ms'], [('ps', bM)])
                        self.mm(pM, self.bd_ms, sq[:, s, 1, 0:n_], False, True, [('i_sq', s, 1), 'bd_ms'], [('ps', bM)])
                        rr = rs[:, s, 0:n_]
                        self.act(rr, pM, AF.Sqrt, [('ps', bM), 'epsc'], [('i_rs', s)], bias=self.epsc)
                        self.recip(rr, rr, [('i_rs', s)], [('i_rs', s)])
                        self.tt('pool', a1, a1, a2, ALU.add, [('i_t1', s), ('i_t2', s)], [('i_t1', s)])
                        self.tt('pool', oo, a1, rr, ALU.mult, [('i_t1', s), ('i_rs', s)], [('i_or', osl)])
                self.dma(dests[pr], orow[:, osl, :], [('i_or', osl)], [('dst_rope', pr)])
                if pr == 4:
                    self.dma(self.kaL, orow[:, osl, LC:T], [('i_or', osl)], ['kaL'])
                if pr == 9:
                    self.dma(self.kcL, orow[:, osl, LC:T], [('i_or', osl)], ['kcL'])
                osl ^= 1
            for j in range(12):
                ti = 20 + j
                load(ti + 1); load(ti + 2)
                pend = None
                for wi, (t0, n_) in enumerate(self.windows()):
                    s = wi % 4
                    bk = s
                    c0 = hcol(t0) - 1
                    hk = self.hkeys(c0, c0 + 258)
                    pt = self.ps[:, bk, 0:258]
                    for kc in range(8):
                        self.mm(pt, wb[:, ti % 4, kc, :], self.hT[:, kc, c0:c0 + 258], kc == 0, kc == 7,
                                hk + [('i_wb', ti % 4)], [('ps', bk)])
                    acc = xacc[:, s, :]
                    self.act(acc, pt[:, 0:256], AF.Identity, [('ps', bk), 'i_cw'], [('i_xa', s)], scale=cw[:, j, 0:1])
                    self.stt('dve', acc, pt[:, 1:257], cw[:, j, 1:2], acc, ALU.mult, ALU.add,
                             [('ps', bk), 'i_cw', ('i_xa', s)], [('i_xa', s)])
                    self.stt('dve', acc, pt[:, 2:258], cw[:, j, 2:3], acc, ALU.mult, ALU.add,
                             [('ps', bk), 'i_cw', ('i_xa', s)], [('i_xa', s)])
                    if pend is not None:
                        pend()
                    pend = (lambda acc=acc, s=s, t0=t0, osl=osl, j=j: self.act(
                        orow[:, osl, t0:t0 + 256], acc, AF.Silu, [('i_xa', s), 'i_cw'], [('i_or', osl)], bias=cw[:, j, 3:4]))
                pend()
                pend = None
                self.dma(self.xbcT[:, j, :], orow[:, osl, :], [('i_or', osl)], [('xbcT', j)])
                osl ^= 1
                if j == 1:
                    self.t.cc(self.kaL, self.kaG, ['kaL'], ['kaG'])
                    self.t.cc(self.kcL, self.kcG, ['kcL'], ['kcG'])
            for j in range(24):
                ti = 32 + j
                load(ti + 1); load(ti + 2)
                if j < 8:
                    self.dma(wtf[:, j % 2], self.wtm[l, :, j, :], (), [('j_wf', j % 2)])
                    self.cp('pool', wtb[:, j, :], wtf[:, j % 2], [('j_wf', j % 2)], [('j_wb', j)])
                for gi, (t0, n_) in enumerate(self.groups()):
                    bk = (j * 5 + gi) % 6
                    c0 = hcol(t0)
                    pt = self.ps[:, bk, 0:n_]
                    for kc in range(8):
                        self.mm(pt, wb[:, ti % 4, kc, :], self.hT[:, kc, c0:c0 + n_], kc == 0, kc == 7,
                                [('hT', gi), ('i_wb', ti % 4)], [('ps', bk)])
                    self.act(orow[:, osl, t0:t0 + n_], pt, AF.Sigmoid, [('ps', bk)], [('i_or', osl)])
                self.dma(self.gtT[:, j, :], orow[:, osl, :], [('i_or', osl)], [('gtT', j)])
                osl ^= 1
            self.t.barrier()
        with nc.sbuf_tensor(f"j_z{l}", [128, 2, 1024], F32) as z_h, \
                nc.sbuf_tensor(f"j_v{l}", [128, 2, 256], BF16) as v_h, \
                nc.sbuf_tensor(f"j_d{l}", [128, NB, 32], F32) as d_h, \
                nc.sbuf_tensor(f"j_db{l}", [128, 32], F32) as db_h:
            wb, zst, vst, dst, dbt = self._wtb, z_h.ap(), v_h.ap(), d_h.ap(), db_h.ap()
            self.dma(dbt, self.dtb[l], (), ['j_db'])
            wk = []
            for tb, cgps in [(tb_, (2,)) for tb_ in range(NB)] + [(-1, ())] + [(tb_, (0, 1)) for tb_ in range(NB)]:
                if tb < 0:
                    self.t.cc(self.vvL, self.vG, [('vvL', t_) for t_ in range(2, NB)], ['vG'])
                    continue
                s = tb % 2
                c0 = hcol(128 * tb)
                hk = self.hkeys(c0, c0 + 128)
                for cgp in cgps:
                    bk = (tb * 3 + cgp) % 4
                    n_ = 512 if cgp < 2 else 256
                    pt = self.ps[:, bk, 0:n_]
                    for kc in range(8):
                        self.mm(pt, self.hT[:, kc, c0:c0 + 128], wb[:, kc, 512 * cgp:512 * cgp + n_], kc == 0, kc == 7,
                                hk + wk, [('ps', bk)])
                    if cgp < 2:
                        self.act(zst[:, s, 512 * cgp:512 * cgp + 512], pt, AF.Silu, [('ps', bk)], [('j_z', s, cgp)])
                    else:
                        self.cp('dve', vst[:, s, :], pt, [('ps', bk)], [('j_v', s)])
                if 0 in cgps:
                    self.dma(self.zs[128 * tb:128 * tb + 128, :], zst[:, s, :], [('j_z', s, 0), ('j_z', s, 1)], [('zs', tb)])
                else:
                    self.dma(self.vv[128 * tb:128 * tb + 128, :], vst[:, s, :], [('j_v', s)], [('vv', tb)])
                    if tb >= 2:
                        self.dma(self.vvL[128 * (tb - 2):128 * (tb - 2) + 128, :], vst[:, s, :], [('j_v', s)], [('vvL', tb)])
            for tb in range(NB):
                bk = 4 + tb % 2
                c0 = hcol(128 * tb)
                hk = self.hkeys(c0, c0 + 128)
                pt = self.ps[:, bk, 0:32]
                for kc in range(8):
                    self.mm(pt, self.hT[:, kc, c0:c0 + 128], wb[:, kc, 1280:1312], kc == 0, kc == 7, hk + wk, [('ps', bk)])
                self.tt('dve', dst[:, tb, :], pt, dbt, ALU.add, [('ps', bk), 'j_db'], [('j_d', tb)])
            dk = [('j_d', tb) for tb in range(NB)]
            self.act(dst, dst, AF.Exp, dk, dk)
            self.act(dst, dst, AF.Ln, dk + ['onec'], dk, bias=self.onec)
            self.dma(self.dts.rearrange("(b p) c -> p b c", p=128), dst, dk, ['dts'])
            self.t.barrier()

    def attn_phase(self, l, which):
        nc = self.nc
        isA = which == 'A'
        do_ctx = l < DEPTH - 1
        qsrc, ksrc, ydst = (self.qaT, self.kaT, self.yaT) if isA else (self.qcT, self.kcT, self.ycT)
        voff = 0 if isA else 128
        nm = f"{which}{l}"
        LOOK = 3
        NS, NP = 4, 6
        with ExitStack() as stk:
            al = lambda name, shape, dt: stk.enter_context(nc.sbuf_tensor(f"{name}{nm}", shape, dt)).ap()
            if isA:
                NKB = 20
            else:
                NKB = 34
            kz = al("a_k", [128, 2, 128 * NKB], BF16)
            Q = al("a_q", [128, 4, T], BF16)
            vx = al("a_v", [128, NKB, 2, 128], BF16)
            P = al("a_p", [128, NP, 512], BF16)
            dsum = al("a_d", [128, 2, 512], F32)
            rden = al("a_r", [64, 2, 512], F32)
            lnd = al("a_l", [128, 2, 512], F32)
            yst = al("a_y", [64, 2, 512], BF16)
            sk = al("a_s", [128, 8], F32)
            es = al("a_e", [128, 2, 512], F32)
            mx = al("a_m", [128, 4, 128], BF16)
            self.memset('pool', kz[64:128, 0, :], 0.0, [('a_k', 0)])
            self.memset('pool', kz[0:64, 1, :], 0.0, [('a_k', 1)])
            self.memset('pool', vx[:, :, :, 64:128], 1.0, ['a_v1'])
            vsrc = lambda t_, a, b: t_[a:b, :].rearrange("(b p) c -> p b c", p=128)
            for g_ in range(2):
                r0, r1 = 64 * g_, 64 * g_ + 64
                vc = slice(voff + 64 * g_, voff + 64 * g_ + 64)
                self.dma(kz[r0:r1, g_, 0:LC], ksrc[r0:r1, 0:LC], (), [('a_k', g_)])
                self.dma(vx[:, 0:2, g_, 0:64], vsrc(self.vv, 0, LC)[:, :, vc], (), [('a_v', g_)])
                if isA:
                    self.dma(kz[r0:r1, g_, 256:384], self.kaG[r0:r1, LH - 128:LH], (), [('a_k', g_)])
                    self.dma(kz[r0:r1, g_, 384:384 + LH], ksrc[r0:r1, LC:T], (), [('a_k', g_)])
                    self.dma(kz[r0:r1, g_, 384 + LH:512 + LH], self.kaG[128 + r0:128 + r1, 0:128], (), [('a_k', g_)])
                    self.dma(vx[:, 2:3, g_, 0:64], vsrc(self.vG, LH - 128, LH)[:, :, vc], (), [('a_v', g_)])
                    self.dma(vx[:, 3:19, g_, 0:64], vsrc(self.vvL, 0, LH)[:, :, vc], (), [('a_v', g_)])
                    self.dma(vx[:, 19:20, g_, 0:64], vsrc(self.vG, LH, LH + 128)[:, :, vc], (), [('a_v', g_)])
                else:
                    self.dma(kz[r0:r1, g_, LC:LC + 2 * LH].rearrange("p (r t) -> p r t", r=2),
                             self.kcG.rearrange("(r p) t -> p r t", r=2)[r0:r1], (), [('a_k', g_)])
                    self.dma(vx[:, 2:34, g_, 0:64], vsrc(self.vG, 0, 2 * LH)[:, :, vc], (), [('a_v', g_)])
            self.dma(Q, qsrc, (), ['a_q'])
            self.cp('dve', mx[:, 0, :], self.msk_b[:, 2, :], ['msk_b'], [('a_m', 0)])
            self.cp('dve', mx[:, 1, :], self.msk_b[:, 0, :], ['msk_b'], [('a_m', 1)])
            self.ts('dve', mx[:, 2, :], self.msk_b[:, 2, :], self.hm[:, 0:1], ALU.mult, ['msk_b', 'hm'], [('a_m', 2)])
            self.ts('dve', mx[:, 3, :], self.msk_b[:, 0, :], self.hm[:, 1:2], ALU.mult, ['msk_b', 'hm'], [('a_m', 3)])
            if isA:
                self.dma(sk, self.sink[l], (), ['a_s'])
                self.act(sk, sk, AF.Exp, ['a_s'], ['a_s'])
                for kvh in range(2):
                    self.cp('dve', es[:, kvh, :].rearrange("p (a b) -> p a b", a=4),
                            sk[:, 4 * kvh:4 * kvh + 4].unsqueeze(2).to_broadcast([128, 4, 128]), ['a_s'], [('a_e', kvh)])
            steps = []
            for qb in range(NB):
                if qb < 2:
                    if not do_ctx:
                        continue
                    kbs = [(0, None), (1, None)]
                elif isA:
                    n = qb - 2
                    kbs = [(n + 2, 2 if n == 0 else 0), (n + 3, None), (n + 4, 3 if n == NB - 3 else 1),
                           (0, None), (1, None)]
                else:
                    kbs = [(kb, None) for kb in range(NKB)]
                for i, (kb, mk) in enumerate(kbs):
                    for kvh in range(2):
                        steps.append(dict(qb=qb, kb=kb, kvh=kvh, mk=mk, first=(i == 0), last=(i == len(kbs) - 1)))
            qseq = {}
            for st in steps:
                qseq.setdefault(st['qb'], len(qseq))
            for i, st in enumerate(steps):
                st['sb'] = i % NS
                st['pb'] = i % NP
                st['ob'] = 4 + 2 * (qseq[st['qb']] % 2) + st['kvh']

            def emit_S(st):
                kvh, qb, kb = st['kvh'], st['qb'], st['kb']
                out = self.ps[:, st['sb'], :].rearrange("p (a b) -> p a b", a=4)
                self.mm(out, kz[:, kvh, 128 * kb:128 * kb + 128], Q[:, :, 128 * qb:128 * qb + 128],
                        True, True, [('a_k', kvh), 'a_q'], [('ps', st['sb'])])
                pp = P[:, st['pb'], :]
                self.act(pp, self.ps[:, st['sb'], :], AF.Exp, [('ps', st['sb'])], [('a_p', st['pb'])], scale=0.125)
                if st['mk'] is not None:
                    self.tt('pool', pp.rearrange("p (a b) -> p a b", a=4), pp.rearrange("p (a b) -> p a b", a=4),
                            mx[:, st['mk'], :].unsqueeze(1).to_broadcast([128, 4, 128]), ALU.mult,
                            [('a_p', st['pb']), ('a_m', st['mk'])], [('a_p', st['pb'])])

            def emit_PV(st):
                kvh, qb, kb, ob = st['kvh'], st['qb'], st['kb'], st['ob']
                self.mm(self.ps[:, ob, :], vx[:, kb, kvh, :], P[:, st['pb'], :], st['first'], st['last'],
                        [('a_p', st['pb']), ('a_v', kvh), 'a_v1'], [('ps', ob)])
                if not st['last']:
                    return
                sl = kvh
                if isA:
                    self.tt('dve', dsum[64:128, sl, :], self.ps[64:128, ob, :], es[64:128, kvh, :], ALU.add,
                            [('ps', ob), ('a_e', kvh)], [('a_d', sl)])
                    self.act(lnd[64:128, sl, :], dsum[64:128, sl, :], AF.Ln, [('a_d', sl)], [('a_l', sl)])
                    self.act(rden[:, sl, :], lnd[64:128, sl, :], AF.Exp, [('a_l', sl)], [('a_r', sl)], scale=-1.0)
                else:
                    self.recip(rden[:, sl, :], self.ps[64:128, ob, :], [('ps', ob)], [('a_r', sl)])
                self.tt('dve', yst[:, sl, :], self.ps[0:64, ob, :], rden[:, sl, :], ALU.mult, [('ps', ob), ('a_r', sl)], [('a_y', sl)])
                ysv = yst[:, sl, :].rearrange("p (a b q) -> p a b q", a=2, b=2)
                for par in range(2):
                    self.dma(ydst[64 * par:64 * par + 64, 2 * kvh:2 * kvh + 2, 128 * qb:128 * qb + 128], ysv[:, :, par, :],
                             [('a_y', sl)], [('yT' + which, qb, kvh, par)])

            for i in range(min(LOOK, len(steps))):
                emit_S(steps[i])
            for i, st in enumerate(steps):
                if i + LOOK < len(steps):
                    emit_S(steps[i + LOOK])
                emit_PV(st)
            self.t.barrier()

    def ssm_phase(self, l):
        nc = self.nc
        do_ctx = l < DEPTH - 1
        sbs = self.dscr(f"sbs{l}", [2, NB, 128, 1024], F32)
        with ExitStack() as stk:
            al = lambda name, shape, dt: stk.enter_context(nc.sbuf_tensor(f"{name}{l}", shape, dt)).ap()
            BT = al("s_bt", [128, 2, T], BF16)
            CT = al("s_ct", [128, 2, T], BF16)
            dt_all = al("s_dt", [128, NB, 32], F32)
            da_all = al("s_da", [128, NB, 32], F32)
            E = al("s_E", [128, NB, 96], F32)
            acf = al("s_ac", [128, 32], F32)
            dsk = al("s_dk", [128, 16], F32)
            gn = al("s_gn", [128, 1024], F32)
            xf = al("s_xf", [128, 2, 8, 128], BF16)
            xf3 = al("s_xf3", [128, 3, 8, 128], BF16)
            xt = al("s_xt", [128, 2, 1024], BF16)
            bm = al("s_bm", [128, 2, 256], BF16)
            w = al("s_w", [128, 2, 32], F32)
            xw = al("s_xw", [128, 2, 2, 1024], BF16)
            H = al("s_H", [128, 2, 1024], F32)
            hb = al("s_hb", [128, 2, 2, 1024], BF16)
            sbt = al("s_sb", [128, 2, 1024], F32)
            R = al("s_R", [128, 2, 1024], F32)
            Lx = al("s_L", [128, 2, 1024], F32)
            M = al("s_M", [128, 4, 1024], BF16)
            ytmp = al("s_yt", [128, 1024], F32)
            ss = al("s_ss", [128, 2], F32)
            yn = al("s_yn", [128, 1024], BF16)
            yst = al("s_ys", [128, 2, 8, 128], BF16)
            xk = [('xbcT', j) for j in range(12)]
            self.dma(BT, self.xbcT[:, 8:10, :], xk, ['s_bt'])
            self.dma(CT, self.xbcT[:, 10:12, :], xk, ['s_ct'])
            self.dma(dt_all, self.dts.rearrange("(b p) c -> p b c", p=128), ['dts'], ['s_dt'])
            self.dma(acf, self.alog[l], (), ['s_ac'])
            self.dma(dsk, self.dsk[l], (), ['s_dk'])
            self.dma(gn, self.sng[l], (), ['s_gn'])
            self.act(acf, acf, AF.Exp, ['s_ac'], ['s_ac'])
            self.ts('dve', acf, acf, -1.0, ALU.mult, ['s_ac'], ['s_ac'])
            self.tt('dve', da_all, dt_all, acf.unsqueeze(1).to_broadcast([128, NB, 32]), ALU.mult, ['s_dt', 's_ac'], ['s_da'])
            self.memset('dve', H, 0.0, [('s_H', 0), ('s_H', 1)])
            psb = lambda b: self.ps[:, b, :].bitcast(BF16)

            def b16(ap):
                return ap.rearrange("p (h d) -> p h d", h=16)

            def bc16(ap):
                return ap.unsqueeze(2).to_broadcast([128, 16, 64])

            def load_xs(c, slot):
                self.dma(xf[:, slot], self.xbcT[:, 0:8, 128 * c:128 * c + 128], xk, [('s_xf', slot)])
                pt = psb(7)
                for f in range(8):
                    self.tr(pt[:, 128 * f:128 * f + 128], xf[:, slot, f, :], self.ident, [('s_xf', slot), 'ident'], [('ps', 7)])
                self.cp('act', xt[:, slot, :], pt, [('ps', 7)], [('s_xt', slot)])

            def p1_load(c):
                if c < NB:
                    self.dma(xf3[:, c % 3], self.xbcT[:, 0:8, 128 * c:128 * c + 128], xk, [('s_xf3', c % 3)])

            def p1_a(c):
                s = c % 2
                pt7 = psb(7)
                for f in range(8):
                    self.tr(pt7[:, 128 * f:128 * f + 128], xf3[:, c % 3, f, :], self.ident, [('s_xf3', c % 3), 'ident'], [('ps', 7)])
                self.cp('act', xt[:, s, :], pt7, [('ps', 7)], [('s_xt', s)])
                pt = psb(0)
                for g in range(2):
                    self.tr(pt[:, 128 * g:128 * g + 128], BT[:, g, 128 * c:128 * c + 128], self.ident, ['s_bt', 'ident'], [('ps', 0)])
                self.cp('dve', bm[:, s, :], pt[:, 0:256], [('ps', 0)], [('s_bm', s)])
                pc = self.ps[:, 1, :]
                for (c0, mi, d0, dn) in ((0, 0, 0, 16), (16, 1, 0, 16), (32, 2, 16, 16), (48, 3, 16, 16), (64, 4, 0, 32)):
                    self.mm(pc[:, c0:c0 + dn], self.msk_f[:, mi, :], da_all[:, c, d0:d0 + dn], True, True,
                            ['s_da', 'msk_f'], [('ps', 1)])
                self.act(E[:, c, :], pc[:, 0:96], AF.Exp, [('ps', 1)], [('s_E', c)])
                self.tt('dve', w[:, s, :].rearrange("p (a b) -> p a b", a=2), dt_all[:, c, :].rearrange("p (a b) -> p a b", a=2),
                        E[:, c, 16:80].rearrange("p (a b) -> p a b", a=2)[:, :, 0:16], ALU.mult, ['s_dt', ('s_E', c)], [('s_w', s)])
                for d in range(2):
                    self.tt('dve' if d == 0 else 'pool', b16(xw[:, s, d, :]), b16(xt[:, s, :]), bc16(w[:, s, 16 * d:16 * d + 16]), ALU.mult,
                            [('s_xt', s), ('s_w', s)], [('s_xw', s, d)])

            def p1_b(c):
                s = c % 2
                for d in range(2):
                    for g in range(2):
                        bk = 2 + 2 * d + g
                        self.mm(self.ps[:, bk, :], bm[:, s, 128 * g:128 * g + 128], xw[:, s, d, 512 * g:512 * g + 512], True, True,
                                [('s_bm', s), ('s_xw', s, d)], [('ps', bk)])
                for d in range(2):
                    self.cp('act', sbt[:, d, :], self.ps[:, 2 + 2 * d:4 + 2 * d, :].rearrange("p a b -> p (a b)"),
                            [('ps', 2 + 2 * d), ('ps', 3 + 2 * d)], [('s_sb', d)])
                    self.dma(sbs[d, c], sbt[:, d, :], [('s_sb', d)], [('sbs', d, c)])

            p1_load(0)
            p1_load(1)
            p1_a(0)
            for c in range(NB):
                p1_load(c + 2)
                if c + 1 < NB:
                    p1_a(c + 1)
                p1_b(c)
            self.dump("dbg_E", E, [('s_E', c_) for c_ in range(NB)])
            rstk = ExitStack()
            alr = lambda name, shape, dt: rstk.enter_context(nc.sbuf_tensor(f"{name}{l}", shape, dt)).ap()
            Hc = alr("s_Hc", [128, 2, 1024], F32)
            Gx = alr("s_Gx", [128, 2, 1024], F32)
            fwd_lat = list(range(2, NB))
            bwd_lat = list(range(NB - 1, 1, -1))

            sbr = alr("s_sbr", [128, 2, 4, 1024], F32)
            rk = [0]

            def recur(orders, store):
                n = len(orders[0])

                def ld(i):
                    if i >= n:
                        return
                    for d in range(2):
                        c = orders[d][i]
                        sl = (rk[0] + i) % 4
                        self.dma(sbr[:, d, sl, :], sbs[d, c], [('sbs', d, c)], [('s_sbr', d, sl)])
                for i in range(3):
                    ld(i)
                for i in range(n):
                    ld(i + 3)
                    for d in range(2):
                        c = orders[d][i]
                        sl = (rk[0] + i) % 4
                        if store:
                            hs = i % 2
                            self.cp('act', hb[:, d, hs, :], H[:, d, :], [('s_H', d)], [('s_hb', d, hs)])
                            self.dma(self.hst[d, c], hb[:, d, hs, :], [('s_hb', d, hs)], [('hst', d, c)])
                        self.tt('dve', b16(H[:, d, :]), b16(H[:, d, :]), bc16(E[:, c, 64 + 16 * d:80 + 16 * d]), ALU.mult,
                                [('s_H', d), ('s_E', c)], [('s_H', d)])
                        self.tt('dve', H[:, d, :], H[:, d, :], sbr[:, d, sl, :], ALU.add, [('s_H', d), ('s_sbr', d, sl)], [('s_H', d)])
                rk[0] += n

            recur(([0, 1], [1, 0]), True)
            for d in range(2):
                self.cp('dve', Hc[:, d, :], H[:, d, :], [('s_H', d)], [('s_Hc', d)])
            recur((fwd_lat, bwd_lat), False)
            self.dma(self.s_src.rearrange("p (d f) -> p d f", d=2), H, [('s_H', 0), ('s_H', 1)], ['s_src'])
            self.t.cc(self.s_src, self.s_dst, ['s_src'], ['s_dst'])
            self.dma(Gx[:, 0, :], self.s_dst[0:128, 0:1024], ['s_dst'], [('s_Gx', 0)])
            self.dma(Gx[:, 1, :], self.s_dst[128:256, 1024:2048], ['s_dst'], [('s_Gx', 1)])
            for d in range(2):
                own, oth = (1, 0) if d == 0 else (0, 1)
                self.ts('dve', Gx[:, d, :], Gx[:, d, :], self.hm[:, oth:oth + 1], ALU.mult, [('s_Gx', d), 'hm'], [('s_Gx', d)])
                self.stt('dve', H[:, d, :], Hc[:, d, :], self.hm[:, own:own + 1], Gx[:, d, :], ALU.mult, ALU.add,
                         [('s_Hc', d), 'hm', ('s_Gx', d)], [('s_H', d)])
            recur((fwd_lat, bwd_lat), True)
            self.t.barrier()
            rstk.close()
            zt3 = al("s_zt3", [128, 3, 1024], F32)
            hb3 = al("s_hb3", [128, 2, 3, 1024], BF16)

            def loads(c):
                s3 = c % 3
                tk_ = slice(128 * c, 128 * c + 128)
                self.dma(xf3[:, s3], self.xbcT[:, 0:8, tk_], xk, [('s_xf3', s3)])
                self.dma(zt3[:, s3, :], self.zs[tk_, :], [('zs', c)], [('s_z3', s3)])
                for d in range(2):
                    self.dma(hb3[:, d, s3, :], self.hst[d, c], [('hst', d, c)], [('s_hb3', d, s3)])
            ya2 = al("s_ya2", [128, 2, 1024], F32)
            yt2 = al("s_yt2", [128, 2, 1024], F32)
            cb2 = al("s_cb2", [128, 2, 2, 2, 128], F32)

            def front(c):
                s = c % 2
                s3 = c % 3
                tk = slice(128 * c, 128 * c + 128)
                pt_ = psb(7)
                for f in range(8):
                    self.tr(pt_[:, 128 * f:128 * f + 128], xf3[:, s3, f, :], self.ident, [('s_xf3', s3), 'ident'], [('ps', 7)])
                self.cp('act', xt[:, s, :], pt_, [('ps', 7)], [('s_xt', s)])
                for d in range(2):
                    self.tt('pool', b16(xw[:, s, d, :]), b16(xt[:, s, :]), bc16(dt_all[:, c, 16 * d:16 * d + 16]), ALU.mult,
                            [('s_xt', s), 's_dt'], [('s_xw', s, d)])
                pcb = self.ps[:, 0, 0:256]
                for g in range(2):
                    self.mm(pcb[:, 128 * g:128 * g + 128], BT[:, g, tk], CT[:, g, tk], True, True, ['s_bt', 's_ct'], [('ps', 0)])
                for d in range(2):
                    self.tt('dve', cb2[:, s, d, :, :], pcb.rearrange("p (g i) -> p g i", g=2),
                            self.msk_f[:, 0 if d == 0 else 2, :].unsqueeze(1).to_broadcast([128, 2, 128]), ALU.mult,
                            [('ps', 0), 'msk_f'], [('s_cb', s, d)])
                R4 = R.rearrange("p a (b f) -> p (a b) f", b=2)
                L4 = Lx.rearrange("p a (b f) -> p (a b) f", b=2)
                its = [(d, g, hf) for d in range(2) for g in range(2) for hf in range(2)]

                def emit_R(k):
                    d, g, hf = its[k]
                    sl = k % 4
                    h0 = 16 * d + 8 * g + 4 * hf
                    for hq in range(4):
                        self.act(R4[:, sl, 128 * hq:128 * hq + 128], self.msk_f[:, 0 if d == 0 else 2, :], AF.Identity,
                                 ['msk_f', 's_da'], [('s_R', sl, hq)], scale=da_all[:, c, h0 + hq:h0 + hq + 1])
                emit_R(0)
                emit_R(1)
                for k, (d, g, hf) in enumerate(its):
                    sl = k % 4
                    bk = 1 + k % 2
                    self.mm(self.ps[:, bk, :], self.msk_f[:, 1 if d == 0 else 3, :], R4[:, sl, :], True, True,
                            [('s_R', sl, hq) for hq in range(4)] + ['msk_f'], [('ps', bk)])
                    self.act(L4[:, sl, :], self.ps[:, bk, :], AF.Exp, [('ps', bk)], [('s_L', sl)])
                    if k + 2 < len(its):
                        emit_R(k + 2)
                    self.tt('dve', M[:, 2 * d + g, 512 * hf:512 * hf + 512].rearrange("p (h i) -> p h i", h=4),
                            L4[:, sl, :].rearrange("p (h i) -> p h i", h=4),
                            cb2[:, s, d, g, :].unsqueeze(1).to_broadcast([128, 4, 128]), ALU.mult,
                            [('s_L', sl), ('s_cb', s, d)], [('s_M', 2 * d + g, hf)])
                for g in range(2):
                    for hg in range(8):
                        hh = 8 * g + hg
                        for d in range(2):
                            self.mm(self.ps[:, 3 + g, 64 * hg:64 * hg + 64], M[:, 2 * d + g, 128 * hg:128 * hg + 128],
                                    xw[:, s, d, 64 * hh:64 * hh + 64], d == 0, d == 1,
                                    [('s_M', 2 * d + g, hg // 4), ('s_xw', s, d)], [('ps', 3 + g)])
                for d in range(2):
                    for g in range(2):
                        bk = 5 + (2 * d + g) % 2
                        self.mm(self.ps[:, bk, :], CT[:, g, tk], hb3[:, d, c % 3, 512 * g:512 * g + 512], True, True,
                                ['s_ct', ('s_hb3', d, c % 3)], [('ps', bk)])
                        dst = (ya2 if d == 0 else yt2)[:, s, 512 * g:512 * g + 512]
                        self.tt('dve', dst.rearrange("p (h e) -> p h e", h=8), self.ps[:, bk, :].rearrange("p (h e) -> p h e", h=8),
                                E[:, c, 32 * d + 8 * g:32 * d + 8 * g + 8].unsqueeze(2).to_broadcast([128, 8, 64]), ALU.mult,
                                [('ps', bk), ('s_E', c)], [('s_ya', s, g) if d == 0 else ('s_yt', s, g)])
                for g in range(2):
                    hs = slice(512 * g, 512 * g + 512)
                    self.tt('dve', ya2[:, s, hs], ya2[:, s, hs], self.ps[:, 3 + g, :], ALU.add,
                            [('s_ya', s, g), ('ps', 3 + g)], [('s_ya', s, g)])

            def tail(c):
                s = c % 2
                tk = slice(128 * c, 128 * c + 128)
                ya_ = ya2[:, s, :]
                yk = [('s_ya', s, 0), ('s_ya', s, 1)]
                self.tt('pool', ya_, ya_, yt2[:, s, :], ALU.add, yk + [('s_yt', s, 0), ('s_yt', s, 1)], yk)
                self.tt('pool', b16(ytmp), b16(xt[:, s, :]), bc16(dsk), ALU.mult, [('s_xt', s), 's_dk'], ['s_y3'])
                self.tt('pool', ya_, ya_, ytmp, ALU.add, yk + ['s_y3'], yk)
                self.tt('pool', ya_, ya_, zt3[:, c % 3, :], ALU.mult, yk + [('s_z3', c % 3)], yk)
                self.act(ytmp, ya_, AF.Square, yk, ['s_y3'])
                self.t.op('dve', lambda: nc.vector.reduce_sum(out=ss[:, 0:1], in_=ytmp, axis=mybir.AxisListType.X), ['s_y3'], ['s_ss'])
                self.act(ss[:, 1:2], ss[:, 0:1], AF.Ln, ['s_ss', 'epsc'], ['s_ss'], bias=self.epsc, scale=1.0 / 1024)
                self.act(ss[:, 1:2], ss[:, 1:2], AF.Exp, ['s_ss'], ['s_ss'], scale=-0.5)
                self.stt('dve', yn, ya_, ss[:, 1:2], gn, ALU.mult, ALU.mult, yk + ['s_ss', 's_gn'], ['s_yn'])
                pt = psb(0)
                for f in range(8):
                    self.tr(pt[:, 128 * f:128 * f + 128], yn[:, 128 * f:128 * f + 128], self.ident, ['s_yn', 'ident'], [('ps', 0)])
                self.cp('act', yst[:, s].rearrange("p f t -> p (f t)"), pt, [('ps', 0)], [('s_ys', s)])
                self.dma(self.ysT[:, :, tk], yst[:, s], [('s_ys', s)], [('ysT', c)])

            chunks = [c for c in range(NB) if not (c < 2 and not do_ctx)]
            loads(chunks[0])
            loads(chunks[1])
            front(chunks[0])
            for i, c in enumerate(chunks):
                if i + 2 < len(chunks):
                    loads(chunks[i + 2])
                if i + 1 < len(chunks):
                    front(chunks[i + 1])
                tail(c)
            self.t.barrier()

    def merge_phase(self, l, xsrc):
        nc = self.nc
        do_ctx = l < DEPTH - 1
        with ExitStack() as stk:
            al = lambda name, shape, dt: stk.enter_context(nc.sbuf_tensor(f"{name}{l}", shape, dt)).ap()
            woa = al("g_woa", [128, 4, 1024], BF16)
            woc = al("g_woc", [128, 4, 1024], BF16)
            wob = al("g_wob", [128, 8, 1024], BF16)
            wout = al("g_wout", [128, 8, 1024], BF16)
            stg = al("g_stg", [128, 4, 1024], F32)
            ya = al("g_ya", [128, 2, 4, 256], BF16)
            yc = al("g_yc", [128, 2, 4, 256], BF16)
            ys = al("g_ys", [128, 2, 8, 256], BF16)
            gt = al("g_gt", [128, 2, 24, 256], BF16)
            xg = al("g_xg", [128, 2, 8, 256], F32)
            mT = al("g_mT", [128, 2, 8, 256], BF16)
            ta = al("g_ta", [128, 2, 256], F32)
            tb = al("g_tb", [128, 2, 256], F32)
            tc_ = al("g_tc", [128, 2, 256], F32)
            k = 0
            for (wsrc, wdst, np_, nk) in ((self.woa[l], woa, 128, 4), (self.woc[l], woc, 128, 4), (self.wob[l], wob, 128, 8),
                                          (self.wout[l], wout, 128, 8)):
                for kc in range(nk):
                    sl = k % 4
                    eng_ = ('dve', 'act', 'dve', 'pool')[k % 4]
                    k += 1
                    self.dma(stg[0:np_, sl, :], wsrc[:, kc, :], (), [('g_stg', sl)])
                    self.cp(eng_, wdst[:, kc, :], stg[0:np_, sl, :], [('g_stg', sl)], [('g_w', id(wdst) % 1000, kc)])
            wk = lambda wdst: [('g_w', id(wdst) % 1000, kc) for kc in range(wdst.shape[1])]
            wins = [(wi, t0, n_) for wi, (t0, n_) in enumerate(self.windows()) if not (wi == 0 and not do_ctx)]

            def mg_loads(wi, t0, n_):
                gi = 0 if wi == 0 else 1 + (wi - 1) // 2
                s = wi % 2
                tk = slice(t0, t0 + n_)
                self.dma(ya[:, s, :, 0:n_], self.yaT[:, :, tk], ['yTA'], [('g_ya', s)])
                self.dma(yc[:, s, :, 0:n_], self.ycT[:, :, tk], ['yTC'], [('g_yc', s)])
                self.dma(ys[:, s, :, 0:n_], self.ysT[:, :, tk], ['ysT'], [('g_ys', s)])
                self.dma(gt[:, s, :, 0:n_], self.gtT[:, :, tk], ['gtT'], [('g_gt', s)])
                self.dma(xg[:, s, :, 0:n_], xsrc[:, :, tk], [('xT', gi)], [('g_xg', s)])

            mg_loads(*wins[0])
            for wpos, (wi, t0, n_) in enumerate(wins):
                if wpos + 1 < len(wins):
                    mg_loads(*wins[wpos + 1])
                gi = 0 if wi == 0 else 1 + (wi - 1) // 2
                s = wi % 2
                cls = 1 if t0 < LC else 0
                tk = slice(t0, t0 + n_)
                for j in range(8):
                    js = j % 2
                    cs = slice(128 * j, 128 * j + 128)
                    bA, bB, bC = 3 * js, 3 * js + 1, 3 * js + 2
                    for h in range(4):
                        self.mm(self.ps[:, bA, 0:n_], woa[:, h, cs], ya[:, s, h, 0:n_], h == 0, h == 3, wk(woa) + [('g_ya', s)], [('ps', bA)])
                    for kc in range(8):
                        self.mm(self.ps[:, bB, 0:n_], wob[:, kc, cs], ys[:, s, kc, 0:n_], kc == 0, kc == 7, wk(wob) + [('g_ys', s)], [('ps', bB)])
                    for h in range(4):
                        self.mm(self.ps[:, bC, 0:n_], woc[:, h, cs], yc[:, s, h, 0:n_], h == 0, h == 3, wk(woc) + [('g_yc', s)], [('ps', bC)])
                    self.tt('dve', ta[:, js, 0:n_], self.ps[:, bA, 0:n_], gt[:, s, j, 0:n_], ALU.mult, [('ps', bA), ('g_gt', s)], [('g_ta', js)])
                    self.tt('dve', tb[:, js, 0:n_], self.ps[:, bB, 0:n_], gt[:, s, 8 + j, 0:n_], ALU.mult, [('ps', bB), ('g_gt', s)], [('g_tb', js)])
                    self.tt('dve', tc_[:, js, 0:n_], self.ps[:, bC, 0:n_], gt[:, s, 16 + j, 0:n_], ALU.mult, [('ps', bC), ('g_gt', s)], [('g_tc', js)])
                    self.tt('pool', ta[:, js, 0:n_], ta[:, js, 0:n_], tb[:, js, 0:n_], ALU.add, [('g_ta', js), ('g_tb', js)], [('g_ta', js)])
                    self.tt('pool', mT[:, s, j, 0:n_], ta[:, js, 0:n_], tc_[:, js, 0:n_], ALU.add, [('g_ta', js), ('g_tc', js)], [('g_mT', s, j)])
                for j in range(8):
                    bk = 6 + j % 2
                    cs = slice(128 * j, 128 * j + 128)
                    for kc in range(8):
                        self.mm(self.ps[:, bk, 0:n_], wout[:, kc, cs], mT[:, s, kc, 0:n_], kc == 0, kc == 7,
                                wk(wout) + [('g_mT', s, kc)], [('ps', bk)])
                    self.stt('dve', xg[:, s, j, 0:n_], self.ps[:, bk, 0:n_], self.modT[:, 16 + j, cls:cls + 1], xg[:, s, j, 0:n_],
                             ALU.mult, ALU.add, [('ps', bk), 'modT', ('g_xg', s)], [('g_xg', s)])
                self.dma(self.xT[:, :, tk], xg[:, s, :, 0:n_], [('g_xg', s)], [('xT', gi)])
            self.t.barrier()

    def ffn_phase(self, l):
        nc = self.nc
        do_ctx = l < DEPTH - 1
        with ExitStack() as stk0:
            al0 = lambda name, shape, dt: stk0.enter_context(nc.sbuf_tensor(f"{name}{l}", shape, dt)).ap()
            wdn = al0("d_w", [128, 22, 1024], BF16)
            dstg = al0("d_stg", [128, 2, 1024], F32)
            self._ffn(l, wdn, dstg)

    def _ffn(self, l, wdn, dstg):
        nc = self.nc
        do_ctx = l < DEPTH - 1
        with ExitStack() as stk:
            al = lambda name, shape, dt: stk.enter_context(nc.sbuf_tensor(f"{name}{l}", shape, dt)).ap()
            ws = al("f_ws", [128, 4, 8, 128], F32)
            wb = al("f_wb", [128, 4, 8, 128], BF16)
            cw = al("f_cw", [128, 22, 4], F32)
            orow = al("f_or", [128, 2, T], BF16)
            acc = al("f_acc", [128, 4, 256], F32)
            sg = al("f_sg", [128, 4, 256], F32)
            self.dma(cw, self.fcw[l], (), ['f_cw'])

            def load(f):
                if f >= 22:
                    return
                for i, src in enumerate((self.wup, self.wgt)):
                    sl = (2 * f + i) % 4
                    self.dma(ws[:, sl], src[l, f], (), [('f_ws', sl)])
                    self.cp('pool', wb[:, sl], ws[:, sl], [('f_ws', sl)], [('f_wb', sl)])
            load(0)
            for f in range(22):
                load(f + 1)
                self.dma(dstg[:, f % 2, :], self.wdn[l, :, f, :], (), [('d_stg', f % 2)])
                self.cp('pool', wdn[:, f, :], dstg[:, f % 2, :], [('d_stg', f % 2)], [('d_w', f)])
                su, sg_ = (2 * f) % 4, (2 * f + 1) % 4
                osl = f % 2
                pend = None
                for wi, (t0, n_) in enumerate(self.windows()):
                    if wi == 0 and not do_ctx:
                        continue
                    s = wi % 4
                    bG, bU = 2 * s, 2 * s + 1
                    c0 = hcol(t0)
                    pG, pU = self.ps[:, bG, 0:258], self.ps[:, bU, 0:256]
                    hk = self.hkeys(c0 - 1, c0 + 257)
                    for kc in range(8):
                        self.mm(pG, wb[:, sg_, kc, :], self.hT[:, kc, c0 - 1:c0 + 257], kc == 0, kc == 7, hk + [('f_wb', sg_)], [('ps', bG)])
                    for kc in range(8):
                        self.mm(pU, wb[:, su, kc, :], self.hT[:, kc, c0:c0 + 256], kc == 0, kc == 7, hk + [('f_wb', su)], [('ps', bU)])
                    a = acc[:, s, :]
                    self.act(a, pG[:, 0:256], AF.Identity, [('ps', bG), 'f_cw'], [('f_acc', s)], scale=cw[:, f, 0:1])
                    self.stt('dve', a, pG[:, 1:257], cw[:, f, 1:2], a, ALU.mult, ALU.add, [('ps', bG), 'f_cw', ('f_acc', s)], [('f_acc', s)])
                    self.stt('dve', a, pG[:, 2:258], cw[:, f, 2:3], a, ALU.mult, ALU.add, [('ps', bG), 'f_cw', ('f_acc', s)], [('f_acc', s)])
                    if pend is not None:
                        pend()

                    def pend(a=a, s=s, t0=t0, osl=osl, f=f, pU=pU, bU=bU):
                        self.act(sg[:, s, :], a, AF.Silu, [('f_acc', s), 'f_cw'], [('f_sg', s)], bias=cw[:, f, 3:4])
                        self.tt('dve', orow[:, osl, t0:t0 + 256], sg[:, s, :], pU, ALU.mult, [('f_sg', s), ('ps', bU)], [('f_or', osl)])
                pend()
                pend = None
                self.dma(self.actT[:, f, :], orow[:, osl, :], [('f_or', osl)], [('actT', f)])
            self.t.barrier()
        with ExitStack() as stk:
            al = lambda name, shape, dt: stk.enter_context(nc.sbuf_tensor(f"{name}{l}", shape, dt)).ap()
            at = al("d_at", [128, 2, 22, 512], BF16)
            xg = al("d_xg", [128, 2, 8, 512], F32)
            wk = []
            ak = [('actT', f) for f in range(22)]
            grps = [(gi, t0, n_) for gi, (t0, n_) in enumerate(self.groups()) if not (gi == 0 and not do_ctx)]

            def dn_loads(gi, t0, n_):
                s = gi % 2
                tk = slice(t0, t0 + n_)
                self.dma(at[:, s, :, 0:n_], self.actT[:, :, tk], ak, [('d_at', s)])
                self.dma(xg[:, s, :, 0:n_], self.xT[:, :, tk], [('xT', gi)], [('d_xg', s)])

            dn_loads(*grps[0])
            for gpos, (gi, t0, n_) in enumerate(grps):
                if gpos + 1 < len(grps):
                    dn_loads(*grps[gpos + 1])
                s = gi % 2
                cls = 1 if t0 < LC else 0
                tk = slice(t0, t0 + n_)
                for j in range(8):
                    bk = j % 4
                    cs = slice(128 * j, 128 * j + 128)
                    for kc in range(22):
                        self.mm(self.ps[:, bk, 0:n_], wdn[:, kc, cs], at[:, s, kc, 0:n_], kc == 0, kc == 21, wk + [('d_at', s)], [('ps', bk)])
                    self.stt('dve', xg[:, s, j, 0:n_], self.ps[:, bk, 0:n_], self.modT[:, 40 + j, cls:cls + 1], xg[:, s, j, 0:n_],
                             ALU.mult, ALU.add, [('ps', bk), 'modT', ('d_xg', s)], [('d_xg', s)])
                self.dma(self.xT[:, :, tk], xg[:, s, :, 0:n_], [('d_xg', s)], [('xT', gi)])
            self.t.barrier()

    def final_norm(self):
        nc = self.nc
        with ExitStack() as stk:
            al = lambda name, shape, dt: stk.enter_context(nc.sbuf_tensor(name, shape, dt)).ap()
            x_sb = al("fn_x", [128, 2, 8, 512], F32)
            sq_sb = al("fn_sq", [128, 2, 8, 512], BF16)
            r_sb = al("fn_r", [128, 2, 512], F32)
            o_sb = al("fn_o", [128, 2, 8, 512], F32)
            g_sb = al("fn_g", [128, 8], F32)
            self.dma(g_sb, self.fng, (), ['fn_g'])
            fgr = [(gi, t0, n_) for gi, (t0, n_) in enumerate(self.groups()) if gi > 0]

            def fn_load(gi, t0, n_):
                self.dma(x_sb[:, gi % 2, :, 0:n_], self.xT[:, :, t0:t0 + n_], [('xT', gi)], [('fn_x', gi % 2)])

            fn_load(*fgr[0])
            for gpos, (gi, t0, n_) in enumerate(fgr):
                if gpos + 1 < len(fgr):
                    fn_load(*fgr[gpos + 1])
                s = gi % 2
                xg = x_sb[:, s, :, 0:n_]
                self.act(sq_sb[:, s, :, 0:n_], xg, AF.Square, [('fn_x', s)], [('fn_sq', s)])
                pt = self.ps[:, s, 0:n_]
                for kc in range(8):
                    self.mm(pt, self.ones_ms, sq_sb[:, s, kc, 0:n_], kc == 0, kc == 7, [('fn_sq', s), 'ones_ms'], [('ps', s)])
                rr = r_sb[:, s, 0:n_]
                self.act(rr, pt, AF.Ln, [('ps', s), 'epsc'], [('fn_r', s)], bias=self.epsc)
                self.act(rr, rr, AF.Exp, [('fn_r', s)], [('fn_r', s)], scale=-0.5)
                for kc in range(8):
                    self.stt('dve', o_sb[:, s, kc, 0:n_], xg[:, kc, :], g_sb[:, kc:kc + 1], rr, ALU.mult, ALU.mult,
                             [('fn_x', s), 'fn_g', ('fn_r', s)], [('fn_o', s, kc)])
                self.dma(self.outT[:, :, t0 - LC:t0 - LC + n_], o_sb[:, s, :, 0:n_], [('fn_o', s, kc) for kc in range(8)], [('outT', gi)])
            self.t.barrier()
        return []


def _fm(w):
    K, C = w.shape
    return np.ascontiguousarray(w.reshape(K // 128, 128, C // 128, 128).transpose(2, 1, 0, 3))


def _rowsp(v, kc):
    return np.ascontiguousarray(v.reshape(kc, 128).T)


def _rope_table(hf):
    rows = L // 64
    t_row = np.repeat(np.arange(rows), 64).astype(np.float32)
    t_col = np.tile(np.arange(64), rows).astype(np.float32)
    n = 16
    inv = (10000.0 ** (-np.arange(n, dtype=np.float32) / n)).astype(np.float32)
    ang = np.concatenate([t_row[:, None] * inv, t_col[:, None] * inv], axis=-1)
    ang = ang[hf * LH:(hf + 1) * LH]
    cos = np.concatenate([np.ones((LC, 32), np.float32), np.cos(ang).astype(np.float32)], 0).T
    sin = np.concatenate([np.zeros((LC, 32), np.float32), np.sin(ang).astype(np.float32)], 0).T
    t1 = np.concatenate([cos, sin, cos, sin], 0)
    t2 = np.concatenate([-sin, cos, -sin, cos], 0)
    return np.ascontiguousarray(np.stack([t1, t2], 0)).astype(np.float32)


def _const_tables():
    k = np.arange(128)[:, None]
    i = np.arange(128)[None, :]
    masks = np.stack([(k <= i), (k > i), (k >= i), (k < i), np.ones((128, 128), bool), (k == i)], 0).astype(np.float32)
    return masks


def _in_cols():
    o = {}
    s = 0
    for name, n in (('a_q', 512), ('a_k', 128), ('a_v', 128), ('b_z', 1024), ('b_xbc', 1536), ('b_dt', 32),
                    ('c_q', 512), ('c_k', 128), ('c_v', 128), ('gates', 3072)):
        o[name] = s
        s += n
    ev = np.arange(0, 64, 2)
    od = np.arange(1, 64, 2)
    tiles = []

    def rope_pair(base, h0, h1):
        a = np.concatenate([base + h0 * 64 + ev, base + h0 * 64 + ev, base + h1 * 64 + ev, base + h1 * 64 + ev])
        b = np.concatenate([base + h0 * 64 + od, base + h0 * 64 + od, base + h1 * 64 + od, base + h1 * 64 + od])
        tiles.append(a)
        tiles.append(b)
    for m in range(4):
        rope_pair(o['a_q'], m, m + 4)
    rope_pair(o['a_k'], 0, 1)
    for m in range(4):
        rope_pair(o['c_q'], m, m + 4)
    rope_pair(o['c_k'], 0, 1)
    for j in range(12):
        tiles.append(o['b_xbc'] + j * 128 + np.arange(128))
    for j in range(24):
        tiles.append(o['gates'] + j * 128 + np.arange(128))
    fm = np.concatenate(tiles)
    tm = np.concatenate([o['b_z'] + np.arange(1024), o['a_v'] + np.arange(128), o['c_v'] + np.arange(128),
                         o['b_dt'] + np.arange(32)])
    return fm, tm


def prep_shared(inp):
    f = lambda a: np.ascontiguousarray(a, dtype=np.float32)
    masks = _const_tables()
    fmc, tmc = _in_cols()
    ev = np.arange(0, 64, 2)
    od = np.arange(1, 64, 2)
    sh = {}
    sh["wmod"] = f(np.stack([_fm(inp["w_mod"][l]) for l in range(DEPTH)]))
    sh["bmod"] = f(np.stack([_rowsp(inp["b_mod"][l], 48) for l in range(DEPTH)]))
    sh["nrm"] = f(np.stack([np.stack([_rowsp(inp["norm1"][l], 8), _rowsp(inp["norm2"][l], 8)], 1) for l in range(DEPTH)]))
    sh["wfm"] = f(np.stack([_fm(inp["w_in"][l][:, fmc]) for l in range(DEPTH)]))
    sh["wtm"] = f(np.stack([inp["w_in"][l][:, tmc].reshape(8, 128, NTM).transpose(1, 0, 2) for l in range(DEPTH)]))
    cg = []
    for l in range(DEPTH):
        q, k = inp["c_q_norm"][l], inp["c_k_norm"][l]
        cg.append(np.stack([np.tile(q[ev], 4), np.tile(q[od], 4), np.tile(k[ev], 4), np.tile(k[od], 4)], 1))
    sh["cgain"] = f(np.stack(cg))
    sh["cw"] = f(np.stack([np.concatenate([inp["ssm_conv_w"][l], inp["ssm_conv_b"][l][None]], 0).reshape(4, 12, 128).transpose(2, 1, 0)
                           for l in range(DEPTH)]))
    sh["dtb"] = f(np.stack([np.broadcast_to(inp["ssm_dt_bias"][l].reshape(1, 32), (128, 32)) for l in range(DEPTH)]))
    sh["alog"] = f(np.stack([np.broadcast_to(inp["ssm_A_log"][l].reshape(1, 32), (128, 32)) for l in range(DEPTH)]))
    sh["dsk"] = f(np.stack([np.broadcast_to(inp["ssm_D"][l].reshape(1, 16), (128, 16)) for l in range(DEPTH)]))
    sh["sng"] = f(np.stack([np.broadcast_to(inp["ssm_norm"][l].reshape(1, 1024), (128, 1024)) for l in range(DEPTH)]))
    sh["sink"] = f(np.stack([np.broadcast_to(inp["a_sink"][l].reshape(1, 8), (128, 8)) for l in range(DEPTH)]))
    sh["woa"] = f(np.stack([inp["w_oa"][l].reshape(4, 128, 1024).transpose(1, 0, 2) for l in range(DEPTH)]))
    sh["woc"] = f(np.stack([inp["w_oc"][l].reshape(4, 128, 1024).transpose(1, 0, 2) for l in range(DEPTH)]))
    sh["wob"] = f(np.stack([inp["w_ob"][l].reshape(8, 128, 1024).transpose(1, 0, 2) for l in range(DEPTH)]))
    sh["wout"] = f(np.stack([inp["w_out"][l].reshape(8, 128, 1024).transpose(1, 0, 2) for l in range(DEPTH)]))
    sh["wup"] = f(np.stack([_fm(inp["ffn_w_up"][l]) for l in range(DEPTH)]))
    sh["wgt"] = f(np.stack([_fm(inp["ffn_w_gate"][l]) for l in range(DEPTH)]))
    sh["fcw"] = f(np.stack([np.concatenate([inp["ffn_conv_w"][l], inp["ffn_conv_b"][l][None]], 0).reshape(4, 22, 128).transpose(2, 1, 0)
                            for l in range(DEPTH)]))
    sh["wdn"] = f(np.stack([inp["ffn_w_down"][l].reshape(22, 128, 1024).transpose(1, 0, 2) for l in range(DEPTH)]))
    sh["fng"] = f(_rowsp(inp["final_norm"], 8))
    sh["masks"] = masks
    sh["ident"] = np.eye(128, dtype=np.float32)
    return sh


def prep_core(inp, c):
    b, hf = c // 2, c % 2
    xl = inp["x"][b]
    xa = np.concatenate([inp["ctx"][b], xl[hf * LH:(hf + 1) * LH]], 0)
    xT0 = np.ascontiguousarray(xa.T.reshape(8, 128, T).transpose(1, 0, 2), dtype=np.float32)
    cv = np.stack([_rowsp(inp["c"][b], 8), _rowsp(inp["c_ctx"], 8)], -1)
    zero = np.zeros((D,), np.float32)
    left = xl[hf * LH - 1] if hf == 1 else zero
    right = xl[(hf + 1) * LH] if hf == 0 else zero
    xh0 = np.stack([_rowsp(left, 8), _rowsp(right, 8)], -1)
    hmask = np.broadcast_to(np.array([[float(hf), float(1 - hf)]], np.float32), (128, 2))
    return {"xT0": xT0, "cvec": np.ascontiguousarray(cv, dtype=np.float32), "xh0": np.ascontiguousarray(xh0, dtype=np.float32),
            "hmask": np.ascontiguousarray(hmask, dtype=np.float32), "rope": _rope_table(hf)}


_PROG = None


def kernel(**inp):
    global _PROG
    inp = {k: np.asarray(v) for k, v in inp.items()}
    if _PROG is None:
        _PROG = Prog()
    p = _PROG
    sh = prep_shared(inp)
    in_maps = []
    for c in range(8):
        m = dict(sh)
        m.update(prep_core(inp, c))
        in_maps.append(m)
    res = run_bass_kernel_spmd(p.nc, in_maps, core_ids=list(range(8)))
    out = np.empty((4, L, D), np.float32)
    for c in range(8):
        oT = res.results[c]["outT"]
        out[c // 2, (c % 2) * LH:(c % 2 + 1) * LH] = oT.transpose(2, 1, 0).reshape(LH, D)
    return out
```

```python
from contextlib import ExitStack
import os
import numpy as np
import concourse.bass as bass
import concourse.mybir as mybir
from concourse.bass_utils import run_bass_kernel_spmd

F32 = mybir.dt.float32
BF16 = mybir.dt.bfloat16
ALU = mybir.AluOpType
AF = mybir.ActivationFunctionType

D = 1024
L = 4096
LH = 2048
LC = 256
T = LH + LC
NB = T // 128
PAIRS = [[0, 1], [2, 3], [4, 5], [6, 7]]
NOCC = False
DEPTH = 2
DFF = 2816
EPS = 1e-6
HTC = T + 6
HL = 259
HR = 260 + LH
NFM = 56
NTM = 1312


def hcol(i):
    return i + 2 if i < LC else i + 4


class Trk:
    ROT = 30000
    NDMA = 40

    def __init__(self, nc):
        self.nc = nc
        self.eng = {'pe': nc.tensor, 'act': nc.scalar, 'dve': nc.vector, 'pool': nc.gpsimd, 'sp': nc.sync}
        self.semh = []
        self.cur = {}
        self.cnt = {}
        for e in ('pe', 'act', 'dve', 'pool'):
            self.cur[e] = self._newsem(f"s_{e}")
            self.cnt[e] = 0
        self.dsem = [self._newsem(f"s_dma{i}") for i in range(self.NDMA)]
        self.dval = [0] * self.NDMA
        self.drr = 0
        self.known = {e: {} for e in self.eng}
        self.res = {}
        self.ninst = 0
        self.pesems = {self.cur['pe']}
        self.ccsem = None
        self.ccv = 0

    def _newsem(self, name):
        h = self.nc.alloc_semaphore(f"{name}_{len(self.semh)}")
        self.semh.append(h)
        return len(self.semh) - 1

    def _wait(self, e, tok):
        sid, val = tok
        if self.known[e].get(sid, 0) >= val:
            return
        self.eng[e].wait_ge(self.semh[sid], val)
        self.known[e][sid] = val

    def _deps(self, reads, writes):
        deps = {}

        def add(tok):
            if tok is None:
                return
            if deps.get(tok[0], 0) < tok[1]:
                deps[tok[0]] = tok[1]
        for r in reads:
            st = self.res.get(r)
            if st is not None:
                add(st[0])
        for w in writes:
            st = self.res.get(w)
            if st is not None:
                add(st[0])
                for s, v in st[1].items():
                    add((s, v))
        return deps

    def _commit(self, tok, reads, writes):
        for r in reads:
            st = self.res.setdefault(r, [None, {}])
            if st[1].get(tok[0], 0) < tok[1]:
                st[1][tok[0]] = tok[1]
        for w in writes:
            self.res[w] = [tok, {}]

    def op(self, e, fn, reads=(), writes=()):
        deps = self._deps(reads, writes)
        for s, v in deps.items():
            if e == 'pe' and s in self.pesems:
                continue
            self._wait(e, (s, v))
        inst = fn()
        if self.cnt[e] >= self.ROT:
            self.cur[e] = self._newsem(f"s_{e}")
            self.cnt[e] = 0
            if e == 'pe':
                self.pesems.add(self.cur[e])
        self.cnt[e] += 1
        tok = (self.cur[e], self.cnt[e])
        inst.then_inc(self.semh[tok[0]], 1)
        self._commit(tok, reads, writes)
        self.ninst += 1
        return tok

    def dma(self, out, in_, reads=(), writes=(), q='sp', slow=False):
        deps = self._deps(reads, writes)
        i = self.drr
        self.drr = (self.drr + 1) % self.NDMA
        if self.dval[i] > 0:
            deps[self.dsem[i]] = max(deps.get(self.dsem[i], 0), self.dval[i])
        for s, v in deps.items():
            self._wait(q, (s, v))
        if slow:
            inst = self.eng[q].dma_start(out=out, in_=in_, allow_slow_non_contiguous=True)
        else:
            inst = self.eng[q].dma_start(out=out, in_=in_)
        self.dval[i] += 16
        tok = (self.dsem[i], self.dval[i])
        inst.then_inc(self.semh[tok[0]], 16)
        self._commit(tok, reads, writes)
        self.ninst += 1
        return tok

    def cc(self, src, dst, reads=(), writes=()):
        if NOCC:
            return None
        deps = self._deps(reads, writes)
        for s_, v in deps.items():
            self._wait('pool', (s_, v))
        if self.ccsem is None:
            self.ccsem = self._newsem("s_cc")
            self.ccv = 0
        inst = self.nc.gpsimd.collective_compute("AllGather", ALU.bypass, replica_groups=PAIRS, ins=[src], outs=[dst])
        self.ccv += 1
        tok = (self.ccsem, self.ccv)
        inst.then_inc(self.semh[tok[0]])
        self._commit(tok, reads, writes)
        self.ninst += 1
        return tok

    def barrier(self):
        toks = {}
        for e in ('pe', 'act', 'dve', 'pool'):
            if self.cnt[e] > 0:
                toks[self.cur[e]] = self.cnt[e]
        for i in range(self.NDMA):
            if self.dval[i] > 0:
                toks[self.dsem[i]] = self.dval[i]
        if self.ccsem is not None and self.ccv > 0:
            toks[self.ccsem] = self.ccv
        for e in self.eng:
            for s, v in toks.items():
                self._wait(e, (s, v))
        self.res = {}

    def finish(self, toks):
        for t in toks:
            self._wait('sp', t)


class Prog:
    def __init__(self, dbg=(), nlayers=DEPTH, stop_after=None):
        self.dbg = set(dbg)
        self.nlayers = nlayers
        self.stop_after = stop_after
        nc = self.nc = bass.Bass("TRN2", target_bir_lowering=False)
        self.t = Trk(nc)
        self.outs = []
        self._uid = 0
        self.build()

    def din(self, name, shape, dt=F32):
        return self.nc.dram_tensor(name, list(shape), dt, kind="ExternalInput").ap()

    def dscr(self, name, shape, dt):
        if name in self.dbg:
            self.outs.append(name)
            return self.nc.dram_tensor(name, list(shape), dt, kind="ExternalOutput").ap()
        return self.nc.dram_tensor(name, list(shape), dt).ap()

    def sb(self, name, shape, dt):
        return self.nc.alloc_sbuf_tensor("sb_" + name, list(shape), dt).ap()

    def dump(self, name, ap, keys, dt=F32):
        if name in self.dbg:
            d = self.dscr(name, list(ap.shape), dt)
            self.dma(d, ap, keys, [('dump', name)])

    def uid(self):
        self._uid += 1
        return self._uid

    def mm(self, out, lhsT, rhs, start, stop, r, w):
        return self.t.op('pe', lambda: self.nc.tensor.matmul(out, lhsT=lhsT, rhs=rhs, start=start, stop=stop), r, w)

    def tr(self, out, in_, ident, r, w):
        return self.t.op('pe', lambda: self.nc.tensor.transpose(out, in_, ident), r, w)

    def act(self, out, in_, func, r, w, bias=None, scale=1.0, accum_out=None):
        kw = {}
        if bias is not None:
            kw['bias'] = bias
        if accum_out is not None:
            kw['accum_out'] = accum_out
        return self.t.op('act', lambda: self.nc.scalar.activation(out=out, in_=in_, func=func, scale=scale, **kw), r, w)

    def tt(self, e, out, in0, in1, op, r, w):
        eng = self.t.eng[e]
        return self.t.op(e, lambda: eng.tensor_tensor(out=out, in0=in0, in1=in1, op=op), r, w)

    def ts(self, e, out, in0, s1, op0, r, w, s2=None, op1=None):
        eng = self.t.eng[e]
        if op1 is None:
            return self.t.op(e, lambda: eng.tensor_scalar(out=out, in0=in0, scalar1=s1, scalar2=None, op0=op0), r, w)
        return self.t.op(e, lambda: eng.tensor_scalar(out=out, in0=in0, scalar1=s1, scalar2=s2, op0=op0, op1=op1), r, w)

    def stt(self, e, out, in0, scalar, in1, op0, op1, r, w):
        eng = self.t.eng[e]
        return self.t.op(e, lambda: eng.scalar_tensor_tensor(out=out, in0=in0, scalar=scalar, in1=in1, op0=op0, op1=op1), r, w)

    def cp(self, e, out, in_, r, w):
        eng = self.t.eng[e]
        if e == 'act':
            return self.t.op(e, lambda: eng.copy(out=out, in_=in_), r, w)
        return self.t.op(e, lambda: eng.tensor_copy(out=out, in_=in_), r, w)

    def recip(self, out, in_, r, w):
        return self.t.op('dve', lambda: self.nc.vector.reciprocal(out=out, in_=in_), r, w)

    def memset(self, e, ap, v, w):
        eng = self.t.eng[e]
        return self.t.op(e, lambda: eng.memset(ap, v), (), w)

    def dma(self, out, in_, r, w, slow=False, q='sp'):
        return self.t.dma(out, in_, r, w, q=q, slow=slow)

    def build(self):
        nc = self.nc
        self.xT0 = self.din("xT0", [128, 8, T])
        self.cvec = self.din("cvec", [128, 8, 2])
        self.xh0 = self.din("xh0", [128, 8, 2])
        self.hmask = self.din("hmask", [128, 2])
        self.wmod = self.din("wmod", [DEPTH, 48, 128, 8, 128])
        self.bmod = self.din("bmod", [DEPTH, 128, 48])
        self.nrm = self.din("nrm", [DEPTH, 128, 2, 8])
        self.wfm = self.din("wfm", [DEPTH, NFM, 128, 8, 128])
        self.wtm = self.din("wtm", [DEPTH, 128, 8, NTM])
        self.rope = self.din("rope", [2, 128, T])
        self.cgain = self.din("cgain", [DEPTH, 128, 4])
        self.cw = self.din("cw", [DEPTH, 128, 12, 4])
        self.dtb = self.din("dtb", [DEPTH, 128, 32])
        self.alog = self.din("alog", [DEPTH, 128, 32])
        self.dsk = self.din("dsk", [DEPTH, 128, 16])
        self.sng = self.din("sng", [DEPTH, 128, 1024])
        self.sink = self.din("sink", [DEPTH, 128, 8])
        self.woa = self.din("woa", [DEPTH, 128, 4, 1024])
        self.woc = self.din("woc", [DEPTH, 128, 4, 1024])
        self.wob = self.din("wob", [DEPTH, 128, 8, 1024])
        self.wout = self.din("wout", [DEPTH, 128, 8, 1024])
        self.wup = self.din("wup", [DEPTH, 22, 128, 8, 128])
        self.wgt = self.din("wgt", [DEPTH, 22, 128, 8, 128])
        self.fcw = self.din("fcw", [DEPTH, 128, 22, 4])
        self.wdn = self.din("wdn", [DEPTH, 128, 22, 1024])
        self.fng = self.din("fng", [128, 8])
        self.masks = self.din("masks", [6, 128, 128])
        self.ident_in = self.din("ident", [128, 128])
        self.outT = nc.dram_tensor("outT", [128, 8, LH], F32, kind="ExternalOutput").ap()

        self.xT = self.dscr("xT", [128, 8, T], F32)
        self.qaT = self.dscr("qaT", [128, 4, T], BF16)
        self.kaT = self.dscr("kaT", [128, T], BF16)
        self.qcT = self.dscr("qcT", [128, 4, T], BF16)
        self.kcT = self.dscr("kcT", [128, T], BF16)
        self.xbcT = self.dscr("xbcT", [128, 12, T], BF16)
        self.gtT = self.dscr("gtT", [128, 24, T], BF16)
        self.zs = self.dscr("zs", [T, 1024], F32)
        self.vv = self.dscr("vv", [T, 256], BF16)
        self.dts = self.dscr("dts", [T, 32], F32)
        self.yaT = self.dscr("yaT", [128, 4, T], BF16)
        self.ycT = self.dscr("ycT", [128, 4, T], BF16)
        self.ysT = self.dscr("ysT", [128, 8, T], BF16)
        self.hst = self.dscr("hst", [2, NB, 128, 1024], BF16)
        self.actT = self.dscr("actT", [128, 22, T], BF16)
        self.xhal = self.dscr("xhal", [128, 8, 2], F32)
        self.xb_src = self.dscr("xb_src", [128, 16], F32)
        self.xb_dst = self.dscr("xb_dst", [256, 16], F32)
        self.kaL = self.dscr("kaL", [128, LH], BF16)
        self.kcL = self.dscr("kcL", [128, LH], BF16)
        self.kaG = self.dscr("kaG", [256, LH], BF16)
        self.kcG = self.dscr("kcG", [256, LH], BF16)
        self.vvL = self.dscr("vvL", [LH, 256], BF16)
        self.vG = self.dscr("vG", [2 * LH, 256], BF16)
        self.s_src = self.dscr("s_src", [128, 2048], F32)
        self.s_dst = self.dscr("s_dst", [256, 2048], F32)

        self.ps = nc.alloc_psum_tensor("ps", [128, 8, 512], F32).ap()
        self.ident_f = self.sb("ident_f", [128, 128], F32)
        self.ident = self.sb("ident", [128, 128], BF16)
        self.ones_ms = self.sb("ones_ms", [128, 128], BF16)
        self.bd_ms = self.sb("bd_ms", [128, 128], BF16)
        self.ones_f = self.sb("ones_f", [128, 128], F32)
        self.msk_f = self.sb("msk_f", [128, 6, 128], F32)
        self.msk_b = self.sb("msk_b", [128, 6, 128], BF16)
        self.epsc = self.sb("epsc", [128, 1], F32)
        self.onec = self.sb("onec", [128, 1], F32)
        self.modT = self.sb("modT", [128, 48, 2], F32)
        self.gm = self.sb("gm", [128, 2, 8, 2], F32)
        self.hm = self.sb("hm", [128, 2], F32)

        self.consts()
        toks = []
        for l in range(self.nlayers):
            self.layer(l)
            if self.stop_after is not None and l == self.stop_after[0]:
                break
        if self.stop_after is None:
            toks = self.final_norm()
        self.t.barrier()

    def consts(self):
        self.dma(self.ident_f, self.ident_in, (), ['ident_f'])
        self.cp('dve', self.ident, self.ident_f, ['ident_f'], ['ident'])
        self.memset('dve', self.ones_ms, 1.0 / 1024, ['ones_ms'])
        self.memset('dve', self.bd_ms, 0.0, ['bd_ms'])
        self.memset('dve', self.bd_ms[0:64, 0:64], 1.0 / 128, ['bd_ms'])
        self.memset('dve', self.bd_ms[64:128, 64:128], 1.0 / 128, ['bd_ms'])
        self.memset('dve', self.ones_f, 1.0, ['ones_f'])
        self.memset('dve', self.epsc, EPS, ['epsc'])
        self.memset('dve', self.onec, 1.0, ['onec'])
        self.dma(self.msk_f, self.masks.rearrange("m p c -> p m c"), (), ['msk_f'])
        self.cp('dve', self.msk_b, self.msk_f, ['msk_f'], ['msk_b'])
        self.dma(self.hm, self.hmask, (), ['hm'])

    def layer(self, l):
        xsrc = self.xT0 if l == 0 else self.xT
        self.mod_phase(l)
        if self.stop_after == (l, 'mod'):
            return
        with self.nc.sbuf_tensor(f"hT{l}a", [128, 8, HTC], BF16) as hT_h:
            self.hT = hT_h.ap()
            self.memset('dve', self.hT, 0.0, [('hT', g_) for g_ in range(5)])
            self.norm_phase(l, 0, xsrc, self.xh0 if l == 0 else self.xhal)
            if self.stop_after == (l, 'n1'):
                return
            self.inproj_phase(l)
        if self.stop_after == (l, 'ip'):
            return
        self.attn_phase(l, 'A')
        self.attn_phase(l, 'C')
        if self.stop_after == (l, 'at'):
            return
        self.ssm_phase(l)
        if self.stop_after == (l, 'ss'):
            return
        self.merge_phase(l, xsrc)
        self.halo_exchange()
        if self.stop_after == (l, 'mg'):
            return
        with self.nc.sbuf_tensor(f"hT{l}b", [128, 8, HTC], BF16) as hT_h:
            self.hT = hT_h.ap()
            self.memset('dve', self.hT, 0.0, [('hT', g_) for g_ in range(5)])
            self.norm_phase(l, 1, self.xT, self.xhal)
            self.ffn_phase(l)
        if l < DEPTH - 1:
            self.halo_exchange()
        if self.stop_after == (l, 'ff'):
            return

    def mod_phase(self, l):
        nc = self.nc
        t = self.t
        with nc.sbuf_tensor(f"m_c{l}", [128, 8, 2], F32) as c_h, \
                nc.sbuf_tensor(f"m_sc{l}", [128, 8, 2], F32) as sc_h, \
                nc.sbuf_tensor(f"m_w{l}", [128, 4, 8, 128], F32) as w_h, \
                nc.sbuf_tensor(f"m_wb{l}", [128, 2, 8, 128], BF16) as wb_h, \
                nc.sbuf_tensor(f"m_scb{l}", [128, 8, 2], BF16) as scb_h, \
                nc.sbuf_tensor(f"m_b{l}", [128, 48], F32) as b_h, \
                nc.sbuf_tensor(f"m_n{l}", [128, 2, 8], F32) as n_h:
            c_sb, sc_sb, w_sb, b_sb, n_sb = c_h.ap(), sc_h.ap(), w_h.ap(), b_h.ap(), n_h.ap()
            self.dma(c_sb, self.cvec, (), ['m_c'])
            self.dma(b_sb, self.bmod[l], (), ['m_b'])
            self.dma(n_sb, self.nrm[l], (), ['m_n'])
            self.act(sc_sb, c_sb, AF.Silu, ['m_c'], ['m_sc'])
            wb_sb, scb = wb_h.ap(), scb_h.ap()
            self.cp('dve', scb, sc_sb, ['m_sc'], ['m_scb'])
            for j in range(3):
                self.dma(w_sb[:, j % 4], self.wmod[l, j], (), [('m_w', j % 4)])
            for j in range(48):
                s = j % 4
                if j + 3 < 48:
                    self.dma(w_sb[:, (j + 3) % 4], self.wmod[l, j + 3], (), [('m_w', (j + 3) % 4)])
                self.cp('dve', wb_sb[:, j % 2], w_sb[:, s], [('m_w', s)], [('m_wb', j % 2)])
                pt = self.ps[:, j % 4, 0:2]
                for kc in range(8):
                    self.mm(pt, wb_sb[:, j % 2, kc, :], scb[:, kc, :], kc == 0, kc == 7,
                            [('m_wb', j % 2), 'm_scb'], [('ps', j % 4)])
                self.ts('dve', self.modT[:, j, :], pt, b_sb[:, j:j + 1], ALU.add, [('ps', j % 4), 'm_b'], [('modT', j)])
            for n in range(2):
                sc_off = 8 + 24 * n
                for kc in range(8):
                    self.ts('dve', self.gm[:, n, kc, :], self.modT[:, sc_off + kc, :], 1.0, ALU.add, [('modT', sc_off + kc)], ['gm'])
                    self.ts('dve', self.gm[:, n, kc, :], self.gm[:, n, kc, :], n_sb[:, n, kc:kc + 1], ALU.mult,
                            ['gm', 'm_n'], ['gm'])
            if 'modT' in self.dbg:
                d = self.dscr("modT", [128, 48, 2], F32)
                self.dma(d, self.modT, ['modT'], ['d_modT'])
            t.barrier()

    @staticmethod
    def groups():
        g = [(0, LC)]
        for i in range(LH // 512):
            g.append((LC + 512 * i, 512))
        return g

    @staticmethod
    def windows():
        return [(256 * w, 256) for w in range(T // 256)]

    def norm_phase(self, l, n, xsrc, xhsrc):
        nc = self.nc
        sh_off = 0 if n == 0 else 24
        with nc.sbuf_tensor(f"n_x{l}{n}", [128, 2, 8, 512], F32) as x_h, \
                nc.sbuf_tensor(f"n_sq{l}{n}", [128, 2, 8, 512], BF16) as sq_h, \
                nc.sbuf_tensor(f"n_r{l}{n}", [128, 2, 512], F32) as r_h, \
                nc.sbuf_tensor(f"n_t{l}{n}", [128, 2, 512], F32) as t_h:
            x_sb, sq_sb, r_sb, t_sb = x_h.ap(), sq_h.ap(), r_h.ap(), t_h.ap()
            glist = list(enumerate(self.groups())) + [(99, (None, 2))]
            for gi, (t0, n_) in glist:
                halo = gi == 99
                s = gi % 2
                cls = 1 if (not halo and t0 < LC) else 0
                xg = x_sb[:, s, :, 0:n_]
                if halo:
                    self.dma(xg, xhsrc, ['xhal'], [('n_x', s)])
                else:
                    self.dma(xg, xsrc[:, :, t0:t0 + n_], [('xT', gi)], [('n_x', s)])
                self.act(sq_sb[:, s, :, 0:n_], xg, AF.Square, [('n_x', s)], [('n_sq', s)])
                pt = self.ps[:, s, 0:n_]
                for kc in range(8):
                    self.mm(pt, self.ones_ms, sq_sb[:, s, kc, 0:n_], kc == 0, kc == 7,
                            [('n_sq', s), 'ones_ms'], [('ps', s)])
                rr = r_sb[:, s, 0:n_]
                self.act(rr, pt, AF.Ln, [('ps', s), 'epsc'], [('n_r', s)], bias=self.epsc)
                self.act(rr, rr, AF.Exp, [('n_r', s)], [('n_r', s)], scale=-0.5)
                c0 = hcol(t0) if not halo else None
                for kc in range(8):
                    ts_ = kc % 2
                    tmp = t_sb[:, ts_, 0:n_]
                    self.stt('dve', tmp, xg[:, kc, :], self.gm[:, n, kc, cls:cls + 1], rr, ALU.mult, ALU.mult,
                             [('n_x', s), 'gm', ('n_r', s)], [('n_t', ts_)])
                    if halo:
                        self.stt('dve', tmp, tmp, self.modT[:, sh_off + kc, 0:1], self.hm, ALU.add, ALU.mult,
                                 [('n_t', ts_), 'modT', 'hm'], [('n_t', ts_)])
                        self.cp('dve', self.hT[:, kc, HL:HL + 1], tmp[:, 0:1], [('n_t', ts_)], [('hT', 1)])
                        self.cp('dve', self.hT[:, kc, HR:HR + 1], tmp[:, 1:2], [('n_t', ts_)], [('hT', 4)])
                        continue
                    self.act(self.hT[:, kc, c0:c0 + n_], tmp, AF.Identity, [('n_t', ts_), 'modT'], [('hT', gi)],
                             bias=self.modT[:, sh_off + kc, cls:cls + 1])
            if 'hT' in self.dbg:
                d = self.dscr("hT", [128, 8, HTC], BF16)
                self.dma(d, self.hT, [('hT', g_) for g_ in range(5)], ['d_hT'])
            self.t.barrier()

    def halo_exchange(self):
        xk = [('xT', gi) for gi in range(5)]
        self.dma(self.xb_src[:, 0:8], self.xT[:, :, LC:LC + 1].rearrange("p k o -> p (k o)"), xk, ['xb_src'], slow=True)
        self.dma(self.xb_src[:, 8:16], self.xT[:, :, T - 1:T].rearrange("p k o -> p (k o)"), xk, ['xb_src'], slow=True)
        self.t.cc(self.xb_src, self.xb_dst, ['xb_src'], ['xb_dst'])
        self.dma(self.xhal[:, :, 0:1].rearrange("p k o -> p (k o)"), self.xb_dst[0:128, 8:16], ['xb_dst'], ['xhal'], slow=True, q='pool')
        self.dma(self.xhal[:, :, 1:2].rearrange("p k o -> p (k o)"), self.xb_dst[128:256, 0:8], ['xb_dst'], ['xhal'], slow=True, q='pool')

    def hkeys(self, c0, c1):
        ks = []
        for gi, (t0, n_) in enumerate(self.groups()):
            a = hcol(t0) - 2
            b = hcol(t0) + n_ + 2
            if c0 < b and c1 > a:
                ks.append(('hT', gi))
        return ks

    def inproj_phase(self, l):
        nc = self.nc
        with nc.sbuf_tensor(f"j_wf{l}", [128, 2, NTM], F32) as wf_h, nc.sbuf_tensor(f"j_wb{l}", [128, 8, NTM], BF16) as wbt_h:
            self._wtf, self._wtb = wf_h.ap(), wbt_h.ap()
            self._inproj(l)

    def _inproj(self, l):
        nc = self.nc
        with nc.sbuf_tensor(f"i_ws{l}", [128, 4, 8, 128], F32) as ws_h, \
                nc.sbuf_tensor(f"i_wb{l}", [128, 4, 8, 128], BF16) as wb_h, \
                nc.sbuf_tensor(f"i_rp{l}", [128, 2, T], F32) as rp_h, \
                nc.sbuf_tensor(f"i_or{l}", [128, 2, T], BF16) as or_h, \
                nc.sbuf_tensor(f"i_cg{l}", [128, 4], F32) as cg_h, \
                nc.sbuf_tensor(f"i_cw{l}", [128, 12, 4], F32) as cw_h, \
                nc.sbuf_tensor(f"i_t1{l}", [128, 2, 512], F32) as t1_h, \
                nc.sbuf_tensor(f"i_t2{l}", [128, 2, 512], F32) as t2_h, \
                nc.sbuf_tensor(f"i_sq{l}", [128, 2, 2, 512], BF16) as sq_h, \
                nc.sbuf_tensor(f"i_xa{l}", [128, 4, 256], F32) as xa_h, \
                nc.sbuf_tensor(f"i_rs{l}", [128, 2, 512], F32) as rs_h:
            ws, wb, rp, orow = ws_h.ap(), wb_h.ap(), rp_h.ap(), or_h.ap()
            wtf = self._wtf
            wtb = self._wtb
            cg, cw, t1, t2, sq, rs = cg_h.ap(), cw_h.ap(), t1_h.ap(), t2_h.ap(), sq_h.ap(), rs_h.ap()
            xacc = xa_h.ap()
            self.dma(rp, self.rope.rearrange("a p t -> p a t"), (), ['i_rp'])
            self.dma(cg, self.cgain[l], (), ['i_cg'])
            self.dma(cw, self.cw[l], (), ['i_cw'])
            loaded = set()

            def load(ti):
                if ti >= NFM or ti in loaded:
                    return
                loaded.add(ti)
                sl = ti % 4
                self.dma(ws[:, sl], self.wfm[l, ti], (), [('i_ws', sl)])
                self.cp('pool', wb[:, sl], ws[:, sl], [('i_ws', sl)], [('i_wb', sl)])

            load(0); load(1)
            osl = 0
            dests = [self.qaT[:, m, :] for m in range(4)] + [self.kaT] + [self.qcT[:, m, :] for m in range(4)] + [self.kcT]
            for pr in range(10):
                tA, tB = 2 * pr, 2 * pr + 1
                load(tA + 2); load(tB + 2)
                isC = pr >= 5
                gq = 0 if pr < 9 else 2
                for gi, (t0, n_) in enumerate(self.groups()):
                    s = gi % 2
                    bA, bB, bM = 2 * s, 2 * s + 1, 4 + s
                    c0 = hcol(t0)
                    hk = [('hT', gi)]
                    pA, pB = self.ps[:, bA, 0:n_], self.ps[:, bB, 0:n_]
                    for kc in range(8):
                        self.mm(pA, wb[:, tA % 4, kc, :], self.hT[:, kc, c0:c0 + n_], kc == 0, kc == 7,
                                hk + [('i_wb', tA % 4)], [('ps', bA)])
                    for kc in range(8):
                        self.mm(pB, wb[:, tB % 4, kc, :], self.hT[:, kc, c0:c0 + n_], kc == 0, kc == 7,
                                hk + [('i_wb', tB % 4)], [('ps', bB)])
                    T1, T2 = rp[:, 0, t0:t0 + n_], rp[:, 1, t0:t0 + n_]
                    a1, a2 = t1[:, s, 0:n_], t2[:, s, 0:n_]
                    oo = orow[:, osl, t0:t0 + n_]
                    if not isC:
                        self.tt('dve', a1, pA, T1, ALU.mult, [('ps', bA), 'i_rp'], [('i_t1', s)])
                        self.tt('dve', a2, pB, T2, ALU.mult, [('ps', bB), 'i_rp'], [('i_t2', s)])
                        self.tt('pool', oo, a1, a2, ALU.add, [('i_t1', s), ('i_t2', s)], [('i_or', osl)])
                    else:
                        self.act(sq[:, s, 0, 0:n_], pA, AF.Square, [('ps', bA)], [('i_sq', s, 0)])
                        self.act(sq[:, s, 1, 0:n_], pB, AF.Square, [('ps', bB)], [('i_sq', s, 1)])
                        self.stt('dve', a1, pA, cg[:, gq:gq + 1], T1, ALU.mult, ALU.mult,
                                 [('ps', bA), 'i_rp', 'i_cg', ('i_sq', s, 0)], [('i_t1', s)])
                        self.stt('dve', a2, pB, cg[:, gq + 1:gq + 2], T2, ALU.mult, ALU.mult,
                                 [('ps', bB), 'i_rp', 'i_cg', ('i_sq', s, 1)], [('i_t2', s)])
                        pM = self.ps[:, bM, 0:n_]
                        self.mm(pM, self.bd_ms, sq[:, s, 0, 0:n_], True, False, [('i_sq', s, 0), 'bd_ms'], [('ps', bM)])
                        self.mm(pM, self.bd_ms, sq[:, s, 1, 0:n_], False, True, [('i_sq', s, 1), 'bd_ms'], [('ps', bM)])
                        rr = rs[:, s, 0:n_]
                        self.act(rr, pM, AF.Sqrt, [('ps', bM), 'epsc'], [('i_rs', s)], bias=self.epsc)
                        self.recip(rr, rr, [('i_rs', s)], [('i_rs', s)])
                        self.tt('pool', a1, a1, a2, ALU.add, [('i_t1', s), ('i_t2', s)], [('i_t1', s)])
                        self.tt('pool', oo, a1, rr, ALU.mult, [('i_t1', s), ('i_rs', s)], [('i_or', osl)])
                self.dma(dests[pr], orow[:, osl, :], [('i_or', osl)], [('dst_rope', pr)])
                if pr == 4:
                    self.dma(self.kaL, orow[:, osl, LC:T], [('i_or', osl)], ['kaL'])
                if pr == 9:
                    self.dma(self.kcL, orow[:, osl, LC:T], [('i_or', osl)], ['kcL'])
                osl ^= 1
            for j in range(12):
                ti = 20 + j
                load(ti + 1); load(ti + 2)
                pend = None
                for wi, (t0, n_) in enumerate(self.windows()):
                    s = wi % 4
                    bk = s
                    c0 = hcol(t0) - 1
                    hk = self.hkeys(c0, c0 + 258)
                    pt = self.ps[:, bk, 0:258]
                    for kc in range(8):
                        self.mm(pt, wb[:, ti % 4, kc, :], self.hT[:, kc, c0:c0 + 258], kc == 0, kc == 7,
                                hk + [('i_wb', ti % 4)], [('ps', bk)])
                    acc = xacc[:, s, :]
                    self.act(acc, pt[:, 0:256], AF.Identity, [('ps', bk), 'i_cw'], [('i_xa', s)], scale=cw[:, j, 0:1])
                    self.stt('dve', acc, pt[:, 1:257], cw[:, j, 1:2], acc, ALU.mult, ALU.add,
                             [('ps', bk), 'i_cw', ('i_xa', s)], [('i_xa', s)])
                    self.stt('dve', acc, pt[:, 2:258], cw[:, j, 2:3], acc, ALU.mult, ALU.add,
                             [('ps', bk), 'i_cw', ('i_xa', s)], [('i_xa', s)])
                    if pend is not None:
                        pend()
                    pend = (lambda acc=acc, s=s, t0=t0, osl=osl, j=j: self.act(
                        orow[:, osl, t0:t0 + 256], acc, AF.Silu, [('i_xa', s), 'i_cw'], [('i_or', osl)], bias=cw[:, j, 3:4]))
                pend()
                pend = None
                self.dma(self.xbcT[:, j, :], orow[:, osl, :], [('i_or', osl)], [('xbcT', j)])
                osl ^= 1
                if j == 1:
                    self.t.cc(self.kaL, self.kaG, ['kaL'], ['kaG'])
                    self.t.cc(self.kcL, self.kcG, ['kcL'], ['kcG'])
            for j in range(24):
                ti = 32 + j
                load(ti + 1); load(ti + 2)
                if j < 8:
                    self.dma(wtf[:, j % 2], self.wtm[l, :, j, :], (), [('j_wf', j % 2)])
                    self.cp('pool', wtb[:, j, :], wtf[:, j % 2], [('j_wf', j % 2)], [('j_wb', j)])
                for gi, (t0, n_) in enumerate(self.groups()):
                    bk = (j * 5 + gi) % 6
                    c0 = hcol(t0)
                    pt = self.ps[:, bk, 0:n_]
                    for kc in range(8):
                        self.mm(pt, wb[:, ti % 4, kc, :], self.hT[:, kc, c0:c0 + n_], kc == 0, kc == 7,
                                [('hT', gi), ('i_wb', ti % 4)], [('ps', bk)])
                    self.act(orow[:, osl, t0:t0 + n_], pt, AF.Sigmoid, [('ps', bk)], [('i_or', osl)])
                self.dma(self.gtT[:, j, :], orow[:, osl, :], [('i_or', osl)], [('gtT', j)])
                osl ^= 1
            self.t.barrier()
        with nc.sbuf_tensor(f"j_z{l}", [128, 2, 1024], F32) as z_h, \
                nc.sbuf_tensor(f"j_v{l}", [128, 2, 256], BF16) as v_h, \
                nc.sbuf_tensor(f"j_d{l}", [128, NB, 32], F32) as d_h, \
                nc.sbuf_tensor(f"j_db{l}", [128, 32], F32) as db_h:
            wb, zst, vst, dst, dbt = self._wtb, z_h.ap(), v_h.ap(), d_h.ap(), db_h.ap()
            self.dma(dbt, self.dtb[l], (), ['j_db'])
            wk = []
            for tb, cgps in [(tb_, (2,)) for tb_ in range(NB)] + [(-1, ())] + [(tb_, (0, 1)) for tb_ in range(NB)]:
                if tb < 0:
                    self.t.cc(self.vvL, self.vG, [('vvL', t_) for t_ in range(2, NB)], ['vG'])
                    continue
                s = tb % 2
                c0 = hcol(128 * tb)
                hk = self.hkeys(c0, c0 + 128)
                for cgp in cgps:
                    bk = (tb * 3 + cgp) % 4
                    n_ = 512 if cgp < 2 else 256
                    pt = self.ps[:, bk, 0:n_]
                    for kc in range(8):
                        self.mm(pt, self.hT[:, kc, c0:c0 + 128], wb[:, kc, 512 * cgp:512 * cgp + n_], kc == 0, kc == 7,
                                hk + wk, [('ps', bk)])
                    if cgp < 2:
                        self.act(zst[:, s, 512 * cgp:512 * cgp + 512], pt, AF.Silu, [('ps', bk)], [('j_z', s, cgp)])
                    else:
                        self.cp('dve', vst[:, s, :], pt, [('ps', bk)], [('j_v', s)])
                if 0 in cgps:
                    self.dma(self.zs[128 * tb:128 * tb + 128, :], zst[:, s, :], [('j_z', s, 0), ('j_z', s, 1)], [('zs', tb)])
                else:
                    self.dma(self.vv[128 * tb:128 * tb + 128, :], vst[:, s, :], [('j_v', s)], [('vv', tb)])
                    if tb >= 2:
                        self.dma(self.vvL[128 * (tb - 2):128 * (tb - 2) + 128, :], vst[:, s, :], [('j_v', s)], [('vvL', tb)])
            for tb in range(NB):
                bk = 4 + tb % 2
                c0 = hcol(128 * tb)
                hk = self.hkeys(c0, c0 + 128)
                pt = self.ps[:, bk, 0:32]
                for kc in range(8):
                    self.mm(pt, self.hT[:, kc, c0:c0 + 128], wb[:, kc, 1280:1312], kc == 0, kc == 7, hk + wk, [('ps', bk)])
                self.tt('dve', dst[:, tb, :], pt, dbt, ALU.add, [('ps', bk), 'j_db'], [('j_d', tb)])
            dk = [('j_d', tb) for tb in range(NB)]
            self.act(dst, dst, AF.Exp, dk, dk)
            self.act(dst, dst, AF.Ln, dk + ['onec'], dk, bias=self.onec)
            self.dma(self.dts.rearrange("(b p) c -> p b c", p=128), dst, dk, ['dts'])
            self.t.barrier()

    def attn_phase(self, l, which):
        nc = self.nc
        isA = which == 'A'
        do_ctx = l < DEPTH - 1
        qsrc, ksrc, ydst = (self.qaT, self.kaT, self.yaT) if isA else (self.qcT, self.kcT, self.ycT)
        voff = 0 if isA else 128
        nm = f"{which}{l}"
        LOOK = 3
        NS, NP = 4, 6
        with ExitStack() as stk:
            al = lambda name, shape, dt: stk.enter_context(nc.sbuf_tensor(f"{name}{nm}", shape, dt)).ap()
            if isA:
                NKB = 20
            else:
                NKB = 34
            kz = al("a_k", [128, 2, 128 * NKB], BF16)
            Q = al("a_q", [128, 4, T], BF16)
            vx = al("a_v", [128, NKB, 2, 128], BF16)
            P = al("a_p", [128, NP, 512], BF16)
            dsum = al("a_d", [128, 2, 512], F32)
            rden = al("a_r", [64, 2, 512], F32)
            lnd = al("a_l", [128, 2, 512], F32)
            yst = al("a_y", [64, 2, 512], BF16)
            sk = al("a_s", [128, 8], F32)
            es = al("a_e", [128, 2, 512], F32)
            mx = al("a_m", [128, 4, 128], BF16)
            self.memset('pool', kz[64:128, 0, :], 0.0, [('a_k', 0)])
            self.memset('pool', kz[0:64, 1, :], 0.0, [('a_k', 1)])
            self.memset('pool', vx[:, :, :, 64:128], 1.0, ['a_v1'])
            vsrc = lambda t_, a, b: t_[a:b, :].rearrange("(b p) c -> p b c", p=128)
            for g_ in range(2):
                r0, r1 = 64 * g_, 64 * g_ + 64
                vc = slice(voff + 64 * g_, voff + 64 * g_ + 64)
                self.dma(kz[r0:r1, g_, 0:LC], ksrc[r0:r1, 0:LC], (), [('a_k', g_)])
                self.dma(vx[:, 0:2, g_, 0:64], vsrc(self.vv, 0, LC)[:, :, vc], (), [('a_v', g_)])
                if isA:
                    self.dma(kz[r0:r1, g_, 256:384], self.kaG[r0:r1, LH - 128:LH], (), [('a_k', g_)])
                    self.dma(kz[r0:r1, g_, 384:384 + LH], ksrc[r0:r1, LC:T], (), [('a_k', g_)])
                    self.dma(kz[r0:r1, g_, 384 + LH:512 + LH], self.kaG[128 + r0:128 + r1, 0:128], (), [('a_k', g_)])
                    self.dma(vx[:, 2:3, g_, 0:64], vsrc(self.vG, LH - 128, LH)[:, :, vc], (), [('a_v', g_)])
                    self.dma(vx[:, 3:19, g_, 0:64], vsrc(self.vvL, 0, LH)[:, :, vc], (), [('a_v', g_)])
                    self.dma(vx[:, 19:20, g_, 0:64], vsrc(self.vG, LH, LH + 128)[:, :, vc], (), [('a_v', g_)])
                else:
                    self.dma(kz[r0:r1, g_, LC:LC + 2 * LH].rearrange("p (r t) -> p r t", r=2),
                             self.kcG.rearrange("(r p) t -> p r t", r=2)[r0:r1], (), [('a_k', g_)])
                    self.dma(vx[:, 2:34, g_, 0:64], vsrc(self.vG, 0, 2 * LH)[:, :, vc], (), [('a_v', g_)])
            self.dma(Q, qsrc, (), ['a_q'])
            self.cp('dve', mx[:, 0, :], self.msk_b[:, 2, :], ['msk_b'], [('a_m', 0)])
            self.cp('dve', mx[:, 1, :], self.msk_b[:, 0, :], ['msk_b'], [('a_m', 1)])
            self.ts('dve', mx[:, 2, :], self.msk_b[:, 2, :], self.hm[:, 0:1], ALU.mult, ['msk_b', 'hm'], [('a_m', 2)])
            self.ts('dve', mx[:, 3, :], self.msk_b[:, 0, :], self.hm[:, 1:2], ALU.mult, ['msk_b', 'hm'], [('a_m', 3)])
            if isA:
                self.dma(sk, self.sink[l], (), ['a_s'])
                self.act(sk, sk, AF.Exp, ['a_s'], ['a_s'])
                for kvh in range(2):
                    self.cp('dve', es[:, kvh, :].rearrange("p (a b) -> p a b", a=4),
                            sk[:, 4 * kvh:4 * kvh + 4].unsqueeze(2).to_broadcast([128, 4, 128]), ['a_s'], [('a_e', kvh)])
            steps = []
            for qb in range(NB):
                if qb < 2:
                    if not do_ctx:
                        continue
                    kbs = [(0, None), (1, None)]
                elif isA:
                    n = qb - 2
                    kbs = [(n + 2, 2 if n == 0 else 0), (n + 3, None), (n + 4, 3 if n == NB - 3 else 1),
                           (0, None), (1, None)]
                else:
                    kbs = [(kb, None) for kb in range(NKB)]
                for i, (kb, mk) in enumerate(kbs):
                    for kvh in range(2):
                        steps.append(dict(qb=qb, kb=kb, kvh=kvh, mk=mk, first=(i == 0), last=(i == len(kbs) - 1)))
            qseq = {}
            for st in steps:
                qseq.setdefault(st['qb'], len(qseq))
            for i, st in enumerate(steps):
                st['sb'] = i % NS
                st['pb'] = i % NP
                st['ob'] = 4 + 2 * (qseq[st['qb']] % 2) + st['kvh']

            def emit_S(st):
                kvh, qb, kb = st['kvh'], st['qb'], st['kb']
                out = self.ps[:, st['sb'], :].rearrange("p (a b) -> p a b", a=4)
                self.mm(out, kz[:, kvh, 128 * kb:128 * kb + 128], Q[:, :, 128 * qb:128 * qb + 128],
                        True, True, [('a_k', kvh), 'a_q'], [('ps', st['sb'])])
                pp = P[:, st['pb'], :]
                self.act(pp, self.ps[:, st['sb'], :], AF.Exp, [('ps', st['sb'])], [('a_p', st['pb'])], scale=0.125)
                if st['mk'] is not None:
                    self.tt('pool', pp.rearrange("p (a b) -> p a b", a=4), pp.rearrange("p (a b) -> p a b", a=4),
                            mx[:, st['mk'], :].unsqueeze(1).to_broadcast([128, 4, 128]), ALU.mult,
                            [('a_p', st['pb']), ('a_m', st['mk'])], [('a_p', st['pb'])])

            def emit_PV(st):
                kvh, qb, kb, ob = st['kvh'], st['qb'], st['kb'], st['ob']
                self.mm(self.ps[:, ob, :], vx[:, kb, kvh, :], P[:, st['pb'], :], st['first'], st['last'],
                        [('a_p', st['pb']), ('a_v', kvh), 'a_v1'], [('ps', ob)])
                if not st['last']:
                    return
                sl = kvh
                if isA:
                    self.tt('dve', dsum[64:128, sl, :], self.ps[64:128, ob, :], es[64:128, kvh, :], ALU.add,
                            [('ps', ob), ('a_e', kvh)], [('a_d', sl)])
                    self.act(lnd[64:128, sl, :], dsum[64:128, sl, :], AF.Ln, [('a_d', sl)], [('a_l', sl)])
                    self.act(rden[:, sl, :], lnd[64:128, sl, :], AF.Exp, [('a_l', sl)], [('a_r', sl)], scale=-1.0)
                else:
                    self.recip(rden[:, sl, :], self.ps[64:128, ob, :], [('ps', ob)], [('a_r', sl)])
                self.tt('dve', yst[:, sl, :], self.ps[0:64, ob, :], rden[:, sl, :], ALU.mult, [('ps', ob), ('a_r', sl)], [('a_y', sl)])
                ysv = yst[:, sl, :].rearrange("p (a b q) -> p a b q", a=2, b=2)
                for par in range(2):
                    self.dma(ydst[64 * par:64 * par + 64, 2 * kvh:2 * kvh + 2, 128 * qb:128 * qb + 128], ysv[:, :, par, :],
                             [('a_y', sl)], [('yT' + which, qb, kvh, par)])

            for i in range(min(LOOK, len(steps))):
                emit_S(steps[i])
            for i, st in enumerate(steps):
                if i + LOOK < len(steps):
                    emit_S(steps[i + LOOK])
                emit_PV(st)
            self.t.barrier()

    def ssm_phase(self, l):
        nc = self.nc
        do_ctx = l < DEPTH - 1
        sbs = self.dscr(f"sbs{l}", [2, NB, 128, 1024], F32)
        with ExitStack() as stk:
            al = lambda name, shape, dt: stk.enter_context(nc.sbuf_tensor(f"{name}{l}", shape, dt)).ap()
            BT = al("s_bt", [128, 2, T], BF16)
            CT = al("s_ct", [128, 2, T], BF16)
            dt_all = al("s_dt", [128, NB, 32], F32)
            da_all = al("s_da", [128, NB, 32], F32)
            E = al("s_E", [128, NB, 96], F32)
            acf = al("s_ac", [128, 32], F32)
            dsk = al("s_dk", [128, 16], F32)
            gn = al("s_gn", [128, 1024], F32)
            xf = al("s_xf", [128, 2, 8, 128], BF16)
            xf3 = al("s_xf3", [128, 3, 8, 128], BF16)
            xt = al("s_xt", [128, 2, 1024], BF16)
            bm = al("s_bm", [128, 2, 256], BF16)
            w = al("s_w", [128, 2, 32], F32)
            xw = al("s_xw", [128, 2, 2, 1024], BF16)
            H = al("s_H", [128, 2, 1024], F32)
            hb = al("s_hb", [128, 2, 2, 1024], BF16)
            sbt = al("s_sb", [128, 2, 1024], F32)
            R = al("s_R", [128, 2, 1024], F32)
            Lx = al("s_L", [128, 2, 1024], F32)
            M = al("s_M", [128, 4, 1024], BF16)
            ytmp = al("s_yt", [128, 1024], F32)
            ss = al("s_ss", [128, 2], F32)
            yn = al("s_yn", [128, 1024], BF16)
            yst = al("s_ys", [128, 2, 8, 128], BF16)
            xk = [('xbcT', j) for j in range(12)]
            self.dma(BT, self.xbcT[:, 8:10, :], xk, ['s_bt'])
            self.dma(CT, self.xbcT[:, 10:12, :], xk, ['s_ct'])
            self.dma(dt_all, self.dts.rearrange("(b p) c -> p b c", p=128), ['dts'], ['s_dt'])
            self.dma(acf, self.alog[l], (), ['s_ac'])
            self.dma(dsk, self.dsk[l], (), ['s_dk'])
            self.dma(gn, self.sng[l], (), ['s_gn'])
            self.act(acf, acf, AF.Exp, ['s_ac'], ['s_ac'])
            self.ts('dve', acf, acf, -1.0, ALU.mult, ['s_ac'], ['s_ac'])
            self.tt('dve', da_all, dt_all, acf.unsqueeze(1).to_broadcast([128, NB, 32]), ALU.mult, ['s_dt', 's_ac'], ['s_da'])
            self.memset('dve', H, 0.0, [('s_H', 0), ('s_H', 1)])
            psb = lambda b: self.ps[:, b, :].bitcast(BF16)

            def b16(ap):
                return ap.rearrange("p (h d) -> p h d", h=16)

            def bc16(ap):
                return ap.unsqueeze(2).to_broadcast([128, 16, 64])

            def load_xs(c, slot):
                self.dma(xf[:, slot], self.xbcT[:, 0:8, 128 * c:128 * c + 128], xk, [('s_xf', slot)])
                pt = psb(7)
                for f in range(8):
                    self.tr(pt[:, 128 * f:128 * f + 128], xf[:, slot, f, :], self.ident, [('s_xf', slot), 'ident'], [('ps', 7)])
                self.cp('act', xt[:, slot, :], pt, [('ps', 7)], [('s_xt', slot)])

            def p1_load(c):
                if c < NB:
                    self.dma(xf3[:, c % 3], self.xbcT[:, 0:8, 128 * c:128 * c + 128], xk, [('s_xf3', c % 3)])

            def p1_a(c):
                s = c % 2
                pt7 = psb(7)
                for f in range(8):
                    self.tr(pt7[:, 128 * f:128 * f + 128], xf3[:, c % 3, f, :], self.ident, [('s_xf3', c % 3), 'ident'], [('ps', 7)])
                self.cp('act', xt[:, s, :], pt7, [('ps', 7)], [('s_xt', s)])
                pt = psb(0)
                for g in range(2):
                    self.tr(pt[:, 128 * g:128 * g + 128], BT[:, g, 128 * c:128 * c + 128], self.ident, ['s_bt', 'ident'], [('ps', 0)])
                self.cp('dve', bm[:, s, :], pt[:, 0:256], [('ps', 0)], [('s_bm', s)])
                pc = self.ps[:, 1, :]
                for (c0, mi, d0, dn) in ((0, 0, 0, 16), (16, 1, 0, 16), (32, 2, 16, 16), (48, 3, 16, 16), (64, 4, 0, 32)):
                    self.mm(pc[:, c0:c0 + dn], self.msk_f[:, mi, :], da_all[:, c, d0:d0 + dn], True, True,
                            ['s_da', 'msk_f'], [('ps', 1)])
                self.act(E[:, c, :], pc[:, 0:96], AF.Exp, [('ps', 1)], [('s_E', c)])
                self.tt('dve', w[:, s, :].rearrange("p (a b) -> p a b", a=2), dt_all[:, c, :].rearrange("p (a b) -> p a b", a=2),
                        E[:, c, 16:80].rearrange("p (a b) -> p a b", a=2)[:, :, 0:16], ALU.mult, ['s_dt', ('s_E', c)], [('s_w', s)])
                for d in range(2):
                    self.tt('dve' if d == 0 else 'pool', b16(xw[:, s, d, :]), b16(xt[:, s, :]), bc16(w[:, s, 16 * d:16 * d + 16]), ALU.mult,
                            [('s_xt', s), ('s_w', s)], [('s_xw', s, d)])

            def p1_b(c):
                s = c % 2
                for d in range(2):
                    for g in range(2):
                        bk = 2 + 2 * d + g
                        self.mm(self.ps[:, bk, :], bm[:, s, 128 * g:128 * g + 128], xw[:, s, d, 512 * g:512 * g + 512], True, True,
                                [('s_bm', s), ('s_xw', s, d)], [('ps', bk)])
                for d in range(2):
                    self.cp('act', sbt[:, d, :], self.ps[:, 2 + 2 * d:4 + 2 * d, :].rearrange("p a b -> p (a b)"),
                            [('ps', 2 + 2 * d), ('ps', 3 + 2 * d)], [('s_sb', d)])
                    self.dma(sbs[d, c], sbt[:, d, :], [('s_sb', d)], [('sbs', d, c)])

            p1_load(0)
            p1_load(1)
            p1_a(0)
            for c in range(NB):
                p1_load(c + 2)
                if c + 1 < NB:
                    p1_a(c + 1)
                p1_b(c)
            self.dump("dbg_E", E, [('s_E', c_) for c_ in range(NB)])
            rstk = ExitStack()
            alr = lambda name, shape, dt: rstk.enter_context(nc.sbuf_tensor(f"{name}{l}", shape, dt)).ap()
            Hc = alr("s_Hc", [128, 2, 1024], F32)
            Gx = alr("s_Gx", [128, 2, 1024], F32)
            fwd_lat = list(range(2, NB))
            bwd_lat = list(range(NB - 1, 1, -1))

            sbr = alr("s_sbr", [128, 2, 4, 1024], F32)
            rk = [0]

            def recur(orders, store):
                n = len(orders[0])

                def ld(i):
                    if i >= n:
                        return
                    for d in range(2):
                        c = orders[d][i]
                        sl = (rk[0] + i) % 4
                        self.dma(sbr[:, d, sl, :], sbs[d, c], [('sbs', d, c)], [('s_sbr', d, sl)])
                for i in range(3):
                    ld(i)
                for i in range(n):
                    ld(i + 3)
                    for d in range(2):
                        c = orders[d][i]
                        sl = (rk[0] + i) % 4
                        if store:
                            hs = i % 2
                            self.cp('act', hb[:, d, hs, :], H[:, d, :], [('s_H', d)], [('s_hb', d, hs)])
                            self.dma(self.hst[d, c], hb[:, d, hs, :], [('s_hb', d, hs)], [('hst', d, c)])
                        self.tt('dve', b16(H[:, d, :]), b16(H[:, d, :]), bc16(E[:, c, 64 + 16 * d:80 + 16 * d]), ALU.mult,
                                [('s_H', d), ('s_E', c)], [('s_H', d)])
                        self.tt('dve', H[:, d, :], H[:, d, :], sbr[:, d, sl, :], ALU.add, [('s_H', d), ('s_sbr', d, sl)], [('s_H', d)])
                rk[0] += n

            recur(([0, 1], [1, 0]), True)
            for d in range(2):
                self.cp('dve', Hc[:, d, :], H[:, d, :], [('s_H', d)], [('s_Hc', d)])
            recur((fwd_lat, bwd_lat), False)
            self.dma(self.s_src.rearrange("p (d f) -> p d f", d=2), H, [('s_H', 0), ('s_H', 1)], ['s_src'])
            self.t.cc(self.s_src, self.s_dst, ['s_src'], ['s_dst'])
            self.dma(Gx[:, 0, :], self.s_dst[0:128, 0:1024], ['s_dst'], [('s_Gx', 0)])
            self.dma(Gx[:, 1, :], self.s_dst[128:256, 1024:2048], ['s_dst'], [('s_Gx', 1)])
            for d in range(2):
                own, oth = (1, 0) if d == 0 else (0, 1)
                self.ts('dve', Gx[:, d, :], Gx[:, d, :], self.hm[:, oth:oth + 1], ALU.mult, [('s_Gx', d), 'hm'], [('s_Gx', d)])
                self.stt('dve', H[:, d, :], Hc[:, d, :], self.hm[:, own:own + 1], Gx[:, d, :], ALU.mult, ALU.add,
                         [('s_Hc', d), 'hm', ('s_Gx', d)], [('s_H', d)])
            recur((fwd_lat, bwd_lat), True)
            self.t.barrier()
            rstk.close()
            zt3 = al("s_zt3", [128, 3, 1024], F32)
            hb3 = al("s_hb3", [128, 2, 3, 1024], BF16)

            def loads(c):
                s3 = c % 3
                tk_ = slice(128 * c, 128 * c + 128)
                self.dma(xf3[:, s3], self.xbcT[:, 0:8, tk_], xk, [('s_xf3', s3)])
                self.dma(zt3[:, s3, :], self.zs[tk_, :], [('zs', c)], [('s_z3', s3)])
                for d in range(2):
                    self.dma(hb3[:, d, s3, :], self.hst[d, c], [('hst', d, c)], [('s_hb3', d, s3)])
            ya2 = al("s_ya2", [128, 2, 1024], F32)
            yt2 = al("s_yt2", [128, 2, 1024], F32)
            cb2 = al("s_cb2", [128, 2, 2, 2, 128], F32)

            def front(c):
                s = c % 2
                s3 = c % 3
                tk = slice(128 * c, 128 * c + 128)
                pt_ = psb(7)
                for f in range(8):
                    self.tr(pt_[:, 128 * f:128 * f + 128], xf3[:, s3, f, :], self.ident, [('s_xf3', s3), 'ident'], [('ps', 7)])
                self.cp('act', xt[:, s, :], pt_, [('ps', 7)], [('s_xt', s)])
                for d in range(2):
                    self.tt('pool', b16(xw[:, s, d, :]), b16(xt[:, s, :]), bc16(dt_all[:, c, 16 * d:16 * d + 16]), ALU.mult,
                            [('s_xt', s), 's_dt'], [('s_xw', s, d)])
                pcb = self.ps[:, 0, 0:256]
                for g in range(2):
                    self.mm(pcb[:, 128 * g:128 * g + 128], BT[:, g, tk], CT[:, g, tk], True, True, ['s_bt', 's_ct'], [('ps', 0)])
                for d in range(2):
                    self.tt('dve', cb2[:, s, d, :, :], pcb.rearrange("p (g i) -> p g i", g=2),
                            self.msk_f[:, 0 if d == 0 else 2, :].unsqueeze(1).to_broadcast([128, 2, 128]), ALU.mult,
                            [('ps', 0), 'msk_f'], [('s_cb', s, d)])
                R4 = R.rearrange("p a (b f) -> p (a b) f", b=2)
                L4 = Lx.rearrange("p a (b f) -> p (a b) f", b=2)
                its = [(d, g, hf) for d in range(2) for g in range(2) for hf in range(2)]

                def emit_R(k):
                    d, g, hf = its[k]
                    sl = k % 4
                    h0 = 16 * d + 8 * g + 4 * hf
                    for hq in range(4):
                        self.act(R4[:, sl, 128 * hq:128 * hq + 128], self.msk_f[:, 0 if d == 0 else 2, :], AF.Identity,
                                 ['msk_f', 's_da'], [('s_R', sl, hq)], scale=da_all[:, c, h0 + hq:h0 + hq + 1])
                emit_R(0)
                emit_R(1)
                for k, (d, g, hf) in enumerate(its):
                    sl = k % 4
                    bk = 1 + k % 2
                    self.mm(self.ps[:, bk, :], self.msk_f[:, 1 if d == 0 else 3, :], R4[:, sl, :], True, True,
                            [('s_R', sl, hq) for hq in range(4)] + ['msk_f'], [('ps', bk)])
                    self.act(L4[:, sl, :], self.ps[:, bk, :], AF.Exp, [('ps', bk)], [('s_L', sl)])
                    if k + 2 < len(its):
                        emit_R(k + 2)
                    self.tt('dve', M[:, 2 * d + g, 512 * hf:512 * hf + 512].rearrange("p (h i) -> p h i", h=4),
                            L4[:, sl, :].rearrange("p (h i) -> p h i", h=4),
                            cb2[:, s, d, g, :].unsqueeze(1).to_broadcast([128, 4, 128]), ALU.mult,
                            [('s_L', sl), ('s_cb', s, d)], [('s_M', 2 * d + g, hf)])
                for g in range(2):
                    for hg in range(8):
                        hh = 8 * g + hg
                        for d in range(2):
                            self.mm(self.ps[:, 3 + g, 64 * hg:64 * hg + 64], M[:, 2 * d + g, 128 * hg:128 * hg + 128],
                                    xw[:, s, d, 64 * hh:64 * hh + 64], d == 0, d == 1,
                                    [('s_M', 2 * d + g, hg // 4), ('s_xw', s, d)], [('ps', 3 + g)])
                for d in range(2):
                    for g in range(2):
                        bk = 5 + (2 * d + g) % 2
                        self.mm(self.ps[:, bk, :], CT[:, g, tk], hb3[:, d, c % 3, 512 * g:512 * g + 512], True, True,
                                ['s_ct', ('s_hb3', d, c % 3)], [('ps', bk)])
                        dst = (ya2 if d == 0 else yt2)[:, s, 512 * g:512 * g + 512]
                        self.tt('dve', dst.rearrange("p (h e) -> p h e", h=8), self.ps[:, bk, :].rearrange("p (h e) -> p h e", h=8),
                                E[:, c, 32 * d + 8 * g:32 * d + 8 * g + 8].unsqueeze(2).to_broadcast([128, 8, 64]), ALU.mult,
                                [('ps', bk), ('s_E', c)], [('s_ya', s, g) if d == 0 else ('s_yt', s, g)])
                for g in range(2):
                    hs = slice(512 * g, 512 * g + 512)
                    self.tt('dve', ya2[:, s, hs], ya2[:, s, hs], self.ps[:, 3 + g, :], ALU.add,
                            [('s_ya', s, g), ('ps', 3 + g)], [('s_ya', s, g)])

            def tail(c):
                s = c % 2
                tk = slice(128 * c, 128 * c + 128)
                ya_ = ya2[:, s, :]
                yk = [('s_ya', s, 0), ('s_ya', s, 1)]
                self.tt('pool', ya_, ya_, yt2[:, s, :], ALU.add, yk + [('s_yt', s, 0), ('s_yt', s, 1)], yk)
                self.tt('pool', b16(ytmp), b16(xt[:, s, :]), bc16(dsk), ALU.mult, [('s_xt', s), 's_dk'], ['s_y3'])
                self.tt('pool', ya_, ya_, ytmp, ALU.add, yk + ['s_y3'], yk)
                self.tt('pool', ya_, ya_, zt3[:, c % 3, :], ALU.mult, yk + [('s_z3', c % 3)], yk)
                self.act(ytmp, ya_, AF.Square, yk, ['s_y3'])
                self.t.op('dve', lambda: nc.vector.reduce_sum(out=ss[:, 0:1], in_=ytmp, axis=mybir.AxisListType.X), ['s_y3'], ['s_ss'])
                self.act(ss[:, 1:2], ss[:, 0:1], AF.Ln, ['s_ss', 'epsc'], ['s_ss'], bias=self.epsc, scale=1.0 / 1024)
                self.act(ss[:, 1:2], ss[:, 1:2], AF.Exp, ['s_ss'], ['s_ss'], scale=-0.5)
                self.stt('dve', yn, ya_, ss[:, 1:2], gn, ALU.mult, ALU.mult, yk + ['s_ss', 's_gn'], ['s_yn'])
                pt = psb(0)
                for f in range(8):
                    self.tr(pt[:, 128 * f:128 * f + 128], yn[:, 128 * f:128 * f + 128], self.ident, ['s_yn', 'ident'], [('ps', 0)])
                self.cp('act', yst[:, s].rearrange("p f t -> p (f t)"), pt, [('ps', 0)], [('s_ys', s)])
                self.dma(self.ysT[:, :, tk], yst[:, s], [('s_ys', s)], [('ysT', c)])

            chunks = [c for c in range(NB) if not (c < 2 and not do_ctx)]
            loads(chunks[0])
            loads(chunks[1])
            front(chunks[0])
            for i, c in enumerate(chunks):
                if i + 2 < len(chunks):
                    loads(chunks[i + 2])
                if i + 1 < len(chunks):
                    front(chunks[i + 1])
                tail(c)
            self.t.barrier()

    def merge_phase(self, l, xsrc):
        nc = self.nc
        do_ctx = l < DEPTH - 1
        with ExitStack() as stk:
            al = lambda name, shape, dt: stk.enter_context(nc.sbuf_tensor(f"{name}{l}", shape, dt)).ap()
            woa = al("g_woa", [128, 4, 1024], BF16)
            woc = al("g_woc", [128, 4, 1024], BF16)
            wob = al("g_wob", [128, 8, 1024], BF16)
            wout = al("g_wout", [128, 8, 1024], BF16)
            stg = al("g_stg", [128, 4, 1024], F32)
            ya = al("g_ya", [128, 2, 4, 256], BF16)
            yc = al("g_yc", [128, 2, 4, 256], BF16)
            ys = al("g_ys", [128, 2, 8, 256], BF16)
            gt = al("g_gt", [128, 2, 24, 256], BF16)
            xg = al("g_xg", [128, 2, 8, 256], F32)
            mT = al("g_mT", [128, 2, 8, 256], BF16)
            ta = al("g_ta", [128, 2, 256], F32)
            tb = al("g_tb", [128, 2, 256], F32)
            tc_ = al("g_tc", [128, 2, 256], F32)
            k = 0
            for (wsrc, wdst, np_, nk) in ((self.woa[l], woa, 128, 4), (self.woc[l], woc, 128, 4), (self.wob[l], wob, 128, 8),
                                          (self.wout[l], wout, 128, 8)):
                for kc in range(nk):
                    sl = k % 4
                    eng_ = ('dve', 'act', 'dve', 'act')[k % 4]
                    k += 1
                    self.dma(stg[0:np_, sl, :], wsrc[:, kc, :], (), [('g_stg', sl)])
                    self.cp(eng_, wdst[:, kc, :], stg[0:np_, sl, :], [('g_stg', sl)], [('g_w', id(wdst) % 1000, kc)])
            wk = lambda wdst: [('g_w', id(wdst) % 1000, kc) for kc in range(wdst.shape[1])]
            wins = [(wi, t0, n_) for wi, (t0, n_) in enumerate(self.windows()) if not (wi == 0 and not do_ctx)]

            def mg_loads(wi, t0, n_):
                gi = 0 if wi == 0 else 1 + (wi - 1) // 2
                s = wi % 2
                tk = slice(t0, t0 + n_)
                self.dma(ya[:, s, :, 0:n_], self.yaT[:, :, tk], ['yTA'], [('g_ya', s)])
                self.dma(yc[:, s, :, 0:n_], self.ycT[:, :, tk], ['yTC'], [('g_yc', s)])
                self.dma(ys[:, s, :, 0:n_], self.ysT[:, :, tk], ['ysT'], [('g_ys', s)])
                self.dma(gt[:, s, :, 0:n_], self.gtT[:, :, tk], ['gtT'], [('g_gt', s)])
                self.dma(xg[:, s, :, 0:n_], xsrc[:, :, tk], [('xT', gi)], [('g_xg', s)])

            mg_loads(*wins[0])
            for wpos, (wi, t0, n_) in enumerate(wins):
                if wpos + 1 < len(wins):
                    mg_loads(*wins[wpos + 1])
                gi = 0 if wi == 0 else 1 + (wi - 1) // 2
                s = wi % 2
                cls = 1 if t0 < LC else 0
                tk = slice(t0, t0 + n_)
                for j in range(8):
                    js = j % 2
                    cs = slice(128 * j, 128 * j + 128)
                    bA, bB, bC = 3 * js, 3 * js + 1, 3 * js + 2
                    for h in range(4):
                        self.mm(self.ps[:, bA, 0:n_], woa[:, h, cs], ya[:, s, h, 0:n_], h == 0, h == 3, wk(woa) + [('g_ya', s)], [('ps', bA)])
                    for kc in range(8):
                        self.mm(self.ps[:, bB, 0:n_], wob[:, kc, cs], ys[:, s, kc, 0:n_], kc == 0, kc == 7, wk(wob) + [('g_ys', s)], [('ps', bB)])
                    for h in range(4):
                        self.mm(self.ps[:, bC, 0:n_], woc[:, h, cs], yc[:, s, h, 0:n_], h == 0, h == 3, wk(woc) + [('g_yc', s)], [('ps', bC)])
                    self.tt('dve', ta[:, js, 0:n_], self.ps[:, bA, 0:n_], gt[:, s, j, 0:n_], ALU.mult, [('ps', bA), ('g_gt', s)], [('g_ta', js)])
                    self.tt('dve', tb[:, js, 0:n_], self.ps[:, bB, 0:n_], gt[:, s, 8 + j, 0:n_], ALU.mult, [('ps', bB), ('g_gt', s)], [('g_tb', js)])
                    self.tt('dve', tc_[:, js, 0:n_], self.ps[:, bC, 0:n_], gt[:, s, 16 + j, 0:n_], ALU.mult, [('ps', bC), ('g_gt', s)], [('g_tc', js)])
                    self.tt('pool', ta[:, js, 0:n_], ta[:, js, 0:n_], tb[:, js, 0:n_], ALU.add, [('g_ta', js), ('g_tb', js)], [('g_ta', js)])
                    self.tt('pool', mT[:, s, j, 0:n_], ta[:, js, 0:n_], tc_[:, js, 0:n_], ALU.add, [('g_ta', js), ('g_tc', js)], [('g_mT', s, j)])
                for j in range(8):
                    bk = 6 + j % 2
                    cs = slice(128 * j, 128 * j + 128)
                    for kc in range(8):
                        self.mm(self.ps[:, bk, 0:n_], wout[:, kc, cs], mT[:, s, kc, 0:n_], kc == 0, kc == 7,
                                wk(wout) + [('g_mT', s, kc)], [('ps', bk)])
                    self.stt('dve', xg[:, s, j, 0:n_], self.ps[:, bk, 0:n_], self.modT[:, 16 + j, cls:cls + 1], xg[:, s, j, 0:n_],
                             ALU.mult, ALU.add, [('ps', bk), 'modT', ('g_xg', s)], [('g_xg', s)])
                self.dma(self.xT[:, :, tk], xg[:, s, :, 0:n_], [('g_xg', s)], [('xT', gi)])
            self.t.barrier()

    def ffn_phase(self, l):
        nc = self.nc
        do_ctx = l < DEPTH - 1
        with ExitStack() as stk0:
            al0 = lambda name, shape, dt: stk0.enter_context(nc.sbuf_tensor(f"{name}{l}", shape, dt)).ap()
            wdn = al0("d_w", [128, 22, 1024], BF16)
            dstg = al0("d_stg", [128, 2, 1024], F32)
            self._ffn(l, wdn, dstg)

    def _ffn(self, l, wdn, dstg):
        nc = self.nc
        do_ctx = l < DEPTH - 1
        with ExitStack() as stk:
            al = lambda name, shape, dt: stk.enter_context(nc.sbuf_tensor(f"{name}{l}", shape, dt)).ap()
            ws = al("f_ws", [128, 4, 8, 128], F32)
            wb = al("f_wb", [128, 4, 8, 128], BF16)
            cw = al("f_cw", [128, 22, 4], F32)
            orow = al("f_or", [128, 2, T], BF16)
            acc = al("f_acc", [128, 4, 256], F32)
            sg = al("f_sg", [128, 4, 256], F32)
            self.dma(cw, self.fcw[l], (), ['f_cw'])

            def load(f):
                if f >= 22:
                    return
                for i, src in enumerate((self.wup, self.wgt)):
                    sl = (2 * f + i) % 4
                    self.dma(ws[:, sl], src[l, f], (), [('f_ws', sl)])
                    self.cp('pool', wb[:, sl], ws[:, sl], [('f_ws', sl)], [('f_wb', sl)])
            load(0)
            for f in range(22):
                load(f + 1)
                self.dma(dstg[:, f % 2, :], self.wdn[l, :, f, :], (), [('d_stg', f % 2)])
                self.cp('pool', wdn[:, f, :], dstg[:, f % 2, :], [('d_stg', f % 2)], [('d_w', f)])
                su, sg_ = (2 * f) % 4, (2 * f + 1) % 4
                osl = f % 2
                pend = None
                for wi, (t0, n_) in enumerate(self.windows()):
                    if wi == 0 and not do_ctx:
                        continue
                    s = wi % 4
                    bG, bU = 2 * s, 2 * s + 1
                    c0 = hcol(t0)
                    pG, pU = self.ps[:, bG, 0:258], self.ps[:, bU, 0:256]
                    hk = self.hkeys(c0 - 1, c0 + 257)
                    for kc in range(8):
                        self.mm(pG, wb[:, sg_, kc, :], self.hT[:, kc, c0 - 1:c0 + 257], kc == 0, kc == 7, hk + [('f_wb', sg_)], [('ps', bG)])
                    for kc in range(8):
                        self.mm(pU, wb[:, su, kc, :], self.hT[:, kc, c0:c0 + 256], kc == 0, kc == 7, hk + [('f_wb', su)], [('ps', bU)])
                    a = acc[:, s, :]
                    self.act(a, pG[:, 0:256], AF.Identity, [('ps', bG), 'f_cw'], [('f_acc', s)], scale=cw[:, f, 0:1])
                    self.stt('dve', a, pG[:, 1:257], cw[:, f, 1:2], a, ALU.mult, ALU.add, [('ps', bG), 'f_cw', ('f_acc', s)], [('f_acc', s)])
                    self.stt('dve', a, pG[:, 2:258], cw[:, f, 2:3], a, ALU.mult, ALU.add, [('ps', bG), 'f_cw', ('f_acc', s)], [('f_acc', s)])
                    if pend is not None:
                        pend()

                    def pend(a=a, s=s, t0=t0, osl=osl, f=f, pU=pU, bU=bU):
                        self.act(sg[:, s, :], a, AF.Silu, [('f_acc', s), 'f_cw'], [('f_sg', s)], bias=cw[:, f, 3:4])
                        self.tt('dve', orow[:, osl, t0:t0 + 256], sg[:, s, :], pU, ALU.mult, [('f_sg', s), ('ps', bU)], [('f_or', osl)])
                pend()
                pend = None
                self.dma(self.actT[:, f, :], orow[:, osl, :], [('f_or', osl)], [('actT', f)])
            self.t.barrier()
        with ExitStack() as stk:
            al = lambda name, shape, dt: stk.enter_context(nc.sbuf_tensor(f"{name}{l}", shape, dt)).ap()
            at = al("d_at", [128, 2, 22, 512], BF16)
            xg = al("d_xg", [128, 2, 8, 512], F32)
            wk = []
            ak = [('actT', f) for f in range(22)]
            grps = [(gi, t0, n_) for gi, (t0, n_) in enumerate(self.groups()) if not (gi == 0 and not do_ctx)]

            def dn_loads(gi, t0, n_):
                s = gi % 2
                tk = slice(t0, t0 + n_)
                self.dma(at[:, s, :, 0:n_], self.actT[:, :, tk], ak, [('d_at', s)])
                self.dma(xg[:, s, :, 0:n_], self.xT[:, :, tk], [('xT', gi)], [('d_xg', s)])

            dn_loads(*grps[0])
            for gpos, (gi, t0, n_) in enumerate(grps):
                if gpos + 1 < len(grps):
                    dn_loads(*grps[gpos + 1])
                s = gi % 2
                cls = 1 if t0 < LC else 0
                tk = slice(t0, t0 + n_)
                for j in range(8):
                    bk = j % 4
                    cs = slice(128 * j, 128 * j + 128)
                    for kc in range(22):
                        self.mm(self.ps[:, bk, 0:n_], wdn[:, kc, cs], at[:, s, kc, 0:n_], kc == 0, kc == 21, wk + [('d_at', s)], [('ps', bk)])
                    self.stt('dve', xg[:, s, j, 0:n_], self.ps[:, bk, 0:n_], self.modT[:, 40 + j, cls:cls + 1], xg[:, s, j, 0:n_],
                             ALU.mult, ALU.add, [('ps', bk), 'modT', ('d_xg', s)], [('d_xg', s)])
                self.dma(self.xT[:, :, tk], xg[:, s, :, 0:n_], [('d_xg', s)], [('xT', gi)])
            self.t.barrier()

    def final_norm(self):
        nc = self.nc
        with ExitStack() as stk:
            al = lambda name, shape, dt: stk.enter_context(nc.sbuf_tensor(name, shape, dt)).ap()
            x_sb = al("fn_x", [128, 2, 8, 512], F32)
            sq_sb = al("fn_sq", [128, 2, 8, 512], BF16)
            r_sb = al("fn_r", [128, 2, 512], F32)
            o_sb = al("fn_o", [128, 2, 8, 512], F32)
            g_sb = al("fn_g", [128, 8], F32)
            self.dma(g_sb, self.fng, (), ['fn_g'])
            fgr = [(gi, t0, n_) for gi, (t0, n_) in enumerate(self.groups()) if gi > 0]

            def fn_load(gi, t0, n_):
                self.dma(x_sb[:, gi % 2, :, 0:n_], self.xT[:, :, t0:t0 + n_], [('xT', gi)], [('fn_x', gi % 2)])

            fn_load(*fgr[0])
            for gpos, (gi, t0, n_) in enumerate(fgr):
                if gpos + 1 < len(fgr):
                    fn_load(*fgr[gpos + 1])
                s = gi % 2
                xg = x_sb[:, s, :, 0:n_]
                self.act(sq_sb[:, s, :, 0:n_], xg, AF.Square, [('fn_x', s)], [('fn_sq', s)])
                pt = self.ps[:, s, 0:n_]
                for kc in range(8):
                    self.mm(pt, self.ones_ms, sq_sb[:, s, kc, 0:n_], kc == 0, kc == 7, [('fn_sq', s), 'ones_ms'], [('ps', s)])
                rr = r_sb[:, s, 0:n_]
                self.act(rr, pt, AF.Ln, [('ps', s), 'epsc'], [('fn_r', s)], bias=self.epsc)
                self.act(rr, rr, AF.Exp, [('fn_r', s)], [('fn_r', s)], scale=-0.5)
                for kc in range(8):
                    self.stt('dve', o_sb[:, s, kc, 0:n_], xg[:, kc, :], g_sb[:, kc:kc + 1], rr, ALU.mult, ALU.mult,
                             [('fn_x', s), 'fn_g', ('fn_r', s)], [('fn_o', s, kc)])
                self.dma(self.outT[:, :, t0 - LC:t0 - LC + n_], o_sb[:, s, :, 0:n_], [('fn_o', s, kc) for kc in range(8)], [('outT', gi)])
            self.t.barrier()
        return []


def _fm(w):
    K, C = w.shape
    return np.ascontiguousarray(w.reshape(K // 128, 128, C // 128, 128).transpose(2, 1, 0, 3))


def _rowsp(v, kc):
    return np.ascontiguousarray(v.reshape(kc, 128).T)


def _rope_table(hf):
    rows = L // 64
    t_row = np.repeat(np.arange(rows), 64).astype(np.float32)
    t_col = np.tile(np.arange(64), rows).astype(np.float32)
    n = 16
    inv = (10000.0 ** (-np.arange(n, dtype=np.float32) / n)).astype(np.float32)
    ang = np.concatenate([t_row[:, None] * inv, t_col[:, None] * inv], axis=-1)
    ang = ang[hf * LH:(hf + 1) * LH]
    cos = np.concatenate([np.ones((LC, 32), np.float32), np.cos(ang).astype(np.float32)], 0).T
    sin = np.concatenate([np.zeros((LC, 32), np.float32), np.sin(ang).astype(np.float32)], 0).T
    t1 = np.concatenate([cos, sin, cos, sin], 0)
    t2 = np.concatenate([-sin, cos, -sin, cos], 0)
    return np.ascontiguousarray(np.stack([t1, t2], 0)).astype(np.float32)


def _const_tables():
    k = np.arange(128)[:, None]
    i = np.arange(128)[None, :]
    masks = np.stack([(k <= i), (k > i), (k >= i), (k < i), np.ones((128, 128), bool), (k == i)], 0).astype(np.float32)
    return masks


def _in_cols():
    o = {}
    s = 0
    for name, n in (('a_q', 512), ('a_k', 128), ('a_v', 128), ('b_z', 1024), ('b_xbc', 1536), ('b_dt', 32),
                    ('c_q', 512), ('c_k', 128), ('c_v', 128), ('gates', 3072)):
        o[name] = s
        s += n
    ev = np.arange(0, 64, 2)
    od = np.arange(1, 64, 2)
    tiles = []

    def rope_pair(base, h0, h1):
        a = np.concatenate([base + h0 * 64 + ev, base + h0 * 64 + ev, base + h1 * 64 + ev, base + h1 * 64 + ev])
        b = np.concatenate([base + h0 * 64 + od, base + h0 * 64 + od, base + h1 * 64 + od, base + h1 * 64 + od])
        tiles.append(a)
        tiles.append(b)
    for m in range(4):
        rope_pair(o['a_q'], m, m + 4)
    rope_pair(o['a_k'], 0, 1)
    for m in range(4):
        rope_pair(o['c_q'], m, m + 4)
    rope_pair(o['c_k'], 0, 1)
    for j in range(12):
        tiles.append(o['b_xbc'] + j * 128 + np.arange(128))
    for j in range(24):
        tiles.append(o['gates'] + j * 128 + np.arange(128))
    fm = np.concatenate(tiles)
    tm = np.concatenate([o['b_z'] + np.arange(1024), o['a_v'] + np.arange(128), o['c_v'] + np.arange(128),
                         o['b_dt'] + np.arange(32)])
    return fm, tm


def prep_shared(inp):
    f = lambda a: np.ascontiguousarray(a, dtype=np.float32)
    masks = _const_tables()
    fmc, tmc = _in_cols()
    ev = np.arange(0, 64, 2)
    od = np.arange(1, 64, 2)
    sh = {}
    sh["wmod"] = f(np.stack([_fm(inp["w_mod"][l]) for l in range(DEPTH)]))
    sh["bmod"] = f(np.stack([_rowsp(inp["b_mod"][l], 48) for l in range(DEPTH)]))
    sh["nrm"] = f(np.stack([np.stack([_rowsp(inp["norm1"][l], 8), _rowsp(inp["norm2"][l], 8)], 1) for l in range(DEPTH)]))
    sh["wfm"] = f(np.stack([_fm(inp["w_in"][l][:, fmc]) for l in range(DEPTH)]))
    sh["wtm"] = f(np.stack([inp["w_in"][l][:, tmc].reshape(8, 128, NTM).transpose(1, 0, 2) for l in range(DEPTH)]))
    cg = []
    for l in range(DEPTH):
        q, k = inp["c_q_norm"][l], inp["c_k_norm"][l]
        cg.append(np.stack([np.tile(q[ev], 4), np.tile(q[od], 4), np.tile(k[ev], 4), np.tile(k[od], 4)], 1))
    sh["cgain"] = f(np.stack(cg))
    sh["cw"] = f(np.stack([np.concatenate([inp["ssm_conv_w"][l], inp["ssm_conv_b"][l][None]], 0).reshape(4, 12, 128).transpose(2, 1, 0)
                           for l in range(DEPTH)]))
    sh["dtb"] = f(np.stack([np.broadcast_to(inp["ssm_dt_bias"][l].reshape(1, 32), (128, 32)) for l in range(DEPTH)]))
    sh["alog"] = f(np.stack([np.broadcast_to(inp["ssm_A_log"][l].reshape(1, 32), (128, 32)) for l in range(DEPTH)]))
    sh["dsk"] = f(np.stack([np.broadcast_to(inp["ssm_D"][l].reshape(1, 16), (128, 16)) for l in range(DEPTH)]))
    sh["sng"] = f(np.stack([np.broadcast_to(inp["ssm_norm"][l].reshape(1, 1024), (128, 1024)) for l in range(DEPTH)]))
    sh["sink"] = f(np.stack([np.broadcast_to(inp["a_sink"][l].reshape(1, 8), (128, 8)) for l in range(DEPTH)]))
    sh["woa"] = f(np.stack([inp["w_oa"][l].reshape(4, 128, 1024).transpose(1, 0, 2) for l in range(DEPTH)]))
    sh["woc"] = f(np.stack([inp["w_oc"][l].reshape(4, 128, 1024).transpose(1, 0, 2) for l in range(DEPTH)]))
    sh["wob"] = f(np.stack([inp["w_ob"][l].reshape(8, 128, 1024).transpose(1, 0, 2) for l in range(DEPTH)]))
    sh["wout"] = f(np.stack([inp["w_out"][l].reshape(8, 128, 1024).transpose(1, 0, 2) for l in range(DEPTH)]))
    sh["wup"] = f(np.stack([_fm(inp["ffn_w_up"][l]) for l in range(DEPTH)]))
    sh["wgt"] = f(np.stack([_fm(inp["ffn_w_gate"][l]) for l in range(DEPTH)]))
    sh["fcw"] = f(np.stack([np.concatenate([inp["ffn_conv_w"][l], inp["ffn_conv_b"][l][None]], 0).reshape(4, 22, 128).transpose(2, 1, 0)
                            for l in range(DEPTH)]))
    sh["wdn"] = f(np.stack([inp["ffn_w_down"][l].reshape(22, 128, 1024).transpose(1, 0, 2) for l in range(DEPTH)]))
    sh["fng"] = f(_rowsp(inp["final_norm"], 8))
    sh["masks"] = masks
    sh["ident"] = np.eye(128, dtype=np.float32)
    return sh


def prep_core(inp, c):
    b, hf = c // 2, c % 2
    xl = inp["x"][b]
    xa = np.concatenate([inp["ctx"][b], xl[hf * LH:(hf + 1) * LH]], 0)
    xT0 = np.ascontiguousarray(xa.T.reshape(8, 128, T).transpose(1, 0, 2), dtype=np.float32)
    cv = np.stack([_rowsp(inp["c"][b], 8), _rowsp(inp["c_ctx"], 8)], -1)
    zero = np.zeros((D,), np.float32)
    left = xl[hf * LH - 1] if hf == 1 else zero
    right = xl[(hf + 1) * LH] if hf == 0 else zero
    xh0 = np.stack([_rowsp(left, 8), _rowsp(right, 8)], -1)
    hmask = np.broadcast_to(np.array([[float(hf), float(1 - hf)]], np.float32), (128, 2))
    return {"xT0": xT0, "cvec": np.ascontiguousarray(cv, dtype=np.float32), "xh0": np.ascontiguousarray(xh0, dtype=np.float32),
            "hmask": np.ascontiguousarray(hmask, dtype=np.float32), "rope": _rope_table(hf)}


_PROG = None


def kernel(**inp):
    global _PROG
    inp = {k: np.asarray(v) for k, v in inp.items()}
    if _PROG is None:
        _PROG = Prog()
    p = _PROG
    sh = prep_shared(inp)
    in_maps = []
    for c in range(8):
        m = dict(sh)
        m.update(prep_core(inp, c))
        in_maps.append(m)
    res = run_bass_kernel_spmd(p.nc, in_maps, core_ids=list(range(8)))
    out = np.empty((4, L, D), np.float32)
    for c in range(8):
        oT = res.results[c]["outT"]
        out[c // 2, (c % 2) * LH:(c % 2 + 1) * LH] = oT.transpose(2, 1, 0).reshape(LH, D)
    return out
```
